# Optimizing a Trainium2 kernel written in Bass

```python
import math
import jax, jax.numpy as jnp
from jax import lax
import numpy as np

D_MODEL = 1024
BATCH = 8
SEQ = 4096
DEPTH = 2
DEC_BATCH = 32
DEC_SEQ = 8
PAST_LEN = 16384
PAGE_SIZE = 128

N_MIXERS = 2
N_POOL_LAYERS = (DEPTH + 1) // 2
N_NSA_LAYERS = DEPTH // 2
ALPHA = (2.0 * DEPTH) ** 0.25
BETA = (8.0 * DEPTH) ** -0.25
LN_EPS = 1e-5

POOL_GROUPS = 4
POOL_WINDOWS = (2, 4, 8, 16)
POOL_CH = D_MODEL // POOL_GROUPS
POOL_BUF = max(POOL_WINDOWS) - 1

N_HEADS = 16
KV_HEADS = 4
HEAD_DIM = D_MODEL // N_HEADS
Q_PER_KV = N_HEADS // KV_HEADS
ROT_DIM = HEAD_DIM // 4
ROPE_THETA = 500000.0
CMP_BLOCK = 32
CMP_STRIDE = 16
CMP_HID = 128
SEL_BLOCK = 64
N_SEL = 16
WINDOW = 512
Q_BLOCK = 128
KV_W = 2 * KV_HEADS * HEAD_DIM
IN_W = N_HEADS * HEAD_DIM + 3 * KV_W + 3 * N_HEADS

D_FF = 2816
CONV_W = 3

NEG = -1e30
BIG = 1e9

kernel_name = 'hybrid_pool_nsa_convffn_decode_step'


def layer_norm(x, g, b):
    xf = x.astype(jnp.float32)
    mu = jnp.mean(xf, -1, keepdims=True)
    var = jnp.mean(jnp.square(xf - mu), -1, keepdims=True)
    y = (xf - mu) * lax.rsqrt(var + LN_EPS)
    return (y * g.astype(jnp.float32) + b.astype(jnp.float32)).astype(x.dtype)


def masked_softmax(s, mask):
    s = jnp.where(mask, s.astype(jnp.float32), NEG)
    return jnp.where(mask, jax.nn.softmax(s, axis=-1), 0.0)


def partial_rope(x, pos):
    half = ROT_DIM // 2
    inv = jnp.power(ROPE_THETA, -2.0 * jnp.arange(half, dtype=jnp.float32) / ROT_DIM)
    ang = pos.astype(jnp.float32)[:, None] * inv[None, :]
    cos = jnp.cos(ang)[None, :, None, :].astype(x.dtype)
    sin = jnp.sin(ang)[None, :, None, :].astype(x.dtype)
    x1 = x[..., :half]
    x2 = x[..., half:ROT_DIM]
    return jnp.concatenate([x1 * cos - x2 * sin, x2 * cos + x1 * sin, x[..., ROT_DIM:]], -1)


def pool_mixer(x, prefix, pos0, w, scale):
    B, T, _ = x.shape
    xx = jnp.concatenate([prefix, x], 1).astype(jnp.float32)
    csum = jnp.concatenate([jnp.zeros((B, 1, D_MODEL), jnp.float32), jnp.cumsum(xx, 1)], 1)
    end = csum[:, POOL_BUF + 1:]
    pos = pos0 + jnp.arange(T)
    outs = []
    for gi, win in enumerate(POOL_WINDOWS):
        sl = slice(gi * POOL_CH, (gi + 1) * POOL_CH)
        start = csum[:, POOL_BUF + 1 - win:POOL_BUF + 1 - win + T, sl]
        cnt = jnp.minimum(pos + 1, win).astype(jnp.float32)[None, :, None]
        outs.append((end[..., sl] - start) / cnt - xx[:, POOL_BUF:, sl])
    d = jnp.stack(outs, 2).astype(x.dtype)
    y = jnp.einsum('btgc,gce->btge', d, w).reshape(B, T, D_MODEL) * scale
    new_prefix = jnp.concatenate([prefix, x], 1)[:, -POOL_BUF:]
    return y, new_prefix


def conv_ffn(x, prefix, w_up, conv_w, conv_b, w_down):
    T = x.shape[1]
    up = x @ w_up
    up_pad = jnp.concatenate([prefix, up], 1)
    c = conv_b
    for k in range(CONV_W):
        c = c + conv_w[k] * up_pad[:, k:k + T]
    g, u = jnp.split(c, 2, -1)
    y = (jax.nn.silu(g) * u) @ w_down
    return y, up_pad[:, -(CONV_W - 1):]


def nsa_project(x, pos, w_in):
    B, T, _ = x.shape
    p = x @ w_in
    nq = N_HEADS * HEAD_DIM
    q = p[..., :nq].reshape(B, T, N_HEADS, HEAD_DIM)
    kv = p[..., nq:nq + 3 * KV_W].reshape(B, T, 3, 2, KV_HEADS, HEAD_DIM)
    gates = jax.nn.sigmoid(p[..., nq + 3 * KV_W:].astype(jnp.float32)).astype(x.dtype).reshape(B, T, N_HEADS, 3)
    q = partial_rope(q, pos) * (HEAD_DIM ** -0.5)
    k = partial_rope(kv[:, :, :, 0].reshape(B, T, 3 * KV_HEADS, HEAD_DIM), pos).reshape(B, T, 3, KV_HEADS, HEAD_DIM)
    kv = jnp.stack([k, kv[:, :, :, 1]], 3)
    return q, kv[:, :, 0], kv[:, :, 1], kv[:, :, 2], gates


def compress(rows, w1, b1, pos_emb, w2, b2):
    B, T = rows.shape[:2]
    n_half = T // CMP_STRIDE
    r = CMP_BLOCK // CMP_STRIDE
    nc = n_half - r + 1
    halves = rows[:, :n_half * CMP_STRIDE].reshape(B, n_half, CMP_STRIDE, 2, KV_HEADS, HEAD_DIM)
    outs = []
    for kvi in range(2):
        h = b1[kvi] + jnp.einsum('td,tdh->h', pos_emb[kvi], w1[kvi])
        for ri in range(r):
            w1r = w1[kvi, ri * CMP_STRIDE:(ri + 1) * CMP_STRIDE]
            pr = jnp.einsum('bntgd,tdh->bngh', halves[:, :, :, kvi], w1r)
            h = h + pr[:, ri:ri + nc]
        outs.append(jax.nn.gelu(h) @ w2[kvi] + b2[kvi])
    cend = jnp.arange(nc) * CMP_STRIDE + CMP_BLOCK - 1
    return outs[0], outs[1], cend


def nsa_core(q, gates, qpos, ck, cv, cend, gather_fn, n_blocks, wkv, wpos):
    B, Q = q.shape[:2]
    qg = q.reshape(B, Q, KV_HEADS, Q_PER_KV, HEAD_DIM)
    t5 = qpos[None, :, None, None, None]
    s = jnp.einsum('bqgrd,bngd->bqgrn', qg, ck)
    p_cmp = masked_softmax(s, cend[None, None, None, None, :] <= t5)
    o_cmp = jnp.einsum('bqgrn,bngd->bqgrd', p_cmp.astype(cv.dtype), cv)
    imp = jnp.sum(p_cmp, 3)
    nc = imp.shape[-1]
    lead = CMP_BLOCK // CMP_STRIDE - 1
    ratio = SEL_BLOCK // CMP_STRIDE
    n_off = ratio + lead
    imp = jnp.pad(imp, ((0, 0), (0, 0), (0, 0), (lead, ratio * n_blocks + n_off - nc - lead)))
    blk = imp[..., 0:ratio * n_blocks:ratio]
    for o in range(1, n_off):
        blk = blk + imp[..., o:o + ratio * n_blocks:ratio]
    jb = jnp.arange(n_blocks)
    tq = qpos[None, :, None, None]
    cur = tq // SEL_BLOCK
    valid = jb * SEL_BLOCK <= tq
    forced = (jb == 0) | (jb == cur) | (jb == cur - 1)
    score = jnp.where(valid & forced, BIG, jnp.where(valid, blk, -1.0))
    k_sel = min(N_SEL, n_blocks)
    top_val, idx = lax.top_k(score, k_sel)
    sel_ok = top_val >= 0.0
    kv_sel = gather_fn(idx)
    kpos = idx[..., None] * SEL_BLOCK + jnp.arange(SEL_BLOCK)
    m = sel_ok[..., None] & (kpos <= qpos[None, :, None, None, None])
    s = jnp.einsum('bqgrd,bqgksd->bqgrks', qg, kv_sel[..., 0, :])
    s = s.reshape(B, Q, KV_HEADS, Q_PER_KV, k_sel * SEL_BLOCK)
    p = masked_softmax(s, m.reshape(B, Q, KV_HEADS, 1, k_sel * SEL_BLOCK))
    p = p.reshape(B, Q, KV_HEADS, Q_PER_KV, k_sel, SEL_BLOCK).astype(kv_sel.dtype)
    o_slc = jnp.einsum('bqgrks,bqgksd->bqgrd', p, kv_sel[..., 1, :])
    s = jnp.einsum('bqgrd,bmgd->bqgrm', qg, wkv[:, :, 0])
    dist = qpos[:, None] - wpos[None, :]
    wm = (wpos[None, :] >= 0) & (dist >= 0) & (dist < WINDOW)
    p = masked_softmax(s, wm[None, :, None, None, :])
    o_win = jnp.einsum('bqgrm,bmgd->bqgrd', p.astype(wkv.dtype), wkv[:, :, 1])
    g = gates.reshape(B, Q, KV_HEADS, Q_PER_KV, 3)
    o = g[..., 0:1] * o_cmp + g[..., 1:2] * o_slc + g[..., 2:3] * o_win
    return o.reshape(B, Q, N_HEADS * HEAD_DIM)


def nsa_prompt(x, w_in, w_o, cw):
    B, T, _ = x.shape
    pos = jnp.arange(T)
    q, cmp_kv, slc_kv, win_kv, gates = nsa_project(x, pos, w_in)
    ck, cv, cend = compress(cmp_kv, *cw)
    n_blocks = T // SEL_BLOCK
    kb = slc_kv.reshape(B, n_blocks, SEL_BLOCK, 2, KV_HEADS, HEAD_DIM)
    b_idx = jnp.arange(B)[:, None, None, None]
    g_idx = jnp.arange(KV_HEADS)[None, None, :, None]

    def gather(idx):
        return kb[b_idx, idx, :, :, g_idx, :]

    win_pad = jnp.pad(win_kv, ((0, 0), (WINDOW, 0), (0, 0), (0, 0), (0, 0)))

    def q_block(qb):
        s0 = qb * Q_BLOCK
        qpos = s0 + jnp.arange(Q_BLOCK)
        qblk = lax.dynamic_slice_in_dim(q, s0, Q_BLOCK, 1)
        gblk = lax.dynamic_slice_in_dim(gates, s0, Q_BLOCK, 1)
        wblk = lax.dynamic_slice_in_dim(win_pad, s0, Q_BLOCK + WINDOW, 1)
        wpos = s0 - WINDOW + jnp.arange(Q_BLOCK + WINDOW)
        return nsa_core(qblk, gblk, qpos, ck, cv, cend, gather, n_blocks, wblk, wpos)

    o = lax.map(q_block, jnp.arange(T // Q_BLOCK))
    o = jnp.moveaxis(o, 0, 1).reshape(B, T, N_HEADS * HEAD_DIM)
    win_len = min(WINDOW, T)
    return o @ w_o, cmp_kv, slc_kv, win_kv[:, T - win_len:]


def nsa_sample(x, cmp_pool, slc_pool, win_buf, page_table, w_in, w_o, cw):
    B, T_new, _ = x.shape
    pos = PAST_LEN + jnp.arange(T_new)
    q, cmp_kv, slc_kv, win_kv, gates = nsa_project(x, pos, w_in)
    n_pages = PAST_LEN // PAGE_SIZE
    past_cmp = cmp_pool[page_table].reshape(B, n_pages * PAGE_SIZE, 2, KV_HEADS, HEAD_DIM)
    ck, cv, cend = compress(jnp.concatenate([past_cmp, cmp_kv], 1), *cw)
    n_past_blk = PAST_LEN // SEL_BLOCK
    n_tail = -(-T_new // SEL_BLOCK)
    n_blocks = n_past_blk + n_tail
    blk_per_page = PAGE_SIZE // SEL_BLOCK
    pool_b = slc_pool.reshape(-1, blk_per_page, SEL_BLOCK, 2, KV_HEADS, HEAD_DIM)
    tail = jnp.pad(slc_kv, ((0, 0), (0, n_tail * SEL_BLOCK - T_new), (0, 0), (0, 0), (0, 0)))
    tail = tail.reshape(B, n_tail, SEL_BLOCK, 2, KV_HEADS, HEAD_DIM)
    b_idx = jnp.arange(B)[:, None, None, None]
    g_idx = jnp.arange(KV_HEADS)[None, None, :, None]

    def gather(idx):
        jp = jnp.minimum(idx, n_past_blk - 1)
        page = page_table[b_idx, jp // blk_per_page]
        from_pool = pool_b[page, jp % blk_per_page, :, :, g_idx, :]
        jt = jnp.clip(idx - n_past_blk, 0, n_tail - 1)
        from_tail = tail[b_idx, jt, :, :, g_idx, :]
        return jnp.where((idx >= n_past_blk)[..., None, None, None], from_tail, from_pool)

    buf_len = win_buf.shape[1]
    wkv = jnp.concatenate([win_buf, win_kv], 1)
    wpos = PAST_LEN - buf_len + jnp.arange(wkv.shape[1])
    o = nsa_core(q, gates, pos, ck, cv, cend, gather, n_blocks, wkv, wpos)
    return o @ w_o, cmp_kv, slc_kv, wkv[:, -buf_len:]


def setup_inputs(seed: int = 0) -> dict:
    key = jax.random.key(seed)
    ks = iter(jax.random.split(key, 40))

    def nrm(shape, scale):
        return jax.random.normal(next(ks), shape, jnp.float32) * scale

    n_pages = PAST_LEN // PAGE_SIZE
    n_phys = (5 * DEC_BATCH * n_pages + 3) // 4
    page_table = jax.random.permutation(next(ks), n_phys)[:DEC_BATCH * n_pages].reshape(DEC_BATCH, n_pages).astype(jnp.int32)
    win_buf = min(WINDOW, PAST_LEN)
    kv_row = (2, KV_HEADS, HEAD_DIM)
    return {
        'x_prompt': nrm((BATCH, SEQ, D_MODEL), 1.0),
        'x_sample': nrm((DEC_BATCH, DEC_SEQ, D_MODEL), 1.0),
        'state_pool': nrm((N_POOL_LAYERS, DEC_BATCH, POOL_BUF, D_MODEL), 1.0),
        'cache_cmp_kv': nrm((N_NSA_LAYERS, n_phys, PAGE_SIZE) + kv_row, 1.0),
        'cache_slc_kv': nrm((N_NSA_LAYERS, n_phys, PAGE_SIZE) + kv_row, 1.0),
        'state_win_kv': nrm((N_NSA_LAYERS, DEC_BATCH, win_buf) + kv_row, 1.0),
        'state_ffn': nrm((DEPTH, DEC_BATCH, CONV_W - 1, 2 * D_FF), 1.0),
        'page_table': page_table,
        'ln_g': 1.0 + nrm((DEPTH, 2, D_MODEL), 0.05),
        'ln_b': nrm((DEPTH, 2, D_MODEL), 0.01),
        'pool_w': nrm((N_POOL_LAYERS, POOL_GROUPS, POOL_CH, POOL_CH), BETA * POOL_CH ** -0.5),
        'pool_scale': 1.0 + nrm((N_POOL_LAYERS, D_MODEL), 0.1),
        'nsa_w_in': nrm((N_NSA_LAYERS, D_MODEL, IN_W), D_MODEL ** -0.5),
        'nsa_w_o': nrm((N_NSA_LAYERS, N_HEADS * HEAD_DIM, D_MODEL), BETA * (N_HEADS * HEAD_DIM) ** -0.5),
        'cmp_w1': nrm((N_NSA_LAYERS, 2, CMP_BLOCK, HEAD_DIM, CMP_HID), (CMP_BLOCK * HEAD_DIM) ** -0.5),
        'cmp_b1': nrm((N_NSA_LAYERS, 2, CMP_HID), 0.01),
        'cmp_pos': nrm((N_NSA_LAYERS, 2, CMP_BLOCK, HEAD_DIM), 0.02),
        'cmp_w2': nrm((N_NSA_LAYERS, 2, CMP_HID, HEAD_DIM), CMP_HID ** -0.5),
        'cmp_b2': nrm((N_NSA_LAYERS, 2, HEAD_DIM), 0.01),
        'ffn_w_up': nrm((DEPTH, D_MODEL, 2 * D_FF), D_MODEL ** -0.5),
        'ffn_conv_w': nrm((DEPTH, CONV_W, 2 * D_FF), CONV_W ** -0.5),
        'ffn_conv_b': nrm((DEPTH, 2 * D_FF), 0.01),
        'ffn_w_down': nrm((DEPTH, D_FF, D_MODEL), BETA * D_FF ** -0.5),
    }


def reference(x_prompt, x_sample, state_pool, cache_cmp_kv, cache_slc_kv, state_win_kv, state_ffn, page_table,
              ln_g, ln_b, pool_w, pool_scale, nsa_w_in, nsa_w_o, cmp_w1, cmp_b1, cmp_pos, cmp_w2, cmp_b2,
              ffn_w_up, ffn_conv_w, ffn_conv_b, ffn_w_down):
    hp, hs = x_prompt, x_sample
    bp = hp.shape[0]
    pool_p, pool_s, cmp_p, cmp_s, slc_p, slc_s, win_p, win_s, ffn_p, ffn_s = ([] for _ in range(10))
    for i in range(DEPTH):
        li = i // N_MIXERS
        if i % N_MIXERS == 0:
            mp, npp = pool_mixer(hp, jnp.zeros((bp, POOL_BUF, D_MODEL), hp.dtype), 0, pool_w[li], pool_scale[li])
            ms, nps = pool_mixer(hs, state_pool[li], PAST_LEN, pool_w[li], pool_scale[li])
            pool_p.append(npp)
            pool_s.append(nps)
        else:
            cw = (cmp_w1[li], cmp_b1[li], cmp_pos[li], cmp_w2[li], cmp_b2[li])
            mp, c_p, s_p, w_p = nsa_prompt(hp, nsa_w_in[li], nsa_w_o[li], cw)
            ms, c_s, s_s, w_s = nsa_sample(hs, cache_cmp_kv[li], cache_slc_kv[li], state_win_kv[li], page_table,
                                           nsa_w_in[li], nsa_w_o[li], cw)
            cmp_p.append(c_p)
            cmp_s.append(c_s)
            slc_p.append(s_p)
            slc_s.append(s_s)
            win_p.append(w_p)
            win_s.append(w_s)
        hp = layer_norm(ALPHA * hp + mp, ln_g[i, 0], ln_b[i, 0])
        hs = layer_norm(ALPHA * hs + ms, ln_g[i, 0], ln_b[i, 0])
        fp, nfp = conv_ffn(hp, jnp.zeros((bp, CONV_W - 1, 2 * D_FF), hp.dtype),
                           ffn_w_up[i], ffn_conv_w[i], ffn_conv_b[i], ffn_w_down[i])
        fs, nfs = conv_ffn(hs, state_ffn[i], ffn_w_up[i], ffn_conv_w[i], ffn_conv_b[i], ffn_w_down[i])
        ffn_p.append(nfp)
        ffn_s.append(nfs)
        hp = layer_norm(ALPHA * hp + fp, ln_g[i, 1], ln_b[i, 1])
        hs = layer_norm(ALPHA * hs + fs, ln_g[i, 1], ln_b[i, 1])
    return (hp, hs, jnp.stack(pool_p), jnp.stack(pool_s), jnp.stack(cmp_p), jnp.stack(cmp_s),
            jnp.stack(slc_p), jnp.stack(slc_s), jnp.stack(win_p), jnp.stack(win_s),
            jnp.stack(ffn_p), jnp.stack(ffn_s))
```

```python
import numpy as np
from contextlib import ExitStack
import concourse.bass as bass
import concourse.mybir as mybir
from concourse.bass_utils import run_bass_kernel_spmd

F32 = mybir.dt.float32
BF16 = mybir.dt.bfloat16
I32 = mybir.dt.int32
ALU = mybir.AluOpType
AF = mybir.ActivationFunctionType

D = 1024
DFF = 2816
NPB = 11
NFC = 44
NKC = 22
ALPHA = float((2.0 * 2) ** 0.25)
LN_EPS = 1e-5
POOL_WINDOWS = (2, 4, 8, 16)
NCORES = 8
SAME_ENGINE_INORDER = False


class Buf:
    __slots__ = ("name", "w", "r", "aliases", "t")

    def __init__(self, name, t=None):
        self.name = name
        self.w = {}
        self.r = {}
        self.aliases = []
        self.t = t

    def __getitem__(self, k):
        return self.t[k]


class Sched:
    def __init__(self, nc, stack):
        self.nc = nc
        self.stack = stack
        self.eng = {"pe": nc.tensor, "act": nc.scalar, "dve": nc.vector,
                    "pool": nc.gpsimd, "sp": nc.sync}
        self.sem = {}
        self.cnt = {}
        for k in ("pe", "act", "dve", "pool"):
            self.sem[k] = stack.enter_context(nc.semaphore("c_" + k))
            self.cnt[k] = 0
        self.seen = {k: {} for k in self.eng}
        self.dsems = {}
        self.n_ins = 0

    def sb(self, name, shape, dtype):
        t = self.stack.enter_context(self.nc.sbuf_tensor("s_" + name, list(shape), dtype))
        return Buf(name, t)

    def ps(self, name, shape, dtype):
        t = self.stack.enter_context(self.nc.psum_tensor("p_" + name, list(shape), dtype))
        return Buf(name, t)

    def dsem(self, key):
        if key not in self.dsems:
            s = self.stack.enter_context(self.nc.semaphore("d_" + key))
            self.dsems[key] = [s, 0]
        return self.dsems[key]

    def _wait(self, e, tok):
        sem, val, owner = tok
        if owner == e and (e == "pe" or SAME_ENGINE_INORDER):
            return
        name = sem.name
        if self.seen[e].get(name, 0) >= val:
            return
        self.seen[e][name] = val
        self.eng[e].wait_ge(sem, val)

    def _deps(self, e, reads, writes, skip=None):
        toks = {}

        def add(d):
            for k, tok in d.items():
                if k == skip:
                    continue
                if k not in toks or toks[k][1] < tok[1]:
                    toks[k] = tok
        for b in reads:
            add(b.w)
            for a in b.aliases:
                add(a.w)
        for b in writes:
            add(b.w)
            add(b.r)
            for a in b.aliases:
                add(a.w)
                add(a.r)
        for tok in toks.values():
            self._wait(e, tok)

    def _commit(self, tok, key, reads, writes):
        for b in reads:
            b.r[key] = tok
        for b in writes:
            if key.startswith("d_") and key in b.w:
                b.w[key] = tok
            else:
                b.w = {key: tok}
            b.r = {}

    def op(self, e, fn, reads=(), writes=()):
        self._deps(e, reads, writes)
        ins = fn()
        self.cnt[e] += 1
        ins.then_inc(self.sem[e], 1)
        tok = (self.sem[e], self.cnt[e], e)
        self._commit(tok, "c_" + e, reads, writes)
        self.n_ins += 1
        return ins

    def dma(self, q, out, in_, reads=(), writes=(), key=None, **kw):
        if key is None:
            key = (writes[0] if writes else reads[0]).name
        self._deps(q, reads, writes, skip="d_" + key)
        ds = self.dsem(key)
        ins = self.eng[q].dma_start(out=out, in_=in_, **kw)
        ds[1] += 16
        ins.then_inc(ds[0], 16)
        tok = (ds[0], ds[1], "dma")
        self._commit(tok, "d_" + key, reads, writes)
        self.n_ins += 1
        return ins

    def finish(self, e="sp"):
        for key, (s, v) in self.dsems.items():
            if v:
                self._wait(e, (s, v, "dma"))
        for k in ("pe", "act", "dve", "pool"):
            if self.cnt[k]:
                self._wait(e, (self.sem[k], self.cnt[k], k))


class WStream:
    def __init__(self, S, slots, seq):
        self.S = S
        self.slots = slots
        self.seq = seq
        self.issued = 0
        self.cons = 0

    def _issue(self):
        k = self.issued
        if k >= len(self.seq):
            return
        slot = self.slots[k % len(self.slots)]
        src, ncol, cbuf = self.seq[k]
        npart = src.shape[0]
        self.S.dma("sp", slot.t[0:npart, 0:ncol], src, reads=[cbuf], writes=[slot], key=slot.name)
        self.issued += 1

    def next(self):
        n = len(self.slots)
        while self.issued < min(self.cons + n, len(self.seq)) and self.issued <= self.cons + n - 1:
            self._issue()
        slot = self.slots[self.cons % n]
        self.cons += 1
        return slot


def build_nc(T, TT, NPG=128, NPH=5120):
    nsub = TT // 128
    ntile = T // TT
    nc = bass.Bass("TRN2", target_bir_lowering=False)

    def din(name, shape, dt=F32):
        return nc.dram_tensor(name, list(shape), dt, kind="ExternalInput").ap()

    def dout(name, shape, dt=F32):
        return nc.dram_tensor(name, list(shape), dt, kind="ExternalOutput").ap()

    xp = din("xp", [T, D])
    xs = din("xs", [32, D])
    xscat = din("xscat", [4, 23, D])
    lnp = din("lnp", [4, 128, 2 * D])
    pscale = din("pscale", [128, D])
    poolw = din("poolw", [128, 8 * 256])
    bmat = din("bmat", [128, 12 * 128])
    bsamp = din("bsamp", [128, 4 * 32])
    ident_d = din("ident", [128, 128])
    wup = din("wup", [2, NPB, 128, 8 * 512])
    wdown = din("wdown", [2, 128, NKC * D])
    cwb = din("cwb", [2, 128, NFC * 4])
    sffn = din("sffn", [2, 128, NFC * 8])

    win_d = din("win", [6, 128, 8 * 512])
    wo_d = din("wo", [128, 8 * D])
    w1_d = din("w1", [2, 64, 32 * 128])
    posT_d = din("posT", [64, 2 * 32])
    b1T_d = din("b1T", [128, 2])
    w2_d = din("w2", [128, 2 * 64])
    b2T_d = din("b2T", [64, 2])
    rope_d = din("rope", [T, 16])
    tri_d = din("tri", [128, 2 * 128])
    pat_d = din("pat", [128, 8])
    emat_d = din("emat", [64, max(T, 4096)])
    scm_d = din("scm", [T // 128, 128, 2 * 64])
    rope_s_d = din("rope_s", [32, 16])
    bdm_d = din("bdm", [32, 32])
    wm0_d = din("wm0", [128, 32])
    rowm_d = din("rowm", [32, 4])
    iota_d = din("iota", [128, 1])
    ptab_d = din("ptab", [4, NPG], I32)
    ccmp_d = din("ccmp", [NPH * 128, 512])
    cslc_d = din("cslc", [NPH * 128, 512])
    swin_d = din("swin", [4, 512, 512])
    cmpkv_s = dout("cmpkv_s", [32, 512])
    slckv_s = dout("slckv_s", [32, 512])
    winkv_s = dout("winkv_s", [4, 512, 512])
    cmpkv_p = dout("cmpkv_p", [T, 512])
    slckv_p = dout("slckv_p", [T, 512])
    winkv_p = dout("winkv_p", [min(512, T), 512])

    def dscr(name, shape):
        return nc.dram_tensor(name, list(shape), BF16, kind="Internal").ap()
    wup_b = dscr("wup_b", [2, NPB, 128, 8 * 512])
    wdown_b = dscr("wdown_b", [2, 128, NKC * D])
    win_b = dscr("win_b", [6, 128, 8 * 512])
    wo_b = dscr("wo_b", [128, 8 * D])
    w1_b = dscr("w1_b", [2, 64, 32 * 128])

    y_p = dout("y_p", [T, D])
    y_s = dout("y_s", [32, D])
    pool_p = dout("pool_p", [15, D])
    pool_s = dout("pool_s", [4, 15, D])
    ffn_p = dout("ffn_p", [2, 128, NFC * 2])
    ffn_s = dout("ffn_s", [2, 128, NFC * 8])

    with ExitStack() as st:
        S = Sched(nc, st)
        ident = S.sb("identb", [128, 128], BF16)
        identF = S.sb("identF", [128, 65], F32)
        poolW = S.sb("poolW", [128, 8, 256], BF16)
        Bm = S.sb("Bm", [128, 12, 128], F32)
        Bs = S.sb("Bs", [128, 4, 32], F32)
        cw = S.sb("cw", [128, 2, NFC, 4], F32)
        halo = S.sb("halo", [128, 2, NFC, 2], F32)
        hc = S.sb("hc", [128, 2, NFC, 2], F32)
        hc2 = S.sb("hc2", [128, NFC], F32)
        shalo = S.sb("shalo", [128, 2, NFC, 8], F32)
        sout = shalo
        xhalo = S.sb("xhalo", [128, D], F32)
        XS = xhalo
        X = [[S.sb(f"X{b}_{s}", [128, D], F32) for s in range(nsub)] for b in range(1)]
        hT = S.sb("hT", [128, 8, TT], BF16)
        hb = [S.sb(f"hb{i}", [128, D], BF16) for i in range(1)]
        tmp = [S.sb(f"tmp{i}", [128, D], F32) for i in range(1)]
        lnslot = [S.sb(f"lnslot{i}", [128, D], F32) for i in range(1)]
        wslots = [S.sb(f"wslot{i}", [128, 8 * 512], BF16) for i in range(2)]
        wd = S.sb("wd", [128, NKC, D], BF16)
        hF = S.sb("hF", [128, NKC, TT], BF16)
        dT = Buf("dT", hF.t[:, 0:8, :])
        dT.aliases.append(hF)
        hF.aliases.append(dT)
        U = [S.sb(f"U{i}", [128, TT + 8], F32) for i in range(3)]
        C = [S.sb(f"C{i}", [128, TT], F32) for i in range(4)]
        SG = [S.sb(f"SG{i}", [128, TT], F32) for i in range(2)]
        stat = [S.sb(f"stat{i}", [128, 16], F32) for i in range(4)]
        NKT = max(T // 128, 32)
        KE = S.sb("KE", [128, 4, NKT * 128], BF16)
        VS = S.sb("VS", [128, NKT, 4, 65], BF16)
        KW = S.sb("KW", [64, 4, 768], BF16)
        VW = S.sb("VW", [128, 6, 4, 65], BF16)
        CKT = S.sb("CKT", [64, 4, 256], BF16)
        CVT = S.sb("CVT", [64, 4, 256], BF16)
        CV = S.sb("CV", [128, 2, 4, 64], BF16)
        cmpT = S.sb("cmpT", [64, 8, 16 + TT], BF16)
        posT = S.sb("posT", [64, 2, 32], BF16)
        b1T = S.sb("b1T", [128, 2], F32)
        hbias = S.sb("hbias", [128, 2], F32)
        w2 = S.sb("w2", [128, 2, 64], BF16)
        b2T = S.sb("b2T", [64, 2], F32)
        tri = S.sb("tri", [128, 2, 128], BF16)
        pat = S.sb("pat", [128, 8], F32)
        qbs = [S.sb(f"qb{i}", [128, 16, 64], BF16) for i in range(nsub)]
        kvb = [S.sb(f"kvb{i}", [128, 512], BF16) for i in range(2)]
        Rs = S.sb("Rs", [128, 4, 8, 8], F32)
        G = S.sb("G", [128, nsub, 48], F32)
        ropeT = S.sb("ropeT", [128, nsub, 16], F32)
        imp = S.sb("imp", [128, 272], F32)
        blk = S.sb("blk", [128, 64], F32)
        score = S.sb("score", [128, 64], F32)
        wk = S.sb("wk", [128, 64], F32)
        m8 = S.sb("m8", [128, 16], F32)
        negb = S.sb("negb", [128, 64], BF16)
        scm = [S.sb(f"scm{i}", [128, 2, 64], F32) for i in range(2)]
        PT = [S.sb(f"PT{i}", [128, 512], BF16) for i in range(2)]
        oT = S.sb("oT", [128, 8, 128], BF16)
        ug = S.sb("ug", [128, 8 * (TT // 16)], F32)
        ug2 = S.sb("ug2", [128, 8 * (TT // 16)], F32)
        gl = S.sb("gl", [128, 8 * (TT // 16)], BF16)
        small = [S.sb(f"small{i}", [128, 8], F32) for i in range(4)]
        wdflat = wd.t[:, :, :].rearrange("p a b -> p (a b)")

        def carve(name, off, nbytes, dtype, pattern=None, **kw):
            ap = wdflat[:, off // 2:(off + nbytes) // 2]
            if dtype != BF16:
                ap = ap.bitcast(dtype)
            if pattern:
                ap = ap.rearrange(pattern, **kw)
            b = Buf(name, ap)
            b.aliases.append(wd)
            wd.aliases.append(b)
            return b
        QN = carve("QN", 16384, 8192, BF16, "p (a b) -> p a b", a=16)
        o_tok = carve("o_tok", 24576, 8192, F32, "p (a b) -> p a b", a=nsub)
        stg = [carve(f"stg{i}", 32768 + 2048 * i, 2048, F32) for i in range(2)]
        E4 = carve("E4", 36864, 4096, F32, "p (a b) -> p a b", a=4)
        Pb = carve("Pb", 40960, 2048, BF16, "p (a b) -> p a b", a=4)
        PTc = carve("PTc", 43008, 2048, BF16, "p (a b c) -> p a b c", a=4, b=2)
        mm = [S.ps(f"mm{i}", [128, 512], F32) for i in range(3)]
        trp = S.ps("trp", [128, 1024], BF16)
        acc = [[S.ps(f"acc{j}_{h}", [128, 512], F32) for h in range(2)] for j in range(2)]

        cnt = {"mm": 0, "U": 0, "C": 0, "stat": 0, "ln": 0, "acc": 0, "hb": 0, "tmp": 0, "stg": 0, "kvb": 0, "scm": 0, "PT": 0, "small": 0}

        def rot(name, lst):
            i = cnt[name]
            cnt[name] += 1
            return lst[i % len(lst)]

        S.dma("pool", ident[:, :], ident_d, writes=[ident])
        S.dma("sp", identF[:, :], ident_d[:, 0:65], writes=[identF])
        S.dma("pool", poolW.t[:, :, :], poolw.rearrange("p (a b) -> p a b", a=8), writes=[poolW])
        S.dma("sp", Bm.t[:, :, :], bmat.rearrange("p (a b) -> p a b", a=12), writes=[Bm])
        S.dma("sp", Bs.t[:, :, :], bsamp.rearrange("p (a b) -> p a b", a=4), writes=[Bs])
        for l in range(2):
            S.dma("sp", cw.t[:, l, :, :], cwb[l].rearrange("p (a b) -> p a b", a=NFC), writes=[cw], key="cw")
            S.dma("sp", shalo.t[:, l, :, :], sffn[l].rearrange("p (a b) -> p a b", a=NFC), writes=[shalo], key="shalo")
        S.op("dve", lambda: nc.vector.memset(halo.t[:, :, :, :], 0.0), writes=[halo])
        S.op("dve", lambda: nc.vector.memset(hc.t[:, :, :, :], 0.0), writes=[hc])
        S.dma("sp", pool_p, xp[T - 15:T, :], key="o_pool")
        S.dma("sp", pool_s, xscat[:, 8:23, :], key="o_pool")

        castA = Buf("castA")
        castB = Buf("castB")
        for pb in range(NPB):
            S.dma("pool", wup_b[0, pb], wup[0, pb], writes=[castA], key="castA")
        for q4 in range(4):
            c0_, c1_ = q4 * (NKC * D // 4), (q4 + 1) * (NKC * D // 4)
            S.dma("pool", wdown_b[0][:, c0_:c1_], wdown[0][:, c0_:c1_], writes=[castA], key="castA")
        for cb in range(6):
            S.dma("pool", win_b[cb], win_d[cb], writes=[castB], key="castB")
        for kvi in range(2):
            S.dma("pool", w1_b[kvi], w1_d[kvi], writes=[castB], key="castB")
        for q4 in range(4):
            S.dma("pool", wo_b[:, q4 * 2048:(q4 + 1) * 2048], wo_d[:, q4 * 2048:(q4 + 1) * 2048], writes=[castB], key="castB")
        for pb in range(NPB):
            S.dma("pool", wup_b[1, pb], wup[1, pb], writes=[castB], key="castB")
        for q4 in range(4):
            c0_, c1_ = q4 * (NKC * D // 4), (q4 + 1) * (NKC * D // 4)
            S.dma("pool", wdown_b[1][:, c0_:c1_], wdown[1][:, c0_:c1_], writes=[castB], key="castB")

        S.dma("pool", posT.t[:, :, :], posT_d.rearrange("p (a b) -> p a b", a=2), writes=[posT])
        S.dma("sp", b1T[:, :], b1T_d, writes=[b1T])
        S.dma("pool", w2.t[:, :, :], w2_d.rearrange("p (a b) -> p a b", a=2), writes=[w2])
        S.dma("sp", b2T[:, :], b2T_d, writes=[b2T])
        S.dma("pool", tri.t[:, :, :], tri_d.rearrange("p (a b) -> p a b", a=2), writes=[tri])
        S.dma("sp", pat[:, :], pat_d, writes=[pat])
        S.op("dve", lambda: nc.vector.memset(KE.t[0:64, :, :], 0.0), writes=[KE])
        for g in range(4):
            S.dma("pool", KE.t[64:128, g, :], emat_d, writes=[KE])
        S.op("dve", lambda: nc.vector.memset(VS.t[:, :, :, :], 1.0), writes=[VS])
        S.op("dve", lambda: nc.vector.memset(VW.t[:, :, :, :], 1.0), writes=[VW])
        S.op("dve", lambda: nc.vector.memset(KW.t[:, :, :], 0.0), writes=[KW])
        S.op("dve", lambda: nc.vector.memset(CKT.t[:, :, :], 0.0), writes=[CKT])
        S.op("dve", lambda: nc.vector.memset(CVT.t[:, :, :], 0.0), writes=[CVT])
        S.op("dve", lambda: nc.vector.memset(CV.t[:, :, :, :], 0.0), writes=[CV])
        S.op("dve", lambda: nc.vector.memset(cmpT.t[:, :, :], 0.0), writes=[cmpT])
        for kvi in range(2):
            S.dma("pool", wslots[kvi].t[0:64, 0:4096], w1_d[kvi], writes=[wslots[kvi]], key="sw_" + wslots[kvi].name)
        bank = mm[0]
        for kvi in range(2):
            w1v = wslots[kvi].t[0:64, 0:4096].rearrange("p (a b) -> p a b", a=32)
            for t in range(32):
                S.op("pe", lambda kvi=kvi, t=t, w1v=w1v: nc.tensor.matmul(
                    bank[:, kvi:kvi + 1], lhsT=w1v[:, t, :], rhs=posT.t[:, kvi, t:t + 1], start=(t == 0), stop=(t == 31)),
                    reads=[wslots[kvi], posT], writes=[bank])
        S.op("dve", lambda: nc.vector.tensor_tensor(out=hbias[:, :], in0=bank[:, 0:2], in1=b1T[:, :], op=ALU.add),
             reads=[bank, b1T], writes=[hbias])

        seq = []
        for i in range(ntile + 1):
            for pb in range(NPB):
                seq.append((wup_b[0, pb], 8 * 512, castA))
            for cb in range(6):
                seq.append((win_b[cb], 8 * 512, castB))
            for rep in range(1 if i < ntile else 4 * (NPG // 8)):
                for kvi in range(2):
                    seq.append((w1_b[kvi], 4096, castB))
            for pb in range(NPB):
                seq.append((wup_b[1, pb], 8 * 512, castB))
        ws = WStream(S, wslots, seq)

        def layer_norm(subs, which):
            sts = []
            for (xb_buf, xs_ap, P, col0) in subs:
                stt = rot("stat", stat)
                sts.append(stt)
                for c in range(2):
                    S.op("dve", lambda c=c: nc.vector.bn_stats(out=stt[0:P, 6 * c:6 * c + 6], in_=xs_ap[:, 512 * c:512 * c + 512]),
                         reads=[xb_buf], writes=[stt])
                S.op("dve", lambda: nc.vector.bn_aggr(out=stt[0:P, 12:14], in_=stt[0:P, 0:12]), reads=[stt], writes=[stt])
                S.op("dve", lambda: nc.vector.tensor_scalar(out=stt[0:P, 14:15], in0=stt[0:P, 13:14], scalar1=LN_EPS, scalar2=None, op0=ALU.add),
                     reads=[stt], writes=[stt])
                S.op("act", lambda: nc.scalar.sqrt(out=stt[0:P, 14:15], in_=stt[0:P, 14:15]), reads=[stt], writes=[stt])
                S.op("dve", lambda: nc.vector.reciprocal(out=stt[0:P, 15:16], in_=stt[0:P, 14:15]), reads=[stt], writes=[stt])
            sl = rot("ln", lnslot)
            S.dma("sp", sl[:, :], lnp[which][:, 0:D], writes=[sl])
            for (xb_buf, xs_ap, P, col0), stt in zip(subs, sts):
                S.op("dve", lambda: nc.vector.scalar_tensor_tensor(out=xs_ap, in0=xs_ap, scalar=stt[0:P, 12:13], in1=sl[0:P, 0:D],
                                                                    op0=ALU.subtract, op1=ALU.mult),
                     reads=[xb_buf, stt, sl], writes=[xb_buf])
            S.dma("sp", sl[:, :], lnp[which][:, D:2 * D], writes=[sl])
            for (xb_buf, xs_ap, P, col0), stt in zip(subs, sts):
                S.op("dve", lambda: nc.vector.scalar_tensor_tensor(out=xs_ap, in0=xs_ap, scalar=stt[0:P, 15:16], in1=sl[0:P, 0:D],
                                                                    op0=ALU.mult, op1=ALU.add),
                     reads=[xb_buf, stt, sl], writes=[xb_buf])

        def to_hT(xb_buf, xs_ap, P, col0):
            h = rot("hb", hb)
            S.op("act", lambda: nc.scalar.copy(out=h[0:P, :], in_=xs_ap), reads=[xb_buf], writes=[h])
            for c in range(8):
                S.op("pe", lambda c=c: nc.tensor.transpose(trp[:, c * 128:c * 128 + P], h[0:P, c * 128:(c + 1) * 128], ident[0:P, 0:P]),
                     reads=[h, ident], writes=[trp])
            S.op("dve", lambda: nc.vector.tensor_copy(
                out=hT.t[:, :, col0:col0 + P],
                in_=trp.t[:, :].rearrange("p (a b) -> p a b", a=8)[:, :, 0:P]), reads=[trp], writes=[hT])

        def ffn(l, subs, ncols, nseq, L, prompt, last):
            S.dma("sp", wd.t[:, :, :], wdown_b[l].rearrange("p (a b) -> p a b", a=NKC), reads=[castA if l == 0 else castB], writes=[wd], key="wd")
            for pb in range(NPB):
                wsl = ws.next()
                wv = wsl.t[:, :].rearrange("p (a b) -> p a b", a=8)
                sgs = [None, None]
                for fc in range(4):
                    fi = pb * 4 + fc
                    bank = rot("mm", mm)
                    for kc in range(8):
                        S.op("pe", lambda kc=kc, fc=fc, bank=bank: nc.tensor.matmul(
                            bank[:, 0:ncols], lhsT=wv[:, kc, fc * 128:(fc + 1) * 128], rhs=hT.t[:, kc, 0:ncols],
                            start=(kc == 0), stop=(kc == 7)), reads=[wsl, hT], writes=[bank])
                    if prompt:
                        c = rot("C", C)
                        S.op("act", lambda: nc.scalar.activation(out=c[:, 0:L], in_=bank[:, 0:L], func=AF.Identity,
                                                                 scale=cw.t[:, l, fi, 2:3], bias=cw.t[:, l, fi, 3:4]),
                             reads=[bank, cw], writes=[c])
                        S.op("dve", lambda: nc.vector.scalar_tensor_tensor(out=c[:, 1:L], in0=bank[:, 0:L - 1], scalar=cw.t[:, l, fi, 1:2],
                                                                            in1=c[:, 1:L], op0=ALU.mult, op1=ALU.add),
                             reads=[bank, cw, c], writes=[c])
                        S.op("dve", lambda: nc.vector.scalar_tensor_tensor(out=c[:, 2:L], in0=bank[:, 0:L - 2], scalar=cw.t[:, l, fi, 0:1],
                                                                            in1=c[:, 2:L], op0=ALU.mult, op1=ALU.add),
                             reads=[bank, cw, c], writes=[c])
                        S.op("pool", lambda: nc.gpsimd.tensor_tensor(out=c[:, 0:2], in0=c[:, 0:2], in1=hc.t[:, l, fi, :], op=ALU.add),
                             reads=[c, hc], writes=[c])
                        S.op("dve", lambda: nc.vector.tensor_copy(out=halo.t[:, l, fi, :], in_=bank[:, L - 2:L]), reads=[bank], writes=[halo])
                        if fc < 2:
                            sg = SG[fc]
                            S.op("act", lambda sg=sg: nc.scalar.activation(out=sg[:, 0:ncols], in_=c[:, 0:ncols], func=AF.Silu),
                                 reads=[c], writes=[sg])
                            sgs[fc] = sg
                        else:
                            sg = sgs[fc - 2]
                            S.op("pool", lambda sg=sg: nc.gpsimd.tensor_tensor(out=hF.t[:, 2 * pb + fc - 2, 0:ncols], in0=sg[:, 0:ncols],
                                                                                in1=c[:, 0:ncols], op=ALU.mult),
                                 reads=[sg, c], writes=[hF])
                        continue
                    u = rot("U", U)
                    uv = u.t[:, 0:nseq * (L + 2)].rearrange("p (a b) -> p a b", a=nseq)
                    S.op("act", lambda: nc.scalar.copy(out=uv[:, :, 2:2 + L],
                                                       in_=bank.t[:, 0:ncols].rearrange("p (a b) -> p a b", a=nseq)),
                         reads=[bank], writes=[u])
                    if prompt:
                        S.op("pool", lambda: nc.gpsimd.tensor_copy(out=uv[:, 0, 0:2], in_=halo.t[:, l, fi, :]), reads=[halo], writes=[u])
                        S.op("pool", lambda: nc.gpsimd.tensor_copy(out=halo.t[:, l, fi, :], in_=uv[:, 0, L:L + 2]), reads=[u], writes=[halo])
                    else:
                        S.op("pool", lambda: nc.gpsimd.tensor_copy(
                            out=uv[:, :, 0:2], in_=shalo.t[:, l, fi, :].rearrange("p (a b) -> p a b", a=4)), reads=[shalo], writes=[u])
                        S.op("pool", lambda: nc.gpsimd.tensor_copy(
                            out=sout.t[:, l, fi, :].rearrange("p (a b) -> p a b", a=4), in_=uv[:, :, L:L + 2]), reads=[u], writes=[sout])
                    c = rot("C", C)
                    cv = c.t[:, 0:ncols].rearrange("p (a b) -> p a b", a=nseq)
                    S.op("dve", lambda: nc.vector.tensor_scalar(out=cv, in0=uv[:, :, 0:L], scalar1=cw.t[:, l, fi, 0:1],
                                                                scalar2=cw.t[:, l, fi, 3:4], op0=ALU.mult, op1=ALU.add),
                         reads=[u, cw], writes=[c])
                    S.op("dve", lambda: nc.vector.scalar_tensor_tensor(out=cv, in0=uv[:, :, 1:1 + L], scalar=cw.t[:, l, fi, 1:2],
                                                                        in1=cv, op0=ALU.mult, op1=ALU.add),
                         reads=[u, cw, c], writes=[c])
                    S.op("dve", lambda: nc.vector.scalar_tensor_tensor(out=cv, in0=uv[:, :, 2:2 + L], scalar=cw.t[:, l, fi, 2:3],
                                                                        in1=cv, op0=ALU.mult, op1=ALU.add),
                         reads=[u, cw, c], writes=[c])
                    if fc < 2:
                        sg = SG[fc]
                        S.op("act", lambda sg=sg: nc.scalar.activation(out=sg[:, 0:ncols], in_=c[:, 0:ncols], func=AF.Silu),
                             reads=[c], writes=[sg])
                        sgs[fc] = sg
                    else:
                        sg = sgs[fc - 2]
                        S.op("pool", lambda sg=sg: nc.gpsimd.tensor_tensor(out=hF.t[:, 2 * pb + fc - 2, 0:ncols], in0=sg[:, 0:ncols],
                                                                            in1=c[:, 0:ncols], op=ALU.mult),
                             reads=[sg, c], writes=[hF])
            if prompt:
                S.op("pool", lambda: nc.gpsimd.tensor_tensor(out=hc2[:, :], in0=halo.t[:, l, :, 1], in1=cw.t[:, l, :, 1], op=ALU.mult),
                     reads=[halo, cw], writes=[hc2])
                S.op("pool", lambda: nc.gpsimd.tensor_tensor(out=hc.t[:, l, :, 0], in0=halo.t[:, l, :, 0], in1=cw.t[:, l, :, 0], op=ALU.mult),
                     reads=[halo, cw], writes=[hc])
                S.op("pool", lambda: nc.gpsimd.tensor_tensor(out=hc.t[:, l, :, 0], in0=hc.t[:, l, :, 0], in1=hc2[:, :], op=ALU.add),
                     reads=[hc, hc2], writes=[hc])
                S.op("pool", lambda: nc.gpsimd.tensor_tensor(out=hc.t[:, l, :, 1], in0=halo.t[:, l, :, 1], in1=cw.t[:, l, :, 0], op=ALU.mult),
                     reads=[halo, cw, hc], writes=[hc])
            if prompt and last:
                S.dma("sp", ffn_p[l].rearrange("p (a b) -> p a b", a=NFC), halo.t[:, l, :, :], reads=[halo], key="o_ffn")
            if not prompt:
                S.dma("sp", ffn_s[l].rearrange("p (a b) -> p a b", a=NFC), sout.t[:, l, :, :], reads=[sout], key="o_ffn")
            for (xb_buf, xs_ap, P, col0) in subs:
                a = rot("acc", acc)
                for half in range(2):
                    for kc in range(NKC):
                        S.op("pe", lambda kc=kc, half=half: nc.tensor.matmul(
                            a[half][0:P, :], lhsT=hF.t[:, kc, col0:col0 + P], rhs=wd.t[:, kc, half * 512:(half + 1) * 512],
                            start=(kc == 0), stop=(kc == NKC - 1)), reads=[hF, wd], writes=[a[half]])
                    S.op("dve", lambda half=half: nc.vector.scalar_tensor_tensor(
                        out=xs_ap[:, half * 512:(half + 1) * 512], in0=xs_ap[:, half * 512:(half + 1) * 512], scalar=ALPHA,
                        in1=a[half][0:P, :], op0=ALU.mult, op1=ALU.add), reads=[xb_buf, a[half]], writes=[xb_buf])
            layer_norm(subs, l * 2 + 1)

        def mixer_pool_finish(subs):
            scale_bc = rot("ln", lnslot)
            S.dma("sp", scale_bc[:, :], pscale, writes=[scale_bc])
            for (xb_buf, xs_ap, P, col0) in subs:
                a = rot("acc", acc)
                for g in range(4):
                    for j in range(2):
                        S.op("pe", lambda g=g, j=j: nc.tensor.matmul(
                            a[g // 2][0:P, (g % 2) * 256:(g % 2) * 256 + 256], lhsT=dT.t[:, 2 * g + j, col0:col0 + P],
                            rhs=poolW.t[:, 2 * g + j, :], start=(j == 0), stop=(j == 1)), reads=[dT, poolW], writes=[a[g // 2]])
                tm = rot("tmp", tmp)
                for half in range(2):
                    S.op("dve", lambda half=half: nc.vector.tensor_tensor(
                        out=tm[0:P, half * 512:(half + 1) * 512], in0=a[half][0:P, :],
                        in1=scale_bc[0:P, half * 512:(half + 1) * 512], op=ALU.mult), reads=[a[half], scale_bc], writes=[tm])
                S.op("dve", lambda: nc.vector.scalar_tensor_tensor(out=xs_ap, in0=xs_ap, scalar=ALPHA, in1=tm[0:P, :],
                                                                    op0=ALU.mult, op1=ALU.add), reads=[xb_buf, tm], writes=[xb_buf])
            layer_norm(subs, 0)
            for (xb_buf, xs_ap, P, col0) in subs:
                to_hT(xb_buf, xs_ap, P, col0)

        def rope_inplace(buf, v, P, nh, cs):
            cosb = cs[:, 0:8].unsqueeze(1).to_broadcast([P, nh, 8])
            sinb = cs[:, 8:16].unsqueeze(1).to_broadcast([P, nh, 8])
            x1, x2 = v[:, :, 0:8], v[:, :, 8:16]
            r = [Rs.t[0:P, k, 0:nh, :] for k in range(4)]
            S.op("dve", lambda: nc.vector.tensor_tensor(out=r[0], in0=x1, in1=cosb, op=ALU.mult), reads=[buf, ropeT], writes=[Rs])
            S.op("dve", lambda: nc.vector.tensor_tensor(out=r[1], in0=x2, in1=sinb, op=ALU.mult), reads=[buf, ropeT], writes=[Rs])
            S.op("dve", lambda: nc.vector.tensor_tensor(out=r[2], in0=x2, in1=cosb, op=ALU.mult), reads=[buf, ropeT], writes=[Rs])
            S.op("dve", lambda: nc.vector.tensor_tensor(out=r[3], in0=x1, in1=sinb, op=ALU.mult), reads=[buf, ropeT], writes=[Rs])
            S.op("dve", lambda: nc.vector.tensor_tensor(out=x1, in0=r[0], in1=r[1], op=ALU.subtract), reads=[Rs], writes=[buf])
            S.op("dve", lambda: nc.vector.tensor_tensor(out=x2, in0=r[2], in1=r[3], op=ALU.add), reads=[Rs], writes=[buf])

        def nsa_project(i, subs, sample=False):
            t0 = i * TT
            qs0 = t0 // 128
            if sample:
                S.dma("sp", ropeT.t[0:32, 0, :], rope_s_d, writes=[ropeT])
            else:
                S.dma("sp", ropeT.t[:, :, :], rope_d[t0:t0 + TT, :].rearrange("(s p) c -> p s c", p=128), writes=[ropeT])
            if i > 0 and not sample:
                S.op("pool", lambda: nc.gpsimd.tensor_copy(out=cmpT.t[:, :, 0:16], in_=cmpT.t[:, :, TT:TT + 16]), reads=[cmpT], writes=[cmpT])
            for cb in range(6):
                wsl = ws.next()
                wv = wsl.t[:, :].rearrange("p (a b) -> p a b", a=8)
                ncb = 512 if cb < 5 else 48
                for (xb_buf, xs_ap, P, col0) in subs:
                    sidx = col0 // 128
                    qb = qbs[sidx]
                    key0 = t0 + col0
                    cs = ropeT.t[0:P, sidx, :]
                    bank = rot("mm", mm)
                    for kc in range(8):
                        S.op("pe", lambda kc=kc, bank=bank: nc.tensor.matmul(
                            bank[0:P, 0:ncb], lhsT=hT.t[:, kc, col0:col0 + P], rhs=wv[:, kc, 0:ncb],
                            start=(kc == 0), stop=(kc == 7)), reads=[hT, wsl], writes=[bank])
                    if cb < 2:
                        sg_ = rot("stg", stg)
                        S.op("act", lambda: nc.scalar.mul(out=sg_[0:P, :], in_=bank[0:P, :], mul=0.125), reads=[bank], writes=[sg_])
                        rope_inplace(sg_, sg_.t[0:P, :].rearrange("p (h d) -> p h d", h=8), P, 8, cs)
                        S.op("act", lambda: nc.scalar.copy(out=qb.t[0:P, cb * 8:cb * 8 + 8, :],
                                                           in_=sg_.t[0:P, :].rearrange("p (h d) -> p h d", h=8)), reads=[sg_], writes=[qb])
                        if cb == 1:
                            for r8 in range(2):
                                for hh in range(8):
                                    S.op("pe", lambda hh=hh: nc.tensor.transpose(trp[0:64, hh * 128:hh * 128 + P], qb.t[0:P, r8 * 8 + hh, :], ident[0:P, 0:P]),
                                         reads=[qb, ident], writes=[trp])
                                S.op("dve", lambda: nc.vector.tensor_copy(
                                    out=QN.t[0:64, r8 * 8:r8 * 8 + 8, col0:col0 + P],
                                    in_=trp.t[0:64, :].rearrange("p (a b) -> p a b", a=8)[:, :, 0:P]), reads=[trp], writes=[QN])
                    elif cb < 5:
                        br = cb - 2
                        sg_ = rot("stg", stg)
                        S.op("act", lambda: nc.scalar.copy(out=sg_[0:P, :], in_=bank[0:P, :]), reads=[bank], writes=[sg_])
                        rope_inplace(sg_, sg_.t[0:P, 0:256].rearrange("p (h d) -> p h d", h=4), P, 4, cs)
                        dst = [cmpkv_p, slckv_p][br] if br < 2 else None
                        if sample:
                            if br < 2:
                                S.dma("sp", [cmpkv_s, slckv_s][br], sg_[0:32, :], reads=[sg_], key="o_" + sg_.name)
                            else:
                                for q_ in range(4):
                                    S.dma("sp", winkv_s[q_, 504:512, :], sg_[8 * q_:8 * q_ + 8, :], reads=[sg_], key="o_" + sg_.name)
                        elif br < 2:
                            S.dma("sp", dst[key0:key0 + P, :], sg_[0:P, :], reads=[sg_], key="o_" + sg_.name)
                        else:
                            lo = max(0, T - 512)
                            if key0 >= lo:
                                S.dma("sp", winkv_p[key0 - lo:key0 - lo + P, :], sg_[0:P, :], reads=[sg_], key="o_" + sg_.name)
                        kb = rot("kvb", kvb)
                        S.op("act", lambda: nc.scalar.copy(out=kb[0:P, :], in_=sg_[0:P, :]), reads=[sg_], writes=[kb])
                        kt = key0 // 128
                        if sample:
                            if br > 0:
                                for j in range(4):
                                    S.op("pe", lambda j=j: nc.tensor.transpose(trp[0:64, j * 128:j * 128 + P], kb[0:P, j * 64:(j + 1) * 64], ident[0:P, 0:P]),
                                         reads=[kb, ident], writes=[trp])
                                tv = trp.t[0:64, 0:512].rearrange("p (a b) -> p a b", a=4)[:, :, 0:P]
                                vv = kb.t[0:P, 256:512].rearrange("p (g d) -> p g d", g=4)
                                S.op("dve", lambda: nc.vector.tensor_copy(out=tailK[:, br - 1, :, :], in_=tv), reads=[trp], writes=[CKT])
                                S.op("pool", lambda: nc.gpsimd.tensor_copy(out=VW.t[0:P, 6 - br, :, 0:64], in_=vv), reads=[kb], writes=[VW])
                        elif br == 0:
                            for j in range(8):
                                S.op("pe", lambda j=j: nc.tensor.transpose(trp[0:64, j * 128:j * 128 + P], kb[0:P, j * 64:(j + 1) * 64], ident[0:P, 0:P]),
                                     reads=[kb, ident], writes=[trp])
                            S.op("dve", lambda: nc.vector.tensor_copy(
                                out=cmpT.t[0:64, :, 16 + col0:16 + col0 + P],
                                in_=trp.t[0:64, :].rearrange("p (a b) -> p a b", a=8)[:, :, 0:P]), reads=[trp], writes=[cmpT])
                        else:
                            for j in range(4):
                                S.op("pe", lambda j=j: nc.tensor.transpose(trp[0:64, j * 128:j * 128 + P], kb[0:P, j * 64:(j + 1) * 64], ident[0:P, 0:P]),
                                     reads=[kb, ident], writes=[trp])
                            tv = trp.t[0:64, 0:512].rearrange("p (a b) -> p a b", a=4)[:, :, 0:P]
                            vv = kb.t[0:P, 256:512].rearrange("p (g d) -> p g d", g=4)
                            if br == 1:
                                S.op("dve", lambda: nc.vector.tensor_copy(out=KE.t[0:64, :, key0:key0 + P], in_=tv), reads=[trp], writes=[KE])
                                S.op("pool", lambda: nc.gpsimd.tensor_copy(out=VS.t[0:P, kt, :, 0:64], in_=vv), reads=[kb], writes=[VS])
                            else:
                                kr = key0 % 768
                                S.op("dve", lambda: nc.vector.tensor_copy(out=KW.t[0:64, :, kr:kr + P], in_=tv), reads=[trp], writes=[KW])
                                S.op("pool", lambda: nc.gpsimd.tensor_copy(out=VW.t[0:P, kt % 6, :, 0:64], in_=vv), reads=[kb], writes=[VW])
                    else:
                        S.op("act", lambda: nc.scalar.activation(out=G.t[0:P, sidx, :], in_=bank[0:P, 0:48], func=AF.Sigmoid),
                             reads=[bank], writes=[G])

        def compress_new(n_lo, n_new, base, cmpT=cmpT, cT=None, CKT=CKT, CKTv=None, CVT=CVT, CVTv=None, CV=CV, CVv=None,
                         ug=ug, ugv=None, ug2=ug2, ug2v=None, gl=gl, glv=None):
            cT = cmpT.t if cT is None else cT
            CKTv = CKT.t if CKTv is None else CKTv
            CVTv = CVT.t if CVTv is None else CVTv
            CVv = CV.t if CVv is None else CVv
            ugv = ug.t if ugv is None else ugv
            ug2v = ug2.t if ug2v is None else ug2v
            glv = gl.t if glv is None else glv
            bank = rot("mm", mm)
            for kvi in range(2):
                wsl = ws.next()
                w1v = wsl.t[0:64, 0:4096].rearrange("p (a b) -> p a b", a=32)
                for g in range(4):
                    j = kvi * 4 + g
                    cview = cT[0:64, j, :].rearrange("p (c b) -> p c b", b=16)
                    for t in range(32):
                        q_, r_ = divmod(base + t, 16)
                        S.op("pe", lambda t=t, j=j, q_=q_, r_=r_, w1v=w1v, cview=cview: nc.tensor.matmul(
                            bank[:, j * n_new:(j + 1) * n_new], lhsT=w1v[:, t, :], rhs=cview[:, q_:q_ + n_new, r_],
                            start=(t == 0), stop=(t == 31)), reads=[wsl, cmpT], writes=[bank])
            nn = 8 * n_new
            for kvi in range(2):
                S.op("act", lambda kvi=kvi: nc.scalar.activation(out=ugv[:, kvi * 4 * n_new:(kvi + 1) * 4 * n_new],
                                                              in_=bank[:, kvi * 4 * n_new:(kvi + 1) * 4 * n_new],
                                                              func=AF.Identity, bias=hbias[:, kvi:kvi + 1], scale=1.0),
                     reads=[bank, hbias], writes=[ug])
            S.op("dve", lambda: nc.vector.tensor_tensor(out=ug2v[:, 0:nn], in0=ugv[:, 0:nn], in1=ugv[:, 0:nn], op=ALU.mult), reads=[ug], writes=[ug2])
            S.op("dve", lambda: nc.vector.tensor_scalar(out=ug2v[:, 0:nn], in0=ug2v[:, 0:nn], scalar1=0.044715, scalar2=1.0, op0=ALU.mult, op1=ALU.add),
                 reads=[ug2], writes=[ug2])
            S.op("dve", lambda: nc.vector.tensor_tensor(out=ug2v[:, 0:nn], in0=ug2v[:, 0:nn], in1=ugv[:, 0:nn], op=ALU.mult), reads=[ug, ug2], writes=[ug2])
            S.op("act", lambda: nc.scalar.activation(out=ug2v[:, 0:nn], in_=ug2v[:, 0:nn], func=AF.Sigmoid, scale=1.5957691216057308),
                 reads=[ug2], writes=[ug2])
            S.op("dve", lambda: nc.vector.tensor_tensor(out=glv[:, 0:nn], in0=ugv[:, 0:nn], in1=ug2v[:, 0:nn], op=ALU.mult), reads=[ug, ug2], writes=[gl])
            for kvi in range(2):
                bank2 = rot("mm", mm)
                S.op("pe", lambda kvi=kvi, bank2=bank2: nc.tensor.matmul(
                    bank2[0:64, 0:4 * n_new], lhsT=w2.t[:, kvi, :], rhs=glv[:, kvi * 4 * n_new:(kvi + 1) * 4 * n_new],
                    start=True, stop=True), reads=[w2, gl], writes=[bank2])
                dstT = CKT if kvi == 0 else CVT
                dstv = CKTv if kvi == 0 else CVTv
                S.op("act", lambda kvi=kvi, bank2=bank2, dstv=dstv: nc.scalar.activation(
                    out=dstv[0:64, :, n_lo:n_lo + n_new], in_=bank2.t[0:64, 0:4 * n_new].rearrange("p (g n) -> p g n", g=4),
                    func=AF.Identity, bias=b2T[:, kvi:kvi + 1], scale=1.0), reads=[bank2, b2T], writes=[dstT])
            for bk in range(n_lo // 128, (n_lo + n_new - 1) // 128 + 1):
                for g in range(4):
                    S.op("pe", lambda g=g, bk=bk: nc.tensor.transpose(trp[:, g * 64:(g + 1) * 64], CVTv[0:64, g, bk * 128:(bk + 1) * 128], ident[0:64, 0:64]),
                         reads=[CVT, ident], writes=[trp])
                S.op("dve", lambda bk=bk: nc.vector.tensor_copy(out=CVv[:, bk, :, :], in_=trp.t[:, 0:256].rearrange("p (g d) -> p g d", g=4)),
                     reads=[trp], writes=[CV])

        def cmp_and_select(qs, s, P, col0, g, Nc):
            abank = acc[0]
            for hi in range(4):
                S.op("pe", lambda hi=hi: nc.tensor.matmul(
                    abank[hi // 2][0:P, (hi % 2) * 256:(hi % 2) * 256 + Nc], lhsT=QN.t[0:64, 4 * g + hi, col0:col0 + P],
                    rhs=CKT.t[0:64, g, 0:Nc], start=True, stop=True), reads=[QN, CKT], writes=[abank[hi // 2]])
            for hp in range(2):
                S.op("act", lambda hp=hp: nc.scalar.activation(
                    out=E4.t[0:P, 2 * hp:2 * hp + 2, 0:Nc], in_=abank[hp].t[0:P, :].rearrange("p (a b) -> p a b", a=2)[:, :, 0:Nc],
                    func=AF.Exp), reads=[abank[hp]], writes=[E4])
            c0 = max(0, 8 * qs - 1)
            c1 = min(8 * qs + 7, Nc)
            p0 = c0 - (8 * qs - 1)
            S.op("dve", lambda: nc.vector.tensor_tensor(
                out=E4.t[0:P, :, c0:c1], in0=E4.t[0:P, :, c0:c1],
                in1=pat[0:P, p0:p0 + (c1 - c0)].unsqueeze(1).to_broadcast([P, 4, c1 - c0]), op=ALU.mult), reads=[E4, pat], writes=[E4])
            sm = rot("small", small)
            S.op("dve", lambda: nc.vector.tensor_reduce(out=sm[0:P, 0:4], in_=E4.t[0:P, :, 0:Nc], axis=mybir.AxisListType.X, op=ALU.add),
                 reads=[E4], writes=[sm])
            S.op("dve", lambda: nc.vector.tensor_scalar(out=sm[0:P, 0:4], in0=sm[0:P, 0:4], scalar1=1e-30, scalar2=None, op0=ALU.max),
                 reads=[sm], writes=[sm])
            S.op("dve", lambda: nc.vector.reciprocal(out=sm[0:P, 4:8], in_=sm[0:P, 0:4]), reads=[sm], writes=[sm])
            S.op("dve", lambda: nc.vector.tensor_tensor(
                out=E4.t[0:P, :, 0:Nc], in0=E4.t[0:P, :, 0:Nc], in1=sm[0:P, 4:8].unsqueeze(2).to_broadcast([P, 4, Nc]), op=ALU.mult),
                reads=[E4, sm], writes=[E4])
            S.op("dve", lambda: nc.vector.memset(imp[:, :], 0.0), writes=[imp])
            S.op("dve", lambda: nc.vector.tensor_reduce(out=imp[0:P, 1:1 + Nc], in_=E4.t[0:P, :, 0:Nc].rearrange("p h n -> p n h"),
                                                        axis=mybir.AxisListType.X, op=ALU.add), reads=[E4], writes=[imp])
            iv = imp.t[0:P, 0:260].rearrange("p (j f) -> p j f", f=4)
            S.op("dve", lambda: nc.vector.tensor_tensor(out=blk[0:P, :], in0=iv[:, 0:64, 0], in1=iv[:, 0:64, 1], op=ALU.add), reads=[imp], writes=[blk])
            S.op("dve", lambda: nc.vector.tensor_tensor(out=blk[0:P, :], in0=blk[0:P, :], in1=iv[:, 0:64, 2], op=ALU.add), reads=[imp, blk], writes=[blk])
            S.op("dve", lambda: nc.vector.tensor_tensor(out=blk[0:P, :], in0=blk[0:P, :], in1=iv[:, 0:64, 3], op=ALU.add), reads=[imp, blk], writes=[blk])
            S.op("dve", lambda: nc.vector.tensor_tensor(out=blk[0:P, :], in0=blk[0:P, :], in1=iv[:, 1:65, 0], op=ALU.add), reads=[imp, blk], writes=[blk])
            sc = scm[s % 2]
            S.op("dve", lambda: nc.vector.tensor_tensor(out=score[0:P, :], in0=blk[0:P, :], in1=sc[0:P, 0, :], op=ALU.mult), reads=[blk, sc], writes=[score])
            S.op("dve", lambda: nc.vector.tensor_tensor(out=score[0:P, :], in0=score[0:P, :], in1=sc[0:P, 1, :], op=ALU.add), reads=[score, sc], writes=[score])
            S.op("dve", lambda: nc.vector.max(out=m8[0:P, 0:8], in_=score[0:P, :]), reads=[score], writes=[m8])
            S.op("dve", lambda: nc.vector.match_replace(out=wk[0:P, :], in_to_replace=m8[0:P, 0:8], in_values=score[0:P, :], imm_value=-1e30),
                 reads=[score, m8], writes=[wk])
            S.op("dve", lambda: nc.vector.max(out=m8[0:P, 8:16], in_=wk[0:P, :]), reads=[wk], writes=[m8])
            S.op("dve", lambda: nc.vector.tensor_scalar(out=wk[0:P, :], in0=score[0:P, :], scalar1=0.0, scalar2=None, op0=ALU.is_ge), reads=[score], writes=[wk])
            S.op("dve", lambda: nc.vector.scalar_tensor_tensor(out=wk[0:P, :], in0=score[0:P, :], scalar=m8[0:P, 15:16], in1=wk[0:P, :],
                                                                op0=ALU.is_ge, op1=ALU.mult), reads=[score, m8, wk], writes=[wk])
            S.op("dve", lambda: nc.vector.tensor_scalar(out=negb[0:P, :], in0=wk[0:P, :], scalar1=-1.0, scalar2=30000.0, op0=ALU.add, op1=ALU.mult),
                 reads=[wk], writes=[negb])
            S.op("pe", lambda: nc.tensor.transpose(trp[0:64, 0:P], negb[0:P, :], ident[0:P, 0:P]), reads=[negb, ident], writes=[trp])
            S.op("dve", lambda: nc.vector.tensor_copy(out=QN.t[64:128, 4 * g:4 * g + 4, col0:col0 + P],
                                                      in_=trp.t[0:64, 0:P].unsqueeze(1).to_broadcast([64, 4, P])), reads=[trp], writes=[QN])
            S.op("act", lambda: nc.scalar.copy(out=Pb.t[0:P, :, 0:Nc], in_=E4.t[0:P, :, 0:Nc]), reads=[E4], writes=[Pb])
            nnt = (Nc + 127) // 128
            for hi in range(4):
                for nt in range(nnt):
                    w = min(128, Nc - 128 * nt)
                    S.op("pe", lambda hi=hi, nt=nt, w=w: nc.tensor.transpose(
                        trp[0:w, (hi * 2 + nt) * 128:(hi * 2 + nt) * 128 + P], Pb.t[0:P, hi, nt * 128:nt * 128 + w], ident[0:P, 0:P]),
                        reads=[Pb, ident], writes=[trp])
            for nt in range(nnt):
                w = min(128, Nc - 128 * nt)
                S.op("dve", lambda nt=nt, w=w: nc.vector.tensor_copy(
                    out=PTc.t[0:w, :, nt, 0:P], in_=trp.t[0:w, :].rearrange("p (a b c) -> p a b c", a=4, b=2)[:, :, nt, 0:P]),
                    reads=[trp], writes=[PTc])
            ob_ = acc[1][0]
            for hi in range(4):
                for nt in range(nnt):
                    w = min(128, Nc - 128 * nt)
                    S.op("pe", lambda hi=hi, nt=nt, w=w: nc.tensor.matmul(
                        ob_[0:P, hi * 64:(hi + 1) * 64], lhsT=PTc.t[0:w, hi, nt, 0:P], rhs=CV.t[0:w, nt, g, :],
                        start=(nt == 0), stop=(nt == nnt - 1)), reads=[PTc, CV], writes=[ob_])
            gv = G.t[0:P, s, :].rearrange("p (h b) -> p h b", b=3)
            S.op("dve", lambda: nc.vector.tensor_tensor(
                out=o_tok.t[0:P, s, 256 * g:256 * g + 256].rearrange("p (h d) -> p h d", h=4),
                in0=ob_.t[0:P, 0:256].rearrange("p (h d) -> p h d", h=4),
                in1=gv[:, 4 * g:4 * g + 4, 0].unsqueeze(2).to_broadcast([P, 4, 64]), op=ALU.mult), reads=[ob_, G], writes=[o_tok])

        def attend(branch, qs0, nsq, g, hi, Pq):
            h = 4 * g + hi
            ob_ = acc[branch - 1][hi % 2]
            kt_lo = 0 if branch == 1 else max(0, qs0 - 4)
            kt_hi = qs0 + nsq - 1
            kts = list(range(kt_lo, kt_hi + 1))

            def job(kt):
                sA = max(0, kt - qs0)
                sB = nsq - 1 if branch == 1 else min(nsq - 1, kt + 4 - qs0)
                return (kt, sA, sB, (sB - sA + 1) * 128)
            jobs = [job(kt) for kt in kts]
            units = []
            i_ = 0
            while i_ < len(jobs):
                if jobs[i_][3] == TT and i_ + 1 < len(jobs) and 2 * TT <= 512:
                    units.append([jobs[i_], jobs[i_ + 1]])
                    i_ += 2
                else:
                    units.append([jobs[i_]])
                    i_ += 1

            def emit_scores(unit):
                bank = rot("mm", mm)
                for j_, (kt, sA, sB, N) in enumerate(unit):
                    qc0 = sA * 128
                    o_ = TT * j_
                    if branch == 1:
                        S.op("pe", lambda: nc.tensor.matmul(bank[:, o_:o_ + N], lhsT=KE.t[:, g, kt * 128:(kt + 1) * 128], rhs=QN.t[:, h, qc0:qc0 + N],
                                                            start=True, stop=True), reads=[KE, QN], writes=[bank])
                    else:
                        kr = (kt * 128) % 768
                        S.op("pe", lambda: nc.tensor.matmul(bank[:, o_:o_ + N], lhsT=KW.t[0:64, g, kr:kr + 128], rhs=QN.t[0:64, h, qc0:qc0 + N],
                                                            start=True, stop=True), reads=[KW, QN], writes=[bank])
                return bank

            def emit_rest(unit, bank):
                pt = rot("PT", PT)
                ntot = TT * (len(unit) - 1) + unit[-1][3]
                S.op("act", lambda: nc.scalar.activation(out=pt[:, 0:ntot], in_=bank[:, 0:ntot], func=AF.Exp), reads=[bank], writes=[pt])
                for j_, (kt, sA, sB, N) in enumerate(unit):
                    o_ = TT * j_
                    for sq in range(sA, sB + 1):
                        delta = qs0 + sq - kt
                        c = o_ + (sq - sA) * 128
                        if delta == 0:
                            S.op("pool", lambda c=c: nc.gpsimd.tensor_tensor(out=pt[:, c:c + 128], in0=pt[:, c:c + 128], in1=tri.t[:, 0, :], op=ALU.mult),
                                 reads=[pt, tri], writes=[pt])
                        elif branch == 2 and delta == 4:
                            S.op("pool", lambda c=c: nc.gpsimd.tensor_tensor(out=pt[:, c:c + 128], in0=pt[:, c:c + 128], in1=tri.t[:, 1, :], op=ALU.mult),
                                 reads=[pt, tri], writes=[pt])
                for j_, (kt, sA, sB, N) in enumerate(unit):
                    o_ = TT * j_
                    qc0 = sA * 128
                    first = (kt == kt_lo)
                    last = (kt == kt_hi)
                    vsrc = VS.t[:, kt, g, :] if branch == 1 else VW.t[:, kt % 6, g, :]
                    S.op("pe", lambda o_=o_, qc0=qc0, N=N, first=first, last=last, vsrc=vsrc: nc.tensor.matmul(
                        ob_[0:65, qc0:qc0 + N], lhsT=vsrc, rhs=pt[:, o_:o_ + N], start=first, stop=last, skip_group_check=True),
                        reads=[pt, VS if branch == 1 else VW], writes=[ob_])

            pend = emit_scores(units[0])
            for ui, unit in enumerate(units):
                cur = pend
                if ui + 1 < len(units):
                    pend = emit_scores(units[ui + 1])
                emit_rest(unit, cur)
            S.op("dve", lambda: nc.vector.tensor_copy(out=imp[0:65, 0:nsq * 128], in_=ob_[0:65, 0:nsq * 128]), reads=[ob_], writes=[imp])
            for sq in range(nsq):
                tb = rot("mm", mm)
                S.op("pe", lambda: nc.tensor.transpose(tb[0:128, 0:65], imp[0:65, sq * 128:(sq + 1) * 128], identF[0:65, 0:65]),
                     reads=[imp, identF], writes=[tb])
                sm = rot("small", small)
                S.op("dve", lambda: nc.vector.reciprocal(out=sm[0:Pq, 0:1], in_=tb[0:Pq, 64:65]), reads=[tb], writes=[sm])
                S.op("dve", lambda: nc.vector.tensor_tensor(out=sm[0:Pq, 1:2], in0=sm[0:Pq, 0:1], in1=G.t[0:Pq, sq, 3 * h + branch:3 * h + branch + 1], op=ALU.mult),
                     reads=[sm, G], writes=[sm])
                S.op("dve", lambda: nc.vector.scalar_tensor_tensor(
                    out=o_tok.t[0:Pq, sq, 64 * h:64 * h + 64], in0=tb[0:Pq, 0:64], scalar=sm[0:Pq, 1:2],
                    in1=o_tok.t[0:Pq, sq, 64 * h:64 * h + 64], op0=ALU.mult, op1=ALU.add), reads=[tb, sm, o_tok], writes=[o_tok])

        def out_proj(subs):
            for (xb_buf, xs_ap, P, col0) in subs:
                sidx = col0 // 128
                h_ = rot("hb", hb)
                S.op("act", lambda: nc.scalar.copy(out=h_[0:P, :], in_=o_tok.t[0:P, sidx, :]), reads=[o_tok], writes=[h_])
                for c in range(8):
                    S.op("pe", lambda c=c: nc.tensor.transpose(trp[:, c * 128:c * 128 + P], h_[0:P, c * 128:(c + 1) * 128], ident[0:P, 0:P]),
                         reads=[h_, ident], writes=[trp])
                S.op("dve", lambda: nc.vector.tensor_copy(out=oT.t[:, :, 0:P], in_=trp.t[:, :].rearrange("p (a b) -> p a b", a=8)[:, :, 0:P]),
                     reads=[trp], writes=[oT])
                a = rot("acc", acc)
                for half in range(2):
                    for c in range(8):
                        S.op("pe", lambda c=c, half=half: nc.tensor.matmul(
                            a[half][0:P, :], lhsT=oT.t[:, c, 0:P], rhs=wd.t[:, c, half * 512:(half + 1) * 512],
                            start=(c == 0), stop=(c == 7)), reads=[oT, wd], writes=[a[half]])
                    S.op("dve", lambda half=half: nc.vector.scalar_tensor_tensor(
                        out=xs_ap[:, half * 512:(half + 1) * 512], in0=xs_ap[:, half * 512:(half + 1) * 512], scalar=ALPHA,
                        in1=a[half][0:P, :], op0=ALU.mult, op1=ALU.add), reads=[xb_buf, a[half]], writes=[xb_buf])
            layer_norm(subs, 2)
            for (xb_buf, xs_ap, P, col0) in subs:
                to_hT(xb_buf, xs_ap, P, col0)

        def nsa_prompt(i, subs):
            t0 = i * TT
            qs0 = t0 // 128
            S.dma("sp", wd.t[:, 0:8, :], wo_b.rearrange("p (a b) -> p a b", a=8), reads=[castB], writes=[wd], key="wd")
            nsa_project(i, subs)
            if i == 0:
                compress_new(0, TT // 16 - 1, 16)
            else:
                compress_new(t0 // 16 - 1, TT // 16, 0)
            for s in range(nsub):
                qs = qs0 + s
                sc = scm[s % 2]
                S.dma("sp", sc.t[:, :, :], scm_d[qs].rearrange("p (a b) -> p a b", a=2), writes=[sc])
                Nc = min(8 * qs + 7, 255)
                for g in range(4):
                    cmp_and_select(qs, s, 128, s * 128, g, Nc)
            for g in range(4):
                for hi in range(4):
                    attend(1, qs0, nsub, g, hi, 128)
                    attend(2, qs0, nsub, g, hi, 128)
            out_proj(subs)

        tailK = CKT.t[0:64, :, :].rearrange("p a b -> p (a b)")[:, 0:256].rearrange("p (a b c) -> p a b c", a=2, b=4)
        NCMP = 8 * NPG - 1
        NBLK = 2 * NPG + 1
        NRG = (NPG + 31) // 32
        hFflat = hF.t[:, :, :].rearrange("p a b -> p (a b)")
        CKTs = hFflat[0:64, 0:4096].rearrange("p (a b) -> p a b", a=4)
        Pb1 = hFflat[0:32, 4096:5120]
        CVTs_b = carve("CVTs", 36864, 8192, BF16, "p (a b) -> p a b", a=4)
        for ob_ in (E4, Pb, PTc):
            CVTs_b.aliases.append(ob_)
            ob_.aliases.append(CVTs_b)
        CVs = Bm.t[:, :, :].rearrange("p a b -> p (a b)").bitcast(BF16)[:, 0:2048].rearrange("p (a b c) -> p a b c", a=8, b=4)
        cmpTs = VS.t[:, :, :, :].rearrange("p a b c -> p (a b c)")[0:64, 0:8320].rearrange("p (a b) -> p a b", a=8)
        QNs = poolW.t[:, :, :].rearrange("p a b -> p (a b)").rearrange("p (r h c) -> p r h c", r=4, h=16)
        E1s = tmp[0]
        imps = xhalo
        blk_s, score_s, wk_s = U[0], U[1], U[2]
        negb_s = SG[0].t[:, :].bitcast(BF16)
        PTc1 = PT[0].t[:, 0:256].rearrange("p (a b) -> p a b", a=8)
        idx_i = C[0].t[:, 0:128].bitcast(I32)
        ptb_i = C[1].t[:, 0:128].bitcast(I32)
        ptf = C[2]
        Vp = [C[3].t[:, :].bitcast(BF16)[:, 0:260].rearrange("p (g d) -> p g d", g=4),
              SG[1].t[:, :].bitcast(BF16)[:, 0:260].rearrange("p (g d) -> p g d", g=4)]
        Vp_b = [C[3], SG[1]]
        ugs = qbs[1].t[:, :, :].rearrange("p a b -> p (a b)").bitcast(F32)
        ug2s = oT.t[:, :, :].rearrange("p a b -> p (a b)").bitcast(F32)
        gls = hb[0].t[:, 0:512]
        ptx = Buf("ptx", qbs[0].t[:, :, :].rearrange("p a b -> p (a b)"))
        ptx.aliases.append(qbs[0])
        qbs[0].aliases.append(ptx)
        PTs = [PT[1], ptx]
        bdm = S.sb("bdm", [32, 32], BF16)
        wm0 = S.sb("wm0", [128, 32], BF16)
        rowm = S.sb("rowm", [32, 4], F32)
        iota_f = S.sb("iota_f", [128, 1], F32)
        Gq = S.sb("Gq", [32, 48], F32)

        stgs = list(stg)
        if nsub > 1:
            for k_ in range(2):
                b_ = Buf(f"stgx{k_}", X[0][1].t[:, 512 * k_:512 * (k_ + 1)])
                b_.aliases.append(X[0][1])
                X[0][1].aliases.append(b_)
                stgs.append(b_)

        def gather_page(src_d, j):
            pf = rot("stg", stgs)
            S._deps("pool", [C[0]], [pf])
            ds_ = S.dsem(pf.name)
            ins = nc.gpsimd.indirect_dma_start(out=pf[:, :], out_offset=None, in_=src_d,
                                               in_offset=bass.IndirectOffsetOnAxis(ap=idx_i[:, j:j + 1], axis=0))
            ds_[1] += 16
            ins.then_inc(ds_[0], 16)
            S._commit((ds_[0], ds_[1], "dma"), "d_" + pf.name, [C[0]], [pf])
            S.n_ins += 1
            return pf

        def sample_seq(q):
            S.dma("sp", ptb_i[:, 0:NPG], ptab_d[q].partition_broadcast(128), writes=[C[1]])
            S.op("dve", lambda: nc.vector.tensor_copy(out=ptf[:, 0:NPG], in_=ptb_i[:, 0:NPG]), reads=[C[1]], writes=[ptf])
            S.op("dve", lambda: nc.vector.tensor_scalar(out=ptf[:, 0:NPG], in0=ptf[:, 0:NPG], scalar1=128.0, scalar2=iota_f[:, 0:1],
                                                        op0=ALU.mult, op1=ALU.add), reads=[ptf, iota_f], writes=[ptf])
            S.op("dve", lambda: nc.vector.tensor_copy(out=idx_i[:, 0:NPG], in_=ptf[:, 0:NPG]), reads=[ptf], writes=[C[0]])
            S.op("dve", lambda: nc.vector.tensor_scalar(out=Gq[:, :], in0=G.t[0:32, 0, :], scalar1=rowm[:, q:q + 1], scalar2=None, op0=ALU.mult),
                 reads=[G, rowm], writes=[Gq])
            nchunk = NPG // 8
            for c in range(nchunk):
                if c > 0:
                    S.op("pool", lambda: nc.gpsimd.tensor_copy(out=cmpTs[:, :, 0:16], in_=cmpTs[:, :, 1024:1040]), reads=[VS], writes=[VS])
                for pp in range(8):
                    pg_ = c * 8 + pp
                    if pg_ == 0:
                        cq = [gather_page(ccmp_d, k_) for k_ in range(min(3, NPG))]
                    if pg_ + 3 < NPG:
                        cq.append(gather_page(ccmp_d, pg_ + 3))
                    pf = cq.pop(0)
                    kb = rot("kvb", kvb)
                    S.op("act", lambda: nc.scalar.copy(out=kb[:, :], in_=pf[:, :]), reads=[pf], writes=[kb])
                    for j in range(8):
                        S.op("pe", lambda j=j: nc.tensor.transpose(trp[0:64, j * 128:(j + 1) * 128], kb[:, j * 64:(j + 1) * 64], ident[:, :]),
                             reads=[kb, ident], writes=[trp])
                    S.op("dve", lambda pp=pp: nc.vector.tensor_copy(
                        out=cmpTs[:, :, 16 + pp * 128:16 + (pp + 1) * 128],
                        in_=trp.t[0:64, :].rearrange("p (a b) -> p a b", a=8)), reads=[trp], writes=[VS])
                kw = dict(cmpT=VS, cT=cmpTs, CKT=hF, CKTv=CKTs, CVT=CVTs_b, CVTv=CVTs_b.t, CV=Bm, CVv=CVs,
                          ug=qbs[1], ugv=ugs, ug2=oT, ug2v=ug2s, gl=hb[0], glv=gls)
                if c == 0:
                    compress_new(0, 63, 16, **kw)
                else:
                    compress_new(64 * c - 1, 64, 0, **kw)
            for g in range(4):
                S.op("dve", lambda: nc.vector.memset(imps[0:32, :], 0.0), writes=[imps])
                for hi in range(4):
                    h = 4 * g + hi
                    pieces = [(a, min(512, NCMP - a)) for a in range(0, NCMP, 512)]
                    for pi, (a, wdt) in enumerate(pieces):
                        S.op("pe", lambda a=a, wdt=wdt, pi=pi: nc.tensor.matmul(
                            acc[1][pi][0:32, 0:wdt], lhsT=QN.t[0:64, h, 0:32], rhs=CKTs[0:64, g, a:a + wdt], start=True, stop=True),
                            reads=[QN, hF], writes=[acc[1][pi]])
                        S.op("act", lambda a=a, wdt=wdt, pi=pi: nc.scalar.activation(out=E1s[0:32, a:a + wdt], in_=acc[1][pi][0:32, 0:wdt], func=AF.Exp),
                             reads=[acc[1][pi]], writes=[E1s])
                    sm = rot("small", small)
                    S.op("dve", lambda: nc.vector.tensor_reduce(out=sm[0:32, 0:1], in_=E1s[0:32, 0:NCMP], axis=mybir.AxisListType.X, op=ALU.add),
                         reads=[E1s], writes=[sm])
                    S.op("dve", lambda: nc.vector.reciprocal(out=sm[0:32, 1:2], in_=sm[0:32, 0:1]), reads=[sm], writes=[sm])
                    S.op("dve", lambda: nc.vector.tensor_scalar(out=E1s[0:32, 0:NCMP], in0=E1s[0:32, 0:NCMP], scalar1=sm[0:32, 1:2], scalar2=None, op0=ALU.mult),
                         reads=[E1s, sm], writes=[E1s])
                    S.op("dve", lambda: nc.vector.tensor_tensor(out=imps[0:32, 1:1 + NCMP], in0=imps[0:32, 1:1 + NCMP], in1=E1s[0:32, 0:NCMP], op=ALU.add),
                         reads=[imps, E1s], writes=[imps])
                    S.op("act", lambda: nc.scalar.copy(out=Pb1[:, 0:NCMP], in_=E1s[0:32, 0:NCMP]), reads=[E1s], writes=[hF])
                    nnt = (NCMP + 127) // 128
                    for nt in range(nnt):
                        w = min(128, NCMP - 128 * nt)
                        S.op("pe", lambda nt=nt, w=w: nc.tensor.transpose(trp[0:w, nt * 32:(nt + 1) * 32], Pb1[:, nt * 128:nt * 128 + w], ident[0:32, 0:32]),
                             reads=[hF, ident], writes=[trp])
                    for nt in range(nnt):
                        w = min(128, NCMP - 128 * nt)
                        S.op("dve", lambda nt=nt, w=w: nc.vector.tensor_copy(out=PTc1[0:w, nt, :], in_=trp[0:w, nt * 32:(nt + 1) * 32]),
                             reads=[trp], writes=[PT[0]])
                    ob_ = acc[0][0]
                    for nt in range(nnt):
                        w = min(128, NCMP - 128 * nt)
                        S.op("pe", lambda nt=nt, w=w: nc.tensor.matmul(
                            ob_[0:32, hi * 64:(hi + 1) * 64], lhsT=PTc1[0:w, nt, :], rhs=CVs[0:w, nt, g, :],
                            start=(nt == 0), stop=(nt == nnt - 1)), reads=[PT[0], Bm], writes=[ob_])
                    S.op("dve", lambda: nc.vector.scalar_tensor_tensor(
                        out=o_tok.t[0:32, 0, 64 * h:64 * h + 64], in0=ob_[0:32, hi * 64:(hi + 1) * 64], scalar=Gq[:, 3 * h:3 * h + 1],
                        in1=o_tok.t[0:32, 0, 64 * h:64 * h + 64], op0=ALU.mult, op1=ALU.add), reads=[ob_, Gq, o_tok], writes=[o_tok])
                nb1 = NBLK - 1
                iv = imps.t[0:32, 0:4 * nb1].rearrange("p (j f) -> p j f", f=4)
                S.op("dve", lambda: nc.vector.memset(blk_s[0:32, 0:NBLK], 0.0), writes=[blk_s])
                S.op("dve", lambda: nc.vector.tensor_tensor(out=blk_s[0:32, 0:nb1], in0=iv[:, :, 0], in1=iv[:, :, 1], op=ALU.add), reads=[imps], writes=[blk_s])
                S.op("dve", lambda: nc.vector.tensor_tensor(out=blk_s[0:32, 0:nb1], in0=blk_s[0:32, 0:nb1], in1=iv[:, :, 2], op=ALU.add), reads=[imps, blk_s], writes=[blk_s])
                S.op("dve", lambda: nc.vector.tensor_tensor(out=blk_s[0:32, 0:nb1], in0=blk_s[0:32, 0:nb1], in1=iv[:, :, 3], op=ALU.add), reads=[imps, blk_s], writes=[blk_s])
                S.op("dve", lambda: nc.vector.tensor_tensor(out=blk_s[0:32, 0:nb1 - 1], in0=blk_s[0:32, 0:nb1 - 1], in1=iv[:, 1:nb1, 0], op=ALU.add),
                     reads=[imps, blk_s], writes=[blk_s])
                S.op("dve", lambda: nc.vector.memset(blk_s[0:32, 0:1], 1e9), writes=[blk_s])
                S.op("dve", lambda: nc.vector.memset(blk_s[0:32, NBLK - 2:NBLK], 1e9), writes=[blk_s])
                S.op("dve", lambda: nc.vector.max(out=m8[0:32, 0:8], in_=blk_s[0:32, 0:NBLK]), reads=[blk_s], writes=[m8])
                S.op("dve", lambda: nc.vector.match_replace(out=wk_s[0:32, 0:NBLK], in_to_replace=m8[0:32, 0:8], in_values=blk_s[0:32, 0:NBLK], imm_value=-1e30),
                     reads=[blk_s, m8], writes=[wk_s])
                S.op("dve", lambda: nc.vector.max(out=m8[0:32, 8:16], in_=wk_s[0:32, 0:NBLK]), reads=[wk_s], writes=[m8])
                S.op("dve", lambda: nc.vector.tensor_scalar(out=wk_s[0:32, 0:NBLK], in0=blk_s[0:32, 0:NBLK], scalar1=m8[0:32, 15:16], scalar2=None, op0=ALU.is_ge),
                     reads=[blk_s, m8], writes=[wk_s])
                S.op("dve", lambda: nc.vector.memset(negb_s[0:32, 0:64 * NRG + 64], 0.0), writes=[SG[0]])
                S.op("dve", lambda: nc.vector.tensor_scalar(out=negb_s[0:32, 0:NBLK], in0=wk_s[0:32, 0:NBLK], scalar1=-1.0, scalar2=30000.0, op0=ALU.add, op1=ALU.mult),
                     reads=[wk_s], writes=[SG[0]])
                for r in range(NRG):
                    S.op("pe", lambda r=r: nc.tensor.transpose(trp[0:64, r * 32:(r + 1) * 32], negb_s[0:32, 64 * r:64 * r + 64], ident[0:32, 0:32]),
                         reads=[SG[0], ident], writes=[trp])
                S.op("dve", lambda: nc.vector.tensor_copy(
                    out=QNs[64:128, 0:NRG, 4 * g:4 * g + 4, :],
                    in_=trp.t[0:64, 0:32 * NRG].rearrange("p (r c) -> p r c", r=NRG).unsqueeze(2).to_broadcast([64, NRG, 4, 32])),
                    reads=[trp], writes=[poolW])

            def stream_attend(branch, ntile, fetch, prep):
                banks = [acc[0][0], acc[0][1], acc[1][0]]

                def region(h):
                    return banks[h // 6], (h % 6) * 65
                started = [False, False, False]

                def scores(ti, desc, g):
                    bank = rot("mm", mm)
                    if desc is None:
                        S.op("pe", lambda: nc.tensor.matmul(bank[0:32, 0:128], lhsT=tailK[:, branch - 1, g, :], rhs=QN.t[0:64, 4 * g:4 * g + 4, 0:32],
                                                            start=True, stop=True), reads=[CKT, QN], writes=[bank])
                    elif branch == 1:
                        S.op("pe", lambda: nc.tensor.matmul(bank[:, 0:128], lhsT=desc[0](g), rhs=QNs[:, desc[3], 4 * g:4 * g + 4, :],
                                                            start=True, stop=True), reads=[KE, poolW], writes=[bank])
                    else:
                        S.op("pe", lambda: nc.tensor.matmul(bank[:, 0:128], lhsT=desc[0](g), rhs=QN.t[0:64, 4 * g:4 * g + 4, 0:32],
                                                            start=True, stop=True), reads=[KW, QN], writes=[bank])
                    return bank

                def rest(ti, desc, g, bank):
                    tail = desc is None
                    nk = 32 if tail else 128
                    pt = rot("PT", PTs)
                    S.op("act", lambda: nc.scalar.activation(out=pt[0:nk, 0:128], in_=bank[0:nk, 0:128], func=AF.Exp), reads=[bank], writes=[pt])
                    if tail:
                        S.op("pool", lambda: nc.gpsimd.tensor_tensor(
                            out=pt.t[0:32, 0:128].rearrange("p (h c) -> p h c", h=4), in0=pt.t[0:32, 0:128].rearrange("p (h c) -> p h c", h=4),
                            in1=bdm[:, :].unsqueeze(1).to_broadcast([32, 4, 32]), op=ALU.mult), reads=[pt, bdm], writes=[pt])
                    elif branch == 2 and ti == 0:
                        S.op("pool", lambda: nc.gpsimd.tensor_tensor(
                            out=pt.t[:, 0:128].rearrange("p (h c) -> p h c", h=4), in0=pt.t[:, 0:128].rearrange("p (h c) -> p h c", h=4),
                            in1=wm0[:, :].unsqueeze(1).to_broadcast([128, 4, 32]), op=ALU.mult), reads=[pt, wm0], writes=[pt])
                    first = not started[0]
                    started[0] = True
                    vsrc = VW.t[0:32, 6 - branch, g, :] if tail else desc[1](g)
                    vbuf = VW if (tail or branch == 2) else Vp_b[ti % 2]
                    S.op("pe", lambda first=first, vsrc=vsrc: nc.tensor.matmul(
                        banks[0][0:65, g * 128:(g + 1) * 128], lhsT=vsrc, rhs=pt[0:nk, 0:128], start=first, stop=tail, skip_group_check=True),
                        reads=[pt, vbuf], writes=[banks[0]])

                AHEAD = 3
                pfs = {}
                for t_ in range(min(AHEAD, ntile)):
                    pfs[t_] = fetch(t_)
                descs = {0: prep(0, pfs.pop(0))}
                work = []
                for ti in range(ntile + 1):
                    for g in range(4):
                        work.append((ti, g))
                pend = scores(0, descs[0], 0)
                for wi, (ti, g) in enumerate(work):
                    if g == 0 and ti < ntile:
                        if ti + AHEAD < ntile:
                            pfs[ti + AHEAD] = fetch(ti + AHEAD)
                        if ti + 1 < ntile:
                            descs[ti + 1] = prep(ti + 1, pfs.pop(ti + 1))
                    cur = pend
                    if wi + 1 < len(work):
                        nti, ng = work[wi + 1]
                        pend = scores(nti, descs.get(nti) if nti < ntile else None, ng)
                    rest(ti, descs.get(ti) if ti < ntile else None, g, cur)
                    if g == 3 and ti in descs:
                        del descs[ti]
                ots = rot("stg", stgs)
                S.op("dve", lambda: nc.vector.tensor_copy(out=ots[0:65, 0:512], in_=banks[0][0:65, 0:512]), reads=[banks[0]], writes=[ots])
                for h in range(16):
                    tb = rot("mm", mm)
                    S.op("pe", lambda h=h: nc.tensor.transpose(tb[0:32, 0:65], ots[0:65, h * 32:(h + 1) * 32], identF[0:65, 0:65]),
                         reads=[ots, identF], writes=[tb])
                    sm = rot("small", small)
                    S.op("dve", lambda: nc.vector.reciprocal(out=sm[0:32, 0:1], in_=tb[0:32, 64:65]), reads=[tb], writes=[sm])
                    S.op("dve", lambda h=h: nc.vector.tensor_tensor(out=sm[0:32, 1:2], in0=sm[0:32, 0:1], in1=Gq[:, 3 * h + branch:3 * h + branch + 1], op=ALU.mult),
                         reads=[sm, Gq], writes=[sm])
                    S.op("dve", lambda h=h: nc.vector.scalar_tensor_tensor(
                        out=o_tok.t[0:32, 0, 64 * h:64 * h + 64], in0=tb[0:32, 0:64], scalar=sm[0:32, 1:2],
                        in1=o_tok.t[0:32, 0, 64 * h:64 * h + 64], op0=ALU.mult, op1=ALU.add), reads=[tb, sm, o_tok], writes=[o_tok])

            def slc_fetch(j):
                return gather_page(cslc_d, j)

            def slc_prep(j, pf):
                kb = rot("kvb", kvb)
                S.op("act", lambda: nc.scalar.copy(out=kb[:, 0:256], in_=pf[:, 0:256]), reads=[pf], writes=[kb])
                vb, vv = Vp_b[j % 2], Vp[j % 2]
                S.op("pool", lambda: nc.gpsimd.tensor_copy(out=vv[:, :, 0:64], in_=pf.t[:, 256:512].rearrange("p (g d) -> p g d", g=4)), reads=[pf], writes=[vb])
                for gg in range(4):
                    S.op("pe", lambda gg=gg: nc.tensor.transpose(trp[0:64, gg * 128:(gg + 1) * 128], kb[:, gg * 64:(gg + 1) * 64], ident[:, :]),
                         reads=[kb, ident], writes=[trp])
                kt = j % 32
                S.op("dve", lambda: nc.vector.tensor_copy(out=KE.t[0:64, :, kt * 128:(kt + 1) * 128],
                                                          in_=trp.t[0:64, 0:512].rearrange("p (a b) -> p a b", a=4)), reads=[trp], writes=[KE])
                return (lambda g: KE.t[:, g, kt * 128:(kt + 1) * 128]), (lambda g: vv[:, g, :]), 128, j // 32

            for vb, vv in zip(Vp_b, Vp):
                S.op("dve", lambda vv=vv: nc.vector.memset(vv[:, :, 64:65], 1.0), writes=[vb])
            stream_attend(1, NPG, slc_fetch, slc_prep)

            def win_fetch(i):
                pf = rot("stg", stgs)
                S.dma("sp", pf[:, :], swin_d[q, i * 128:(i + 1) * 128, :], writes=[pf], key="hw_" + pf.name)
                return pf

            def win_prep(i, pf):
                kb = rot("kvb", kvb)
                S.op("act", lambda: nc.scalar.copy(out=kb[:, 0:256], in_=pf[:, 0:256]), reads=[pf], writes=[kb])
                S.op("pool", lambda: nc.gpsimd.tensor_copy(out=VW.t[:, i, :, 0:64], in_=pf.t[:, 256:512].rearrange("p (g d) -> p g d", g=4)), reads=[pf], writes=[VW])
                for gg in range(4):
                    S.op("pe", lambda gg=gg: nc.tensor.transpose(trp[0:64, gg * 128:(gg + 1) * 128], kb[:, gg * 64:(gg + 1) * 64], ident[:, :]),
                         reads=[kb, ident], writes=[trp])
                S.op("dve", lambda: nc.vector.tensor_copy(out=KW.t[0:64, :, i * 128:(i + 1) * 128],
                                                          in_=trp.t[0:64, 0:512].rearrange("p (a b) -> p a b", a=4)), reads=[trp], writes=[KW])
                return (lambda g: KW.t[0:64, g, i * 128:(i + 1) * 128]), (lambda g: VW.t[:, i, g, :]), 128, 0
            stream_attend(2, 4, win_fetch, win_prep)

        def nsa_sample(subs):
            S.dma("sp", wd.t[:, 0:8, :], wo_b.rearrange("p (a b) -> p a b", a=8), reads=[castB], writes=[wd], key="wd")
            S.dma("pool", bdm[:, :], bdm_d, writes=[bdm])
            S.dma("pool", wm0[:, :], wm0_d, writes=[wm0])
            S.dma("sp", rowm[:, :], rowm_d, writes=[rowm])
            S.dma("sp", iota_f[:, :], iota_d, writes=[iota_f])
            S.dma("sp", winkv_s[:, 0:504, :], swin_d[:, 8:512, :], key="o_pool")
            nsa_project(0, subs, sample=True)
            S.op("dve", lambda: nc.vector.memset(o_tok.t[0:32, 0, :], 0.0), writes=[o_tok])
            for r in range(NRG):
                S.op("dve", lambda r=r: nc.vector.tensor_copy(out=QNs[0:64, r, :, :], in_=QN.t[0:64, :, 0:32]), reads=[QN], writes=[poolW])
            for q in range(4):
                sample_seq(q)
            out_proj(subs)

        for i in range(ntile):
            t0 = i * TT
            Xi = X[0]
            for s in range(nsub):
                S.dma("sp", Xi[s][:, :], xp[t0 + s * 128:t0 + (s + 1) * 128, :], writes=[Xi[s]])
            for s in range(nsub):
                first = (i == 0 and s == 0)
                prev = xhalo if s == 0 else Xi[s - 1]
                for hh in range(2):
                    bank = rot("mm", mm)
                    for cc in range(4):
                        c = hh * 4 + cc
                        w = c // 2
                        S.op("pe", lambda c=c, cc=cc, w=w, bank=bank: nc.tensor.matmul(
                            bank[:, cc * 128:(cc + 1) * 128], lhsT=Xi[s][:, c * 128:(c + 1) * 128],
                            rhs=Bm.t[:, w * 3 + (1 if first else 0), :], start=True, stop=first),
                            reads=[Xi[s], Bm], writes=[bank])
                        if not first:
                            S.op("pe", lambda c=c, cc=cc, w=w, bank=bank: nc.tensor.matmul(
                                bank[:, cc * 128:(cc + 1) * 128], lhsT=prev[:, c * 128:(c + 1) * 128],
                                rhs=Bm.t[:, w * 3 + 2, :], start=False, stop=True),
                                reads=[prev, Bm], writes=[bank])
                    S.op("act", lambda hh=hh, bank=bank: nc.scalar.copy(
                        out=dT.t[:, hh * 4:hh * 4 + 4, s * 128:(s + 1) * 128],
                        in_=bank.t[:, :].rearrange("p (a b) -> p a b", a=4)), reads=[bank], writes=[dT])
            S.op("pool", lambda: nc.gpsimd.tensor_copy(out=xhalo[:, :], in_=Xi[nsub - 1][:, :]), reads=[Xi[nsub - 1]], writes=[xhalo])
            subs = [(Xi[s], Xi[s][:, :], 128, s * 128) for s in range(nsub)]
            mixer_pool_finish(subs)
            ffn(0, subs, TT, 1, TT, True, i == ntile - 1)
            for (xb_buf, xs_ap, P, col0) in subs:
                to_hT(xb_buf, xs_ap, P, col0)
            nsa_prompt(i, subs)
            ffn(1, subs, TT, 1, TT, True, i == ntile - 1)
            for s in range(nsub):
                S.dma("sp", y_p[t0 + s * 128:t0 + (s + 1) * 128, :], Xi[s][:, :], reads=[Xi[s]], key="o_" + Xi[s].name)

        Xs = X[0][0]
        S.op("dve", lambda: nc.vector.memset(XS[:, :], 0.0), writes=[XS])
        S.dma("sp", XS[0:92, :], xscat.rearrange("a b c -> (a b) c"), writes=[XS])
        S.dma("sp", Xs[0:32, :], xs, writes=[Xs])
        for hh in range(2):
            bank = rot("mm", mm)
            for cc in range(4):
                c = hh * 4 + cc
                S.op("pe", lambda c=c, cc=cc, bank=bank: nc.tensor.matmul(
                    bank[:, cc * 128:cc * 128 + 32], lhsT=XS[:, c * 128:(c + 1) * 128], rhs=Bs.t[:, c // 2, :],
                    start=True, stop=True), reads=[XS, Bs], writes=[bank])
            S.op("act", lambda hh=hh, bank=bank: nc.scalar.copy(
                out=dT.t[:, hh * 4:hh * 4 + 4, 0:32],
                in_=bank.t[:, :].rearrange("p (a b) -> p a b", a=4)[:, :, 0:32]), reads=[bank], writes=[dT])
        subs = [(Xs, Xs[0:32, :], 32, 0)]
        mixer_pool_finish(subs)
        ffn(0, subs, 32, 4, 8, False, True)
        to_hT(Xs, Xs[0:32, :], 32, 0)
        nsa_sample(subs)
        ffn(1, subs, 32, 4, 8, False, True)
        S.dma("sp", y_s, Xs[0:32, :], reads=[Xs], key="o_" + Xs.name)
        S.finish("sp")
    return nc


def pool_consts():
    bm = np.zeros((128, 12, 128), np.float32)
    bs = np.zeros((128, 4, 32), np.float32)
    for wi, win in enumerate(POOL_WINDOWS):
        for t in range(128):
            for s in range(max(0, t - win + 1), t + 1):
                bm[s, wi * 3 + 0, t] += 1.0 / win
                bm[s, wi * 3 + 1, t] += 1.0 / min(t + 1, win)
            bm[t, wi * 3 + 0, t] -= 1.0
            bm[t, wi * 3 + 1, t] -= 1.0
            for s in range(t - win + 1, 0):
                bm[128 + s, wi * 3 + 2, t] += 1.0 / win
        for q in range(4):
            for t in range(8):
                r = 15 + t
                for i in range(r - win + 1, r + 1):
                    bs[q * 23 + i, wi, q * 8 + t] += 1.0 / win
                bs[q * 23 + r, wi, q * 8 + t] -= 1.0
    return bm.reshape(128, -1), bs.reshape(128, -1)


def fc_features():
    idx = np.zeros((NFC, 128), np.int64)
    for pb in range(NPB):
        for fc in range(4):
            base = (0 if fc < 2 else DFF) + 256 * pb + 128 * (fc % 2)
            idx[pb * 4 + fc] = base + np.arange(128)
    return idx


def prep_shared(inp):
    f = np.float32
    sh = {}
    ln_g, ln_b = np.asarray(inp["ln_g"], f), np.asarray(inp["ln_b"], f)
    lnp = np.zeros((4, 128, 2 * D), f)
    for l in range(2):
        for k in range(2):
            lnp[l * 2 + k, :, :D] = ln_g[l, k][None]
            lnp[l * 2 + k, :, D:] = ln_b[l, k][None]
    sh["lnp"] = lnp
    sh["pscale"] = np.ascontiguousarray(np.broadcast_to(np.asarray(inp["pool_scale"], f)[0][None], (128, D)))
    pw = np.asarray(inp["pool_w"], f)[0]
    sh["poolw"] = np.ascontiguousarray(pw.reshape(4, 2, 128, 256).transpose(2, 0, 1, 3).reshape(128, 8 * 256))
    bm, bs = pool_consts()
    sh["bmat"], sh["bsamp"] = bm, bs
    sh["ident"] = np.eye(128, dtype=f)
    wu = np.asarray(inp["ffn_w_up"], f)
    idx = fc_features()
    wup = np.zeros((2, NPB, 128, 8, 512), f)
    for l in range(2):
        for pb in range(NPB):
            cols = idx[pb * 4:pb * 4 + 4].reshape(-1)
            blk = wu[l][:, cols]
            wup[l, pb] = blk.reshape(8, 128, 512).transpose(1, 0, 2)
    sh["wup"] = wup.reshape(2, NPB, 128, 8 * 512)
    wdn = np.asarray(inp["ffn_w_down"], f)
    sh["wdown"] = np.ascontiguousarray(wdn.reshape(2, NKC, 128, D).transpose(0, 2, 1, 3).reshape(2, 128, NKC * D))
    cwv = np.asarray(inp["ffn_conv_w"], f)
    cbv = np.asarray(inp["ffn_conv_b"], f)
    cwb = np.zeros((2, 128, NFC, 4), f)
    for l in range(2):
        for k in range(3):
            cwb[l, :, :, k] = cwv[l, k][idx].T
        cwb[l, :, :, 3] = cbv[l][idx].T
    sh["cwb"] = cwb.reshape(2, 128, NFC * 4)
    w_in = np.asarray(inp["nsa_w_in"], f)[0]
    wpad = np.zeros((D, 6 * 512), f)
    wpad[:, :2608] = w_in
    sh["win"] = np.ascontiguousarray(wpad.reshape(8, 128, 6, 512).transpose(2, 1, 0, 3).reshape(6, 128, 8 * 512))
    w_o = np.asarray(inp["nsa_w_o"], f)[0]
    sh["wo"] = np.ascontiguousarray(w_o.reshape(8, 128, D).transpose(1, 0, 2).reshape(128, 8 * D))
    w1 = np.asarray(inp["cmp_w1"], f)[0]
    sh["w1"] = np.ascontiguousarray(w1.transpose(0, 2, 1, 3).reshape(2, 64, 32 * 128))
    sh["posT"] = np.ascontiguousarray(np.asarray(inp["cmp_pos"], f)[0].transpose(2, 0, 1).reshape(64, 64))
    sh["b1T"] = np.ascontiguousarray(np.asarray(inp["cmp_b1"], f)[0].T)
    sh["w2"] = np.ascontiguousarray(np.asarray(inp["cmp_w2"], f)[0].transpose(1, 0, 2).reshape(128, 128))
    sh["b2T"] = np.ascontiguousarray(np.asarray(inp["cmp_b2"], f)[0].T)
    return sh, idx


def nsa_consts(T, pos0=0):
    f = np.float32
    c = {}
    inv = np.power(f(500000.0), (f(-2.0) * np.arange(8, dtype=f) / f(16.0))).astype(f)
    ang = (np.arange(pos0, pos0 + T).astype(f)[:, None] * inv[None, :]).astype(f)
    c["rope"] = np.concatenate([np.cos(ang), np.sin(ang)], 1).astype(f)
    k = np.arange(128)
    tri = np.zeros((128, 2, 128), f)
    tri[:, 0, :] = (k[:, None] <= k[None, :])
    tri[:, 1, :] = (k[:, None] > k[None, :])
    c["tri"] = tri.reshape(128, 256)
    c["pat"] = (k[:, None] >= 16 * np.arange(8)[None, :] + 15).astype(f)
    c["emat"] = (np.arange(max(T, 4096))[None, :] // 64 == np.arange(64)[:, None]).astype(f)
    t = np.arange(T)
    j = np.arange(64)
    cur = t // 64
    valid = (j[None, :] * 64 <= t[:, None])
    forced = (j[None, :] == 0) | (j[None, :] == cur[:, None]) | (j[None, :] == cur[:, None] - 1)
    mul = (valid & ~forced).astype(f)
    add = np.where(valid & forced, f(1e9), np.where(valid, f(0.0), f(-1.0))).astype(f)
    c["scm"] = np.ascontiguousarray(np.stack([mul, add], 1).reshape(T // 128, 128, 128))
    return c


def sample_consts(past_len):
    f = np.float32
    c = {}
    inv = np.power(f(500000.0), (f(-2.0) * np.arange(8, dtype=f) / f(16.0))).astype(f)
    pos = np.tile(past_len + np.arange(8), 4).astype(f)
    ang = (pos[:, None] * inv[None, :]).astype(f)
    c["rope_s"] = np.concatenate([np.cos(ang), np.sin(ang)], 1).astype(f)
    r = np.arange(32)
    c["bdm"] = ((r[:, None] // 8 == r[None, :] // 8) & (r[:, None] % 8 <= r[None, :] % 8)).astype(f)
    c["wm0"] = (np.arange(128)[:, None] > (r[None, :] % 8)).astype(f)
    c["rowm"] = (r[:, None] // 8 == np.arange(4)[None, :]).astype(f)
    c["iota"] = np.arange(128, dtype=f).reshape(128, 1)
    return c


def kernel(**inp):
    f = np.float32
    T, TT = 4096, 256
    nc = build_nc(T, TT)
    sh, idx = prep_shared(inp)
    sh.update(nsa_consts(T))
    sh.update(sample_consts(16384))
    ccmp = np.asarray(inp["cache_cmp_kv"], f)[0]
    cslc = np.asarray(inp["cache_slc_kv"], f)[0]
    sh["ccmp"] = ccmp.reshape(ccmp.shape[0] * 128, 512)
    sh["cslc"] = cslc.reshape(cslc.shape[0] * 128, 512)
    swin = np.asarray(inp["state_win_kv"], f)[0].reshape(32, 512, 512)
    ptab = np.asarray(inp["page_table"], np.int32)
    x_prompt = np.asarray(inp["x_prompt"], f)
    x_sample = np.asarray(inp["x_sample"], f)
    state_pool = np.asarray(inp["state_pool"], f)[0]
    state_ffn = np.asarray(inp["state_ffn"], f)
    in_maps = []
    for c in range(NCORES):
        m = dict(sh)
        m["xp"] = np.ascontiguousarray(x_prompt[c])
        sl = slice(4 * c, 4 * c + 4)
        m["xs"] = np.ascontiguousarray(x_sample[sl].reshape(32, D))
        m["xscat"] = np.ascontiguousarray(np.concatenate([state_pool[sl], x_sample[sl]], 1))
        sf = state_ffn[:, sl]
        m["sffn"] = np.ascontiguousarray(sf[:, :, :, idx].transpose(0, 4, 3, 1, 2).reshape(2, 128, NFC * 8))
        m["swin"] = np.ascontiguousarray(swin[sl])
        m["ptab"] = np.ascontiguousarray(ptab[sl])
        in_maps.append(m)
    res = run_bass_kernel_spmd(nc, in_maps, core_ids=list(range(NCORES)))
    R = res.results
    y_prompt = np.stack([R[c]["y_p"] for c in range(NCORES)], 0)
    y_sample = np.concatenate([R[c]["y_s"].reshape(4, 8, D) for c in range(NCORES)], 0)
    pool_p = np.stack([R[c]["pool_p"] for c in range(NCORES)], 0)[None]
    pool_s = np.concatenate([R[c]["pool_s"] for c in range(NCORES)], 0)[None]
    ffn_p = np.zeros((2, NCORES, 2, 2 * DFF), f)
    ffn_s = np.zeros((2, 4 * NCORES, 2, 2 * DFF), f)
    for c in range(NCORES):
        fp = R[c]["ffn_p"].reshape(2, 128, NFC, 2)
        fs = R[c]["ffn_s"].reshape(2, 128, NFC, 4, 2)
        for l in range(2):
            ffn_p[l, c][:, idx.reshape(-1)] = fp[l].transpose(2, 1, 0).reshape(2, -1)
            ffn_s[l, 4 * c:4 * c + 4][:, :, idx.reshape(-1)] = fs[l].transpose(2, 3, 1, 0).reshape(4, 2, -1)
    z = lambda *s: np.zeros(s, f)
    cmp_p = np.stack([R[c]["cmpkv_p"] for c in range(NCORES)], 0).reshape(1, NCORES, T, 2, 4, 64)
    slc_p = np.stack([R[c]["slckv_p"] for c in range(NCORES)], 0).reshape(1, NCORES, T, 2, 4, 64)
    win_p = np.stack([R[c]["winkv_p"] for c in range(NCORES)], 0).reshape(1, NCORES, 512, 2, 4, 64)
    cmp_s = np.concatenate([R[c]["cmpkv_s"] for c in range(NCORES)], 0).reshape(1, 32, 8, 2, 4, 64)
    slc_s = np.concatenate([R[c]["slckv_s"] for c in range(NCORES)], 0).reshape(1, 32, 8, 2, 4, 64)
    win_s = np.concatenate([R[c]["winkv_s"] for c in range(NCORES)], 0).reshape(1, 32, 512, 2, 4, 64)
    return (y_prompt, y_sample, pool_p, pool_s, cmp_p, cmp_s, slc_p, slc_s, win_p, win_s, ffn_p, ffn_s)
```

```python
import numpy as np
from contextlib import ExitStack
import concourse.bass as bass
import concourse.mybir as mybir
from concourse.bass_utils import run_bass_kernel_spmd

F32 = mybir.dt.float32
BF16 = mybir.dt.bfloat16
I32 = mybir.dt.int32
ALU = mybir.AluOpType
AF = mybir.ActivationFunctionType

D = 1024
DFF = 2816
NPB = 11
NFC = 44
NKC = 22
ALPHA = float((2.0 * 2) ** 0.25)
LN_EPS = 1e-5
POOL_WINDOWS = (2, 4, 8, 16)
NCORES = 8
SAME_ENGINE_INORDER = False


class Buf:
    __slots__ = ("name", "w", "r", "aliases", "t")

    def __init__(self, name, t=None):
        self.name = name
        self.w = {}
        self.r = {}
        self.aliases = []
        self.t = t

    def __getitem__(self, k):
        return self.t[k]


class Sched:
    def __init__(self, nc, stack):
        self.nc = nc
        self.stack = stack
        self.eng = {"pe": nc.tensor, "act": nc.scalar, "dve": nc.vector,
                    "pool": nc.gpsimd, "sp": nc.sync}
        self.sem = {}
        self.cnt = {}
        for k in ("pe", "act", "dve", "pool"):
            self.sem[k] = stack.enter_context(nc.semaphore("c_" + k))
            self.cnt[k] = 0
        self.seen = {k: {} for k in self.eng}
        self.dsems = {}
        self.n_ins = 0

    def sb(self, name, shape, dtype):
        t = self.stack.enter_context(self.nc.sbuf_tensor("s_" + name, list(shape), dtype))
        return Buf(name, t)

    def ps(self, name, shape, dtype):
        t = self.stack.enter_context(self.nc.psum_tensor("p_" + name, list(shape), dtype))
        return Buf(name, t)

    def dsem(self, key):
        if key not in self.dsems:
            s = self.stack.enter_context(self.nc.semaphore("d_" + key))
            self.dsems[key] = [s, 0]
        return self.dsems[key]

    def _wait(self, e, tok):
        sem, val, owner = tok
        if owner == e and (e == "pe" or SAME_ENGINE_INORDER):
            return
        name = sem.name
        if self.seen[e].get(name, 0) >= val:
            return
        self.seen[e][name] = val
        self.eng[e].wait_ge(sem, val)

    def _deps(self, e, reads, writes, skip=None):
        toks = {}

        def add(d):
            for k, tok in d.items():
                if k == skip:
                    continue
                if k not in toks or toks[k][1] < tok[1]:
                    toks[k] = tok
        for b in reads:
            add(b.w)
            for a in b.aliases:
                add(a.w)
        for b in writes:
            add(b.w)
            add(b.r)
            for a in b.aliases:
                add(a.w)
                add(a.r)
        for tok in toks.values():
            self._wait(e, tok)

    def _commit(self, tok, key, reads, writes):
        for b in reads:
            b.r[key] = tok
        for b in writes:
            if key.startswith("d_") and key in b.w:
                b.w[key] = tok
            else:
                b.w = {key: tok}
            b.r = {}

    def op(self, e, fn, reads=(), writes=()):
        self._deps(e, reads, writes)
        ins = fn()
        self.cnt[e] += 1
        ins.then_inc(self.sem[e], 1)
        tok = (self.sem[e], self.cnt[e], e)
        self._commit(tok, "c_" + e, reads, writes)
        self.n_ins += 1
        return ins

    def dma(self, q, out, in_, reads=(), writes=(), key=None, **kw):
        if key is None:
            key = (writes[0] if writes else reads[0]).name
        self._deps(q, reads, writes, skip="d_" + key)
        ds = self.dsem(key)
        ins = self.eng[q].dma_start(out=out, in_=in_, **kw)
        ds[1] += 16
        ins.then_inc(ds[0], 16)
        tok = (ds[0], ds[1], "dma")
        self._commit(tok, "d_" + key, reads, writes)
        self.n_ins += 1
        return ins

    def finish(self, e="sp"):
        for key, (s, v) in self.dsems.items():
            if v:
                self._wait(e, (s, v, "dma"))
        for k in ("pe", "act", "dve", "pool"):
            if self.cnt[k]:
                self._wait(e, (self.sem[k], self.cnt[k], k))


class WStream:
    def __init__(self, S, slots, seq):
        self.S = S
        self.slots = slots
        self.seq = seq
        self.issued = 0
        self.cons = 0

    def _issue(self):
        k = self.issued
        if k >= len(self.seq):
            return
        slot = self.slots[k % len(self.slots)]
        src, ncol, cbuf = self.seq[k]
        npart = src.shape[0]
        self.S.dma("sp", slot.t[0:npart, 0:ncol], src, reads=[cbuf], writes=[slot], key=slot.name)
        self.issued += 1

    def next(self):
        n = len(self.slots)
        while self.issued < min(self.cons + n, len(self.seq)) and self.issued <= self.cons + n - 1:
            self._issue()
        slot = self.slots[self.cons % n]
        self.cons += 1
        return slot


def build_nc(T, TT, NPG=128, NPH=5120):
    nsub = TT // 128
    ntile = T // TT
    nc = bass.Bass("TRN2", target_bir_lowering=False)

    def din(name, shape, dt=F32):
        return nc.dram_tensor(name, list(shape), dt, kind="ExternalInput").ap()

    def dout(name, shape, dt=F32):
        return nc.dram_tensor(name, list(shape), dt, kind="ExternalOutput").ap()

    xp = din("xp", [T, D])
    xs = din("xs", [32, D])
    xscat = din("xscat", [4, 23, D])
    lnp = din("lnp", [4, 128, 2 * D])
    pscale = din("pscale", [128, D])
    poolw = din("poolw", [128, 8 * 256])
    bmat = din("bmat", [128, 12 * 128])
    bsamp = din("bsamp", [128, 4 * 32])
    ident_d = din("ident", [128, 128])
    wup = din("wup", [2, NPB, 128, 8 * 512])
    wdown = din("wdown", [2, 128, NKC * D])
    cwb = din("cwb", [2, 128, NFC * 4])
    sffn = din("sffn", [2, 128, NFC * 8])

    win_d = din("win", [6, 128, 8 * 512])
    wo_d = din("wo", [128, 8 * D])
    w1_d = din("w1", [2, 64, 32 * 128])
    posT_d = din("posT", [64, 2 * 32])
    b1T_d = din("b1T", [128, 2])
    w2_d = din("w2", [128, 2 * 64])
    b2T_d = din("b2T", [64, 2])
    rope_d = din("rope", [T, 16])
    tri_d = din("tri", [128, 2 * 128])
    pat_d = din("pat", [128, 8])
    emat_d = din("emat", [64, max(T, 4096)])
    scm_d = din("scm", [T // 128, 128, 2 * 64])
    rope_s_d = din("rope_s", [32, 16])
    bdm_d = din("bdm", [32, 32])
    wm0_d = din("wm0", [128, 32])
    rowm_d = din("rowm", [32, 4])
    iota_d = din("iota", [128, 1])
    ptab_d = din("ptab", [4, NPG], I32)
    ccmp_d = din("ccmp", [NPH * 128, 512])
    cslc_d = din("cslc", [NPH * 128, 512])
    swin_d = din("swin", [4, 512, 512])
    cmpkv_s = dout("cmpkv_s", [32, 512])
    slckv_s = dout("slckv_s", [32, 512])
    winkv_s = dout("winkv_s", [4, 512, 512])
    cmpkv_p = dout("cmpkv_p", [T, 512])
    slckv_p = dout("slckv_p", [T, 512])
    winkv_p = dout("winkv_p", [min(512, T), 512])

    def dscr(name, shape):
        return nc.dram_tensor(name, list(shape), BF16, kind="Internal").ap()
    wup_b = dscr("wup_b", [2, NPB, 128, 8 * 512])
    wdown_b = dscr("wdown_b", [2, 128, NKC * D])
    win_b = dscr("win_b", [6, 128, 8 * 512])
    wo_b = dscr("wo_b", [128, 8 * D])
    w1_b = dscr("w1_b", [2, 64, 32 * 128])

    y_p = dout("y_p", [T, D])
    y_s = dout("y_s", [32, D])
    pool_p = dout("pool_p", [15, D])
    pool_s = dout("pool_s", [4, 15, D])
    ffn_p = dout("ffn_p", [2, 128, NFC * 2])
    ffn_s = dout("ffn_s", [2, 128, NFC * 8])

    with ExitStack() as st:
        S = Sched(nc, st)
        ident = S.sb("identb", [128, 128], BF16)
        identF = S.sb("identF", [128, 65], F32)
        poolW = S.sb("poolW", [128, 8, 256], BF16)
        Bm = S.sb("Bm", [128, 12, 128], F32)
        Bs = S.sb("Bs", [128, 4, 32], F32)
        cw = S.sb("cw", [128, 2, NFC, 4], F32)
        halo = S.sb("halo", [128, 2, NFC, 2], F32)
        hc = S.sb("hc", [128, 2, NFC, 2], F32)
        hc2 = S.sb("hc2", [128, NFC], F32)
        shalo = S.sb("shalo", [128, 2, NFC, 8], F32)
        sout = shalo
        xhalo = S.sb("xhalo", [128, D], F32)
        XS = xhalo
        X = [[S.sb(f"X{b}_{s}", [128, D], F32) for s in range(nsub)] for b in range(1)]
        hT = S.sb("hT", [128, 8, TT], BF16)
        hb = [S.sb(f"hb{i}", [128, D], BF16) for i in range(1)]
        tmp = [S.sb(f"tmp{i}", [128, D], F32) for i in range(1)]
        lnslot = [S.sb(f"lnslot{i}", [128, D], F32) for i in range(1)]
        wslots = [S.sb(f"wslot{i}", [128, 8 * 512], BF16) for i in range(2)]
        wd = S.sb("wd", [128, NKC, D], BF16)
        hF = S.sb("hF", [128, NKC, TT], BF16)
        dT = Buf("dT", hF.t[:, 0:8, :])
        dT.aliases.append(hF)
        hF.aliases.append(dT)
        U = [S.sb(f"U{i}", [128, TT + 8], F32) for i in range(3)]
        C = [S.sb(f"C{i}", [128, TT], F32) for i in range(4)]
        SG = [S.sb(f"SG{i}", [128, TT], F32) for i in range(2)]
        stat = [S.sb(f"stat{i}", [128, 16], F32) for i in range(4)]
        NKT = max(T // 128, 32)
        KE = S.sb("KE", [128, 4, NKT * 128], BF16)
        VS = S.sb("VS", [128, NKT, 4, 65], BF16)
        KW = S.sb("KW", [64, 4, 768], BF16)
        VW = S.sb("VW", [128, 6, 4, 65], BF16)
        CKT = S.sb("CKT", [64, 4, 256], BF16)
        CVT = S.sb("CVT", [64, 4, 256], BF16)
        CV = S.sb("CV", [128, 2, 4, 64], BF16)
        cmpT = S.sb("cmpT", [64, 8, 16 + TT], BF16)
        posT = S.sb("posT", [64, 2, 32], BF16)
        b1T = S.sb("b1T", [128, 2], F32)
        hbias = S.sb("hbias", [128, 2], F32)
        w2 = S.sb("w2", [128, 2, 64], BF16)
        b2T = S.sb("b2T", [64, 2], F32)
        tri = S.sb("tri", [128, 2, 128], BF16)
        pat = S.sb("pat", [128, 8], F32)
        qbs = [S.sb(f"qb{i}", [128, 16, 64], BF16) for i in range(nsub)]
        kvb = [S.sb(f"kvb{i}", [128, 512], BF16) for i in range(2)]
        Rs = S.sb("Rs", [128, 4, 8, 8], F32)
        G = S.sb("G", [128, nsub, 48], F32)
        ropeT = S.sb("ropeT", [128, nsub, 16], F32)
        imp = S.sb("imp", [128, 272], F32)
        blk = S.sb("blk", [128, 64], F32)
        score = S.sb("score", [128, 64], F32)
        wk = S.sb("wk", [128, 64], F32)
        m8 = S.sb("m8", [128, 16], F32)
        negb = S.sb("negb", [128, 64], BF16)
        scm = [S.sb(f"scm{i}", [128, 2, 64], F32) for i in range(2)]
        PT = [S.sb(f"PT{i}", [128, 512], BF16) for i in range(2)]
        oT = S.sb("oT", [128, 8, 128], BF16)
        ug = S.sb("ug", [128, 8 * (TT // 16)], F32)
        ug2 = S.sb("ug2", [128, 8 * (TT // 16)], F32)
        gl = S.sb("gl", [128, 8 * (TT // 16)], BF16)
        small = [S.sb(f"small{i}", [128, 8], F32) for i in range(4)]
        wdflat = wd.t[:, :, :].rearrange("p a b -> p (a b)")

        def carve(name, off, nbytes, dtype, pattern=None, **kw):
            ap = wdflat[:, off // 2:(off + nbytes) // 2]
            if dtype != BF16:
                ap = ap.bitcast(dtype)
            if pattern:
                ap = ap.rearrange(pattern, **kw)
            b = Buf(name, ap)
            b.aliases.append(wd)
            wd.aliases.append(b)
            return b
        QN = carve("QN", 16384, 8192, BF16, "p (a b) -> p a b", a=16)
        o_tok = carve("o_tok", 24576, 8192, F32, "p (a b) -> p a b", a=nsub)
        stg = [carve(f"stg{i}", 32768 + 2048 * i, 2048, F32) for i in range(2)]
        E4 = carve("E4", 36864, 4096, F32, "p (a b) -> p a b", a=4)
        Pb = carve("Pb", 40960, 2048, BF16, "p (a b) -> p a b", a=4)
        PTc = carve("PTc", 43008, 2048, BF16, "p (a b c) -> p a b c", a=4, b=2)
        mm = [S.ps(f"mm{i}", [128, 512], F32) for i in range(3)]
        trp = S.ps("trp", [128, 1024], BF16)
        acc = [[S.ps(f"acc{j}_{h}", [128, 512], F32) for h in range(2)] for j in range(2)]

        cnt = {"mm": 0, "U": 0, "C": 0, "stat": 0, "ln": 0, "acc": 0, "hb": 0, "tmp": 0, "stg": 0, "kvb": 0, "scm": 0, "PT": 0, "small": 0}

        def rot(name, lst):
            i = cnt[name]
            cnt[name] += 1
            return lst[i % len(lst)]

        S.dma("pool", ident[:, :], ident_d, writes=[ident])
        S.dma("sp", identF[:, :], ident_d[:, 0:65], writes=[identF])
        S.dma("pool", poolW.t[:, :, :], poolw.rearrange("p (a b) -> p a b", a=8), writes=[poolW])
        S.dma("sp", Bm.t[:, :, :], bmat.rearrange("p (a b) -> p a b", a=12), writes=[Bm])
        S.dma("sp", Bs.t[:, :, :], bsamp.rearrange("p (a b) -> p a b", a=4), writes=[Bs])
        for l in range(2):
            S.dma("sp", cw.t[:, l, :, :], cwb[l].rearrange("p (a b) -> p a b", a=NFC), writes=[cw], key="cw")
            S.dma("sp", shalo.t[:, l, :, :], sffn[l].rearrange("p (a b) -> p a b", a=NFC), writes=[shalo], key="shalo")
        S.op("dve", lambda: nc.vector.memset(halo.t[:, :, :, :], 0.0), writes=[halo])
        S.op("dve", lambda: nc.vector.memset(hc.t[:, :, :, :], 0.0), writes=[hc])
        S.dma("sp", pool_p, xp[T - 15:T, :], key="o_pool")
        S.dma("sp", pool_s, xscat[:, 8:23, :], key="o_pool")

        castA = Buf("castA")
        castB = Buf("castB")
        for pb in range(NPB):
            S.dma("pool", wup_b[0, pb], wup[0, pb], writes=[castA], key="castA")
        for q4 in range(4):
            c0_, c1_ = q4 * (NKC * D // 4), (q4 + 1) * (NKC * D // 4)
            S.dma("pool", wdown_b[0][:, c0_:c1_], wdown[0][:, c0_:c1_], writes=[castA], key="castA")
        for cb in range(6):
            S.dma("pool", win_b[cb], win_d[cb], writes=[castB], key="castB")
        for kvi in range(2):
            S.dma("pool", w1_b[kvi], w1_d[kvi], writes=[castB], key="castB")
        for q4 in range(4):
            S.dma("pool", wo_b[:, q4 * 2048:(q4 + 1) * 2048], wo_d[:, q4 * 2048:(q4 + 1) * 2048], writes=[castB], key="castB")
        for pb in range(NPB):
            S.dma("pool", wup_b[1, pb], wup[1, pb], writes=[castB], key="castB")
        for q4 in range(4):
            c0_, c1_ = q4 * (NKC * D // 4), (q4 + 1) * (NKC * D // 4)
            S.dma("pool", wdown_b[1][:, c0_:c1_], wdown[1][:, c0_:c1_], writes=[castB], key="castB")

        S.dma("pool", posT.t[:, :, :], posT_d.rearrange("p (a b) -> p a b", a=2), writes=[posT])
        S.dma("sp", b1T[:, :], b1T_d, writes=[b1T])
        S.dma("pool", w2.t[:, :, :], w2_d.rearrange("p (a b) -> p a b", a=2), writes=[w2])
        S.dma("sp", b2T[:, :], b2T_d, writes=[b2T])
        S.dma("pool", tri.t[:, :, :], tri_d.rearrange("p (a b) -> p a b", a=2), writes=[tri])
        S.dma("sp", pat[:, :], pat_d, writes=[pat])
        S.op("dve", lambda: nc.vector.memset(KE.t[0:64, :, :], 0.0), writes=[KE])
        for g in range(4):
            S.dma("pool", KE.t[64:128, g, :], emat_d, writes=[KE])
        S.op("dve", lambda: nc.vector.memset(VS.t[:, :, :, :], 1.0), writes=[VS])
        S.op("dve", lambda: nc.vector.memset(VW.t[:, :, :, :], 1.0), writes=[VW])
        S.op("dve", lambda: nc.vector.memset(KW.t[:, :, :], 0.0), writes=[KW])
        S.op("dve", lambda: nc.vector.memset(CKT.t[:, :, :], 0.0), writes=[CKT])
        S.op("dve", lambda: nc.vector.memset(CVT.t[:, :, :], 0.0), writes=[CVT])
        S.op("dve", lambda: nc.vector.memset(CV.t[:, :, :, :], 0.0), writes=[CV])
        S.op("dve", lambda: nc.vector.memset(cmpT.t[:, :, :], 0.0), writes=[cmpT])
        for kvi in range(2):
            S.dma("pool", wslots[kvi].t[0:64, 0:4096], w1_d[kvi], writes=[wslots[kvi]], key="sw_" + wslots[kvi].name)
        bank = mm[0]
        for kvi in range(2):
            w1v = wslots[kvi].t[0:64, 0:4096].rearrange("p (a b) -> p a b", a=32)
            for t in range(32):
                S.op("pe", lambda kvi=kvi, t=t, w1v=w1v: nc.tensor.matmul(
                    bank[:, kvi:kvi + 1], lhsT=w1v[:, t, :], rhs=posT.t[:, kvi, t:t + 1], start=(t == 0), stop=(t == 31)),
                    reads=[wslots[kvi], posT], writes=[bank])
        S.op("dve", lambda: nc.vector.tensor_tensor(out=hbias[:, :], in0=bank[:, 0:2], in1=b1T[:, :], op=ALU.add),
             reads=[bank, b1T], writes=[hbias])

        seq = []
        for i in range(ntile + 1):
            for pb in range(NPB):
                seq.append((wup_b[0, pb], 8 * 512, castA))
            for cb in range(6):
                seq.append((win_b[cb], 8 * 512, castB))
            for rep in range(1 if i < ntile else 4 * (NPG // 8)):
                for kvi in range(2):
                    seq.append((w1_b[kvi], 4096, castB))
            for pb in range(NPB):
                seq.append((wup_b[1, pb], 8 * 512, castB))
        ws = WStream(S, wslots, seq)

        def layer_norm(subs, which):
            sts = []
            for (xb_buf, xs_ap, P, col0) in subs:
                stt = rot("stat", stat)
                sts.append(stt)
                for c in range(2):
                    S.op("dve", lambda c=c: nc.vector.bn_stats(out=stt[0:P, 6 * c:6 * c + 6], in_=xs_ap[:, 512 * c:512 * c + 512]),
                         reads=[xb_buf], writes=[stt])
                S.op("dve", lambda: nc.vector.bn_aggr(out=stt[0:P, 12:14], in_=stt[0:P, 0:12]), reads=[stt], writes=[stt])
                S.op("dve", lambda: nc.vector.tensor_scalar(out=stt[0:P, 14:15], in0=stt[0:P, 13:14], scalar1=LN_EPS, scalar2=None, op0=ALU.add),
                     reads=[stt], writes=[stt])
                S.op("act", lambda: nc.scalar.sqrt(out=stt[0:P, 14:15], in_=stt[0:P, 14:15]), reads=[stt], writes=[stt])
                S.op("dve", lambda: nc.vector.reciprocal(out=stt[0:P, 15:16], in_=stt[0:P, 14:15]), reads=[stt], writes=[stt])
            sl = rot("ln", lnslot)
            S.dma("sp", sl[:, :], lnp[which][:, 0:D], writes=[sl])
            for (xb_buf, xs_ap, P, col0), stt in zip(subs, sts):
                S.op("dve", lambda: nc.vector.scalar_tensor_tensor(out=xs_ap, in0=xs_ap, scalar=stt[0:P, 12:13], in1=sl[0:P, 0:D],
                                                                    op0=ALU.subtract, op1=ALU.mult),
                     reads=[xb_buf, stt, sl], writes=[xb_buf])
            S.dma("sp", sl[:, :], lnp[which][:, D:2 * D], writes=[sl])
            for (xb_buf, xs_ap, P, col0), stt in zip(subs, sts):
                S.op("dve", lambda: nc.vector.scalar_tensor_tensor(out=xs_ap, in0=xs_ap, scalar=stt[0:P, 15:16], in1=sl[0:P, 0:D],
                                                                    op0=ALU.mult, op1=ALU.add),
                     reads=[xb_buf, stt, sl], writes=[xb_buf])

        def to_hT(xb_buf, xs_ap, P, col0):
            h = rot("hb", hb)
            S.op("act", lambda: nc.scalar.copy(out=h[0:P, :], in_=xs_ap), reads=[xb_buf], writes=[h])
            for c in range(8):
                S.op("pe", lambda c=c: nc.tensor.transpose(trp[:, c * 128:c * 128 + P], h[0:P, c * 128:(c + 1) * 128], ident[0:P, 0:P]),
                     reads=[h, ident], writes=[trp])
            S.op("dve", lambda: nc.vector.tensor_copy(
                out=hT.t[:, :, col0:col0 + P],
                in_=trp.t[:, :].rearrange("p (a b) -> p a b", a=8)[:, :, 0:P]), reads=[trp], writes=[hT])

        def ffn(l, subs, ncols, nseq, L, prompt, last):
            S.dma("sp", wd.t[:, :, :], wdown_b[l].rearrange("p (a b) -> p a b", a=NKC), reads=[castA if l == 0 else castB], writes=[wd], key="wd")
            for pb in range(NPB):
                wsl = ws.next()
                wv = wsl.t[:, :].rearrange("p (a b) -> p a b", a=8)
                sgs = [None, None]
                for fc in range(4):
                    fi = pb * 4 + fc
                    bank = rot("mm", mm)
                    for kc in range(8):
                        S.op("pe", lambda kc=kc, fc=fc, bank=bank: nc.tensor.matmul(
                            bank[:, 0:ncols], lhsT=wv[:, kc, fc * 128:(fc + 1) * 128], rhs=hT.t[:, kc, 0:ncols],
                            start=(kc == 0), stop=(kc == 7)), reads=[wsl, hT], writes=[bank])
                    if prompt:
                        c = rot("C", C)
                        S.op("act", lambda: nc.scalar.activation(out=c[:, 0:L], in_=bank[:, 0:L], func=AF.Identity,
                                                                 scale=cw.t[:, l, fi, 2:3], bias=cw.t[:, l, fi, 3:4]),
                             reads=[bank, cw], writes=[c])
                        S.op("dve", lambda: nc.vector.scalar_tensor_tensor(out=c[:, 1:L], in0=bank[:, 0:L - 1], scalar=cw.t[:, l, fi, 1:2],
                                                                            in1=c[:, 1:L], op0=ALU.mult, op1=ALU.add),
                             reads=[bank, cw, c], writes=[c])
                        S.op("dve", lambda: nc.vector.scalar_tensor_tensor(out=c[:, 2:L], in0=bank[:, 0:L - 2], scalar=cw.t[:, l, fi, 0:1],
                                                                            in1=c[:, 2:L], op0=ALU.mult, op1=ALU.add),
                             reads=[bank, cw, c], writes=[c])
                        S.op("pool", lambda: nc.gpsimd.tensor_tensor(out=c[:, 0:2], in0=c[:, 0:2], in1=hc.t[:, l, fi, :], op=ALU.add),
                             reads=[c, hc], writes=[c])
                        S.op("dve", lambda: nc.vector.tensor_copy(out=halo.t[:, l, fi, :], in_=bank[:, L - 2:L]), reads=[bank], writes=[halo])
                        if fc < 2:
                            sg = SG[fc]
                            S.op("act", lambda sg=sg: nc.scalar.activation(out=sg[:, 0:ncols], in_=c[:, 0:ncols], func=AF.Silu),
                                 reads=[c], writes=[sg])
                            sgs[fc] = sg
                        else:
                            sg = sgs[fc - 2]
                            S.op("pool", lambda sg=sg: nc.gpsimd.tensor_tensor(out=hF.t[:, 2 * pb + fc - 2, 0:ncols], in0=sg[:, 0:ncols],
                                                                                in1=c[:, 0:ncols], op=ALU.mult),
                                 reads=[sg, c], writes=[hF])
                        continue
                    u = rot("U", U)
                    uv = u.t[:, 0:nseq * (L + 2)].rearrange("p (a b) -> p a b", a=nseq)
                    S.op("act", lambda: nc.scalar.copy(out=uv[:, :, 2:2 + L],
                                                       in_=bank.t[:, 0:ncols].rearrange("p (a b) -> p a b", a=nseq)),
                         reads=[bank], writes=[u])
                    if prompt:
                        S.op("pool", lambda: nc.gpsimd.tensor_copy(out=uv[:, 0, 0:2], in_=halo.t[:, l, fi, :]), reads=[halo], writes=[u])
                        S.op("pool", lambda: nc.gpsimd.tensor_copy(out=halo.t[:, l, fi, :], in_=uv[:, 0, L:L + 2]), reads=[u], writes=[halo])
                    else:
                        S.op("pool", lambda: nc.gpsimd.tensor_copy(
                            out=uv[:, :, 0:2], in_=shalo.t[:, l, fi, :].rearrange("p (a b) -> p a b", a=4)), reads=[shalo], writes=[u])
                        S.op("pool", lambda: nc.gpsimd.tensor_copy(
                            out=sout.t[:, l, fi, :].rearrange("p (a b) -> p a b", a=4), in_=uv[:, :, L:L + 2]), reads=[u], writes=[sout])
                    c = rot("C", C)
                    cv = c.t[:, 0:ncols].rearrange("p (a b) -> p a b", a=nseq)
                    S.op("dve", lambda: nc.vector.tensor_scalar(out=cv, in0=uv[:, :, 0:L], scalar1=cw.t[:, l, fi, 0:1],
                                                                scalar2=cw.t[:, l, fi, 3:4], op0=ALU.mult, op1=ALU.add),
                         reads=[u, cw], writes=[c])
                    S.op("dve", lambda: nc.vector.scalar_tensor_tensor(out=cv, in0=uv[:, :, 1:1 + L], scalar=cw.t[:, l, fi, 1:2],
                                                                        in1=cv, op0=ALU.mult, op1=ALU.add),
                         reads=[u, cw, c], writes=[c])
                    S.op("dve", lambda: nc.vector.scalar_tensor_tensor(out=cv, in0=uv[:, :, 2:2 + L], scalar=cw.t[:, l, fi, 2:3],
                                                                        in1=cv, op0=ALU.mult, op1=ALU.add),
                         reads=[u, cw, c], writes=[c])
                    if fc < 2:
                        sg = SG[fc]
                        S.op("act", lambda sg=sg: nc.scalar.activation(out=sg[:, 0:ncols], in_=c[:, 0:ncols], func=AF.Silu),
                             reads=[c], writes=[sg])
                        sgs[fc] = sg
                    else:
                        sg = sgs[fc - 2]
                        S.op("pool", lambda sg=sg: nc.gpsimd.tensor_tensor(out=hF.t[:, 2 * pb + fc - 2, 0:ncols], in0=sg[:, 0:ncols],
                                                                            in1=c[:, 0:ncols], op=ALU.mult),
                             reads=[sg, c], writes=[hF])
            if prompt:
                S.op("pool", lambda: nc.gpsimd.tensor_tensor(out=hc2[:, :], in0=halo.t[:, l, :, 1], in1=cw.t[:, l, :, 1], op=ALU.mult),
                     reads=[halo, cw], writes=[hc2])
                S.op("pool", lambda: nc.gpsimd.tensor_tensor(out=hc.t[:, l, :, 0], in0=halo.t[:, l, :, 0], in1=cw.t[:, l, :, 0], op=ALU.mult),
                     reads=[halo, cw], writes=[hc])
                S.op("pool", lambda: nc.gpsimd.tensor_tensor(out=hc.t[:, l, :, 0], in0=hc.t[:, l, :, 0], in1=hc2[:, :], op=ALU.add),
                     reads=[hc, hc2], writes=[hc])
                S.op("pool", lambda: nc.gpsimd.tensor_tensor(out=hc.t[:, l, :, 1], in0=halo.t[:, l, :, 1], in1=cw.t[:, l, :, 0], op=ALU.mult),
                     reads=[halo, cw, hc], writes=[hc])
            if prompt and last:
                S.dma("sp", ffn_p[l].rearrange("p (a b) -> p a b", a=NFC), halo.t[:, l, :, :], reads=[halo], key="o_ffn")
            if not prompt:
                S.dma("sp", ffn_s[l].rearrange("p (a b) -> p a b", a=NFC), sout.t[:, l, :, :], reads=[sout], key="o_ffn")
            for (xb_buf, xs_ap, P, col0) in subs:
                a = rot("acc", acc)
                for half in range(2):
                    for kc in range(NKC):
                        S.op("pe", lambda kc=kc, half=half: nc.tensor.matmul(
                            a[half][0:P, :], lhsT=hF.t[:, kc, col0:col0 + P], rhs=wd.t[:, kc, half * 512:(half + 1) * 512],
                            start=(kc == 0), stop=(kc == NKC - 1)), reads=[hF, wd], writes=[a[half]])
                    S.op("dve", lambda half=half: nc.vector.scalar_tensor_tensor(
                        out=xs_ap[:, half * 512:(half + 1) * 512], in0=xs_ap[:, half * 512:(half + 1) * 512], scalar=ALPHA,
                        in1=a[half][0:P, :], op0=ALU.mult, op1=ALU.add), reads=[xb_buf, a[half]], writes=[xb_buf])
            layer_norm(subs, l * 2 + 1)

        def mixer_pool_finish(subs):
            scale_bc = rot("ln", lnslot)
            S.dma("sp", scale_bc[:, :], pscale, writes=[scale_bc])
            for (xb_buf, xs_ap, P, col0) in subs:
                a = rot("acc", acc)
                for g in range(4):
                    for j in range(2):
                        S.op("pe", lambda g=g, j=j: nc.tensor.matmul(
                            a[g // 2][0:P, (g % 2) * 256:(g % 2) * 256 + 256], lhsT=dT.t[:, 2 * g + j, col0:col0 + P],
                            rhs=poolW.t[:, 2 * g + j, :], start=(j == 0), stop=(j == 1)), reads=[dT, poolW], writes=[a[g // 2]])
                tm = rot("tmp", tmp)
                for half in range(2):
                    S.op("dve", lambda half=half: nc.vector.tensor_tensor(
                        out=tm[0:P, half * 512:(half + 1) * 512], in0=a[half][0:P, :],
                        in1=scale_bc[0:P, half * 512:(half + 1) * 512], op=ALU.mult), reads=[a[half], scale_bc], writes=[tm])
                S.op("dve", lambda: nc.vector.scalar_tensor_tensor(out=xs_ap, in0=xs_ap, scalar=ALPHA, in1=tm[0:P, :],
                                                                    op0=ALU.mult, op1=ALU.add), reads=[xb_buf, tm], writes=[xb_buf])
            layer_norm(subs, 0)
            for (xb_buf, xs_ap, P, col0) in subs:
                to_hT(xb_buf, xs_ap, P, col0)

        def rope_inplace(buf, v, P, nh, cs):
            cosb = cs[:, 0:8].unsqueeze(1).to_broadcast([P, nh, 8])
            sinb = cs[:, 8:16].unsqueeze(1).to_broadcast([P, nh, 8])
            x1, x2 = v[:, :, 0:8], v[:, :, 8:16]
            r = [Rs.t[0:P, k, 0:nh, :] for k in range(4)]
            S.op("dve", lambda: nc.vector.tensor_tensor(out=r[0], in0=x1, in1=cosb, op=ALU.mult), reads=[buf, ropeT], writes=[Rs])
            S.op("dve", lambda: nc.vector.tensor_tensor(out=r[1], in0=x2, in1=sinb, op=ALU.mult), reads=[buf, ropeT], writes=[Rs])
            S.op("dve", lambda: nc.vector.tensor_tensor(out=r[2], in0=x2, in1=cosb, op=ALU.mult), reads=[buf, ropeT], writes=[Rs])
            S.op("dve", lambda: nc.vector.tensor_tensor(out=r[3], in0=x1, in1=sinb, op=ALU.mult), reads=[buf, ropeT], writes=[Rs])
            S.op("dve", lambda: nc.vector.tensor_tensor(out=x1, in0=r[0], in1=r[1], op=ALU.subtract), reads=[Rs], writes=[buf])
            S.op("dve", lambda: nc.vector.tensor_tensor(out=x2, in0=r[2], in1=r[3], op=ALU.add), reads=[Rs], writes=[buf])

        def nsa_project(i, subs, sample=False):
            t0 = i * TT
            qs0 = t0 // 128
            if sample:
                S.dma("sp", ropeT.t[0:32, 0, :], rope_s_d, writes=[ropeT])
            else:
                S.dma("sp", ropeT.t[:, :, :], rope_d[t0:t0 + TT, :].rearrange("(s p) c -> p s c", p=128), writes=[ropeT])
            if i > 0 and not sample:
                S.op("pool", lambda: nc.gpsimd.tensor_copy(out=cmpT.t[:, :, 0:16], in_=cmpT.t[:, :, TT:TT + 16]), reads=[cmpT], writes=[cmpT])
            for cb in range(6):
                wsl = ws.next()
                wv = wsl.t[:, :].rearrange("p (a b) -> p a b", a=8)
                ncb = 512 if cb < 5 else 48
                for (xb_buf, xs_ap, P, col0) in subs:
                    sidx = col0 // 128
                    qb = qbs[sidx]
                    key0 = t0 + col0
                    cs = ropeT.t[0:P, sidx, :]
                    bank = rot("mm", mm)
                    for kc in range(8):
                        S.op("pe", lambda kc=kc, bank=bank: nc.tensor.matmul(
                            bank[0:P, 0:ncb], lhsT=hT.t[:, kc, col0:col0 + P], rhs=wv[:, kc, 0:ncb],
                            start=(kc == 0), stop=(kc == 7)), reads=[hT, wsl], writes=[bank])
                    if cb < 2:
                        sg_ = rot("stg", stg)
                        S.op("act", lambda: nc.scalar.mul(out=sg_[0:P, :], in_=bank[0:P, :], mul=0.125), reads=[bank], writes=[sg_])
                        rope_inplace(sg_, sg_.t[0:P, :].rearrange("p (h d) -> p h d", h=8), P, 8, cs)
                        S.op("act", lambda: nc.scalar.copy(out=qb.t[0:P, cb * 8:cb * 8 + 8, :],
                                                           in_=sg_.t[0:P, :].rearrange("p (h d) -> p h d", h=8)), reads=[sg_], writes=[qb])
                        if cb == 1:
                            for r8 in range(2):
                                for hh in range(8):
                                    S.op("pe", lambda hh=hh: nc.tensor.transpose(trp[0:64, hh * 128:hh * 128 + P], qb.t[0:P, r8 * 8 + hh, :], ident[0:P, 0:P]),
                                         reads=[qb, ident], writes=[trp])
                                S.op("dve", lambda: nc.vector.tensor_copy(
                                    out=QN.t[0:64, r8 * 8:r8 * 8 + 8, col0:col0 + P],
                                    in_=trp.t[0:64, :].rearrange("p (a b) -> p a b", a=8)[:, :, 0:P]), reads=[trp], writes=[QN])
                    elif cb < 5:
                        br = cb - 2
                        sg_ = rot("stg", stg)
                        S.op("act", lambda: nc.scalar.copy(out=sg_[0:P, :], in_=bank[0:P, :]), reads=[bank], writes=[sg_])
                        rope_inplace(sg_, sg_.t[0:P, 0:256].rearrange("p (h d) -> p h d", h=4), P, 4, cs)
                        dst = [cmpkv_p, slckv_p][br] if br < 2 else None
                        if sample:
                            if br < 2:
                                S.dma("sp", [cmpkv_s, slckv_s][br], sg_[0:32, :], reads=[sg_], key="o_" + sg_.name)
                            else:
                                for q_ in range(4):
                                    S.dma("sp", winkv_s[q_, 504:512, :], sg_[8 * q_:8 * q_ + 8, :], reads=[sg_], key="o_" + sg_.name)
                        elif br < 2:
                            S.dma("sp", dst[key0:key0 + P, :], sg_[0:P, :], reads=[sg_], key="o_" + sg_.name)
                        else:
                            lo = max(0, T - 512)
                            if key0 >= lo:
                                S.dma("sp", winkv_p[key0 - lo:key0 - lo + P, :], sg_[0:P, :], reads=[sg_], key="o_" + sg_.name)
                        kb = rot("kvb", kvb)
                        S.op("act", lambda: nc.scalar.copy(out=kb[0:P, :], in_=sg_[0:P, :]), reads=[sg_], writes=[kb])
                        kt = key0 // 128
                        if sample:
                            if br > 0:
                                for j in range(4):
                                    S.op("pe", lambda j=j: nc.tensor.transpose(trp[0:64, j * 128:j * 128 + P], kb[0:P, j * 64:(j + 1) * 64], ident[0:P, 0:P]),
                                         reads=[kb, ident], writes=[trp])
                                tv = trp.t[0:64, 0:512].rearrange("p (a b) -> p a b", a=4)[:, :, 0:P]
                                vv = kb.t[0:P, 256:512].rearrange("p (g d) -> p g d", g=4)
                                S.op("dve", lambda: nc.vector.tensor_copy(out=tailK[:, br - 1, :, :], in_=tv), reads=[trp], writes=[CKT])
                                S.op("pool", lambda: nc.gpsimd.tensor_copy(out=VW.t[0:P, 6 - br, :, 0:64], in_=vv), reads=[kb], writes=[VW])
                        elif br == 0:
                            for j in range(8):
                                S.op("pe", lambda j=j: nc.tensor.transpose(trp[0:64, j * 128:j * 128 + P], kb[0:P, j * 64:(j + 1) * 64], ident[0:P, 0:P]),
                                     reads=[kb, ident], writes=[trp])
                            S.op("dve", lambda: nc.vector.tensor_copy(
                                out=cmpT.t[0:64, :, 16 + col0:16 + col0 + P],
                                in_=trp.t[0:64, :].rearrange("p (a b) -> p a b", a=8)[:, :, 0:P]), reads=[trp], writes=[cmpT])
                        else:
                            for j in range(4):
                                S.op("pe", lambda j=j: nc.tensor.transpose(trp[0:64, j * 128:j * 128 + P], kb[0:P, j * 64:(j + 1) * 64], ident[0:P, 0:P]),
                                     reads=[kb, ident], writes=[trp])
                            tv = trp.t[0:64, 0:512].rearrange("p (a b) -> p a b", a=4)[:, :, 0:P]
                            vv = kb.t[0:P, 256:512].rearrange("p (g d) -> p g d", g=4)
                            if br == 1:
                                S.op("dve", lambda: nc.vector.tensor_copy(out=KE.t[0:64, :, key0:key0 + P], in_=tv), reads=[trp], writes=[KE])
                                S.op("pool", lambda: nc.gpsimd.tensor_copy(out=VS.t[0:P, kt, :, 0:64], in_=vv), reads=[kb], writes=[VS])
                            else:
                                kr = key0 % 768
                                S.op("dve", lambda: nc.vector.tensor_copy(out=KW.t[0:64, :, kr:kr + P], in_=tv), reads=[trp], writes=[KW])
                                S.op("pool", lambda: nc.gpsimd.tensor_copy(out=VW.t[0:P, kt % 6, :, 0:64], in_=vv), reads=[kb], writes=[VW])
                    else:
                        S.op("act", lambda: nc.scalar.activation(out=G.t[0:P, sidx, :], in_=bank[0:P, 0:48], func=AF.Sigmoid),
                             reads=[bank], writes=[G])

        def compress_new(n_lo, n_new, base, cmpT=cmpT, cT=None, CKT=CKT, CKTv=None, CVT=CVT, CVTv=None, CV=CV, CVv=None,
                         ug=ug, ugv=None, ug2=ug2, ug2v=None, gl=gl, glv=None):
            cT = cmpT.t if cT is None else cT
            CKTv = CKT.t if CKTv is None else CKTv
            CVTv = CVT.t if CVTv is None else CVTv
            CVv = CV.t if CVv is None else CVv
            ugv = ug.t if ugv is None else ugv
            ug2v = ug2.t if ug2v is None else ug2v
            glv = gl.t if glv is None else glv
            bank = rot("mm", mm)
            for kvi in range(2):
                wsl = ws.next()
                w1v = wsl.t[0:64, 0:4096].rearrange("p (a b) -> p a b", a=32)
                for g in range(4):
                    j = kvi * 4 + g
                    cview = cT[0:64, j, :].rearrange("p (c b) -> p c b", b=16)
                    for t in range(32):
                        q_, r_ = divmod(base + t, 16)
                        S.op("pe", lambda t=t, j=j, q_=q_, r_=r_, w1v=w1v, cview=cview: nc.tensor.matmul(
                            bank[:, j * n_new:(j + 1) * n_new], lhsT=w1v[:, t, :], rhs=cview[:, q_:q_ + n_new, r_],
                            start=(t == 0), stop=(t == 31)), reads=[wsl, cmpT], writes=[bank])
            nn = 8 * n_new
            for kvi in range(2):
                S.op("act", lambda kvi=kvi: nc.scalar.activation(out=ugv[:, kvi * 4 * n_new:(kvi + 1) * 4 * n_new],
                                                              in_=bank[:, kvi * 4 * n_new:(kvi + 1) * 4 * n_new],
                                                              func=AF.Identity, bias=hbias[:, kvi:kvi + 1], scale=1.0),
                     reads=[bank, hbias], writes=[ug])
            S.op("dve", lambda: nc.vector.tensor_tensor(out=ug2v[:, 0:nn], in0=ugv[:, 0:nn], in1=ugv[:, 0:nn], op=ALU.mult), reads=[ug], writes=[ug2])
            S.op("dve", lambda: nc.vector.tensor_scalar(out=ug2v[:, 0:nn], in0=ug2v[:, 0:nn], scalar1=0.044715, scalar2=1.0, op0=ALU.mult, op1=ALU.add),
                 reads=[ug2], writes=[ug2])
            S.op("dve", lambda: nc.vector.tensor_tensor(out=ug2v[:, 0:nn], in0=ug2v[:, 0:nn], in1=ugv[:, 0:nn], op=ALU.mult), reads=[ug, ug2], writes=[ug2])
            S.op("act", lambda: nc.scalar.activation(out=ug2v[:, 0:nn], in_=ug2v[:, 0:nn], func=AF.Sigmoid, scale=1.5957691216057308),
                 reads=[ug2], writes=[ug2])
            S.op("dve", lambda: nc.vector.tensor_tensor(out=glv[:, 0:nn], in0=ugv[:, 0:nn], in1=ug2v[:, 0:nn], op=ALU.mult), reads=[ug, ug2], writes=[gl])
            for kvi in range(2):
                bank2 = rot("mm", mm)
                S.op("pe", lambda kvi=kvi, bank2=bank2: nc.tensor.matmul(
                    bank2[0:64, 0:4 * n_new], lhsT=w2.t[:, kvi, :], rhs=glv[:, kvi * 4 * n_new:(kvi + 1) * 4 * n_new],
                    start=True, stop=True), reads=[w2, gl], writes=[bank2])
                dstT = CKT if kvi == 0 else CVT
                dstv = CKTv if kvi == 0 else CVTv
                S.op("act", lambda kvi=kvi, bank2=bank2, dstv=dstv: nc.scalar.activation(
                    out=dstv[0:64, :, n_lo:n_lo + n_new], in_=bank2.t[0:64, 0:4 * n_new].rearrange("p (g n) -> p g n", g=4),
                    func=AF.Identity, bias=b2T[:, kvi:kvi + 1], scale=1.0), reads=[bank2, b2T], writes=[dstT])
            for bk in range(n_lo // 128, (n_lo + n_new - 1) // 128 + 1):
                for g in range(4):
                    S.op("pe", lambda g=g, bk=bk: nc.tensor.transpose(trp[:, g * 64:(g + 1) * 64], CVTv[0:64, g, bk * 128:(bk + 1) * 128], ident[0:64, 0:64]),
                         reads=[CVT, ident], writes=[trp])
                S.op("dve", lambda bk=bk: nc.vector.tensor_copy(out=CVv[:, bk, :, :], in_=trp.t[:, 0:256].rearrange("p (g d) -> p g d", g=4)),
                     reads=[trp], writes=[CV])

        def cmp_and_select(qs, s, P, col0, g, Nc):
            abank = acc[0]
            for hi in range(4):
                S.op("pe", lambda hi=hi: nc.tensor.matmul(
                    abank[hi // 2][0:P, (hi % 2) * 256:(hi % 2) * 256 + Nc], lhsT=QN.t[0:64, 4 * g + hi, col0:col0 + P],
                    rhs=CKT.t[0:64, g, 0:Nc], start=True, stop=True), reads=[QN, CKT], writes=[abank[hi // 2]])
            for hp in range(2):
                S.op("act", lambda hp=hp: nc.scalar.activation(
                    out=E4.t[0:P, 2 * hp:2 * hp + 2, 0:Nc], in_=abank[hp].t[0:P, :].rearrange("p (a b) -> p a b", a=2)[:, :, 0:Nc],
                    func=AF.Exp), reads=[abank[hp]], writes=[E4])
            c0 = max(0, 8 * qs - 1)
            c1 = min(8 * qs + 7, Nc)
            p0 = c0 - (8 * qs - 1)
            S.op("dve", lambda: nc.vector.tensor_tensor(
                out=E4.t[0:P, :, c0:c1], in0=E4.t[0:P, :, c0:c1],
                in1=pat[0:P, p0:p0 + (c1 - c0)].unsqueeze(1).to_broadcast([P, 4, c1 - c0]), op=ALU.mult), reads=[E4, pat], writes=[E4])
            sm = rot("small", small)
            S.op("dve", lambda: nc.vector.tensor_reduce(out=sm[0:P, 0:4], in_=E4.t[0:P, :, 0:Nc], axis=mybir.AxisListType.X, op=ALU.add),
                 reads=[E4], writes=[sm])
            S.op("dve", lambda: nc.vector.tensor_scalar(out=sm[0:P, 0:4], in0=sm[0:P, 0:4], scalar1=1e-30, scalar2=None, op0=ALU.max),
                 reads=[sm], writes=[sm])
            S.op("dve", lambda: nc.vector.reciprocal(out=sm[0:P, 4:8], in_=sm[0:P, 0:4]), reads=[sm], writes=[sm])
            S.op("dve", lambda: nc.vector.tensor_tensor(
                out=E4.t[0:P, :, 0:Nc], in0=E4.t[0:P, :, 0:Nc], in1=sm[0:P, 4:8].unsqueeze(2).to_broadcast([P, 4, Nc]), op=ALU.mult),
                reads=[E4, sm], writes=[E4])
            S.op("dve", lambda: nc.vector.memset(imp[:, :], 0.0), writes=[imp])
            S.op("dve", lambda: nc.vector.tensor_reduce(out=imp[0:P, 1:1 + Nc], in_=E4.t[0:P, :, 0:Nc].rearrange("p h n -> p n h"),
                                                        axis=mybir.AxisListType.X, op=ALU.add), reads=[E4], writes=[imp])
            iv = imp.t[0:P, 0:260].rearrange("p (j f) -> p j f", f=4)
            S.op("dve", lambda: nc.vector.tensor_tensor(out=blk[0:P, :], in0=iv[:, 0:64, 0], in1=iv[:, 0:64, 1], op=ALU.add), reads=[imp], writes=[blk])
            S.op("dve", lambda: nc.vector.tensor_tensor(out=blk[0:P, :], in0=blk[0:P, :], in1=iv[:, 0:64, 2], op=ALU.add), reads=[imp, blk], writes=[blk])
            S.op("dve", lambda: nc.vector.tensor_tensor(out=blk[0:P, :], in0=blk[0:P, :], in1=iv[:, 0:64, 3], op=ALU.add), reads=[imp, blk], writes=[blk])
            S.op("dve", lambda: nc.vector.tensor_tensor(out=blk[0:P, :], in0=blk[0:P, :], in1=iv[:, 1:65, 0], op=ALU.add), reads=[imp, blk], writes=[blk])
            sc = scm[s % 2]
            S.op("dve", lambda: nc.vector.tensor_tensor(out=score[0:P, :], in0=blk[0:P, :], in1=sc[0:P, 0, :], op=ALU.mult), reads=[blk, sc], writes=[score])
            S.op("dve", lambda: nc.vector.tensor_tensor(out=score[0:P, :], in0=score[0:P, :], in1=sc[0:P, 1, :], op=ALU.add), reads=[score, sc], writes=[score])
            S.op("dve", lambda: nc.vector.max(out=m8[0:P, 0:8], in_=score[0:P, :]), reads=[score], writes=[m8])
            S.op("dve", lambda: nc.vector.match_replace(out=wk[0:P, :], in_to_replace=m8[0:P, 0:8], in_values=score[0:P, :], imm_value=-1e30),
                 reads=[score, m8], writes=[wk])
            S.op("dve", lambda: nc.vector.max(out=m8[0:P, 8:16], in_=wk[0:P, :]), reads=[wk], writes=[m8])
            S.op("dve", lambda: nc.vector.tensor_scalar(out=wk[0:P, :], in0=score[0:P, :], scalar1=0.0, scalar2=None, op0=ALU.is_ge), reads=[score], writes=[wk])
            S.op("dve", lambda: nc.vector.scalar_tensor_tensor(out=wk[0:P, :], in0=score[0:P, :], scalar=m8[0:P, 15:16], in1=wk[0:P, :],
                                                                op0=ALU.is_ge, op1=ALU.mult), reads=[score, m8, wk], writes=[wk])
            S.op("dve", lambda: nc.vector.tensor_scalar(out=negb[0:P, :], in0=wk[0:P, :], scalar1=-1.0, scalar2=30000.0, op0=ALU.add, op1=ALU.mult),
                 reads=[wk], writes=[negb])
            S.op("pe", lambda: nc.tensor.transpose(trp[0:64, 0:P], negb[0:P, :], ident[0:P, 0:P]), reads=[negb, ident], writes=[trp])
            S.op("dve", lambda: nc.vector.tensor_copy(out=QN.t[64:128, 4 * g:4 * g + 4, col0:col0 + P],
                                                      in_=trp.t[0:64, 0:P].unsqueeze(1).to_broadcast([64, 4, P])), reads=[trp], writes=[QN])
            S.op("act", lambda: nc.scalar.copy(out=Pb.t[0:P, :, 0:Nc], in_=E4.t[0:P, :, 0:Nc]), reads=[E4], writes=[Pb])
            nnt = (Nc + 127) // 128
            for hi in range(4):
                for nt in range(nnt):
                    w = min(128, Nc - 128 * nt)
                    S.op("pe", lambda hi=hi, nt=nt, w=w: nc.tensor.transpose(
                        trp[0:w, (hi * 2 + nt) * 128:(hi * 2 + nt) * 128 + P], Pb.t[0:P, hi, nt * 128:nt * 128 + w], ident[0:P, 0:P]),
                        reads=[Pb, ident], writes=[trp])
            for nt in range(nnt):
                w = min(128, Nc - 128 * nt)
                S.op("dve", lambda nt=nt, w=w: nc.vector.tensor_copy(
                    out=PTc.t[0:w, :, nt, 0:P], in_=trp.t[0:w, :].rearrange("p (a b c) -> p a b c", a=4, b=2)[:, :, nt, 0:P]),
                    reads=[trp], writes=[PTc])
            ob_ = acc[1][0]
            for hi in range(4):
                for nt in range(nnt):
                    w = min(128, Nc - 128 * nt)
                    S.op("pe", lambda hi=hi, nt=nt, w=w: nc.tensor.matmul(
                        ob_[0:P, hi * 64:(hi + 1) * 64], lhsT=PTc.t[0:w, hi, nt, 0:P], rhs=CV.t[0:w, nt, g, :],
                        start=(nt == 0), stop=(nt == nnt - 1)), reads=[PTc, CV], writes=[ob_])
            gv = G.t[0:P, s, :].rearrange("p (h b) -> p h b", b=3)
            S.op("dve", lambda: nc.vector.tensor_tensor(
                out=o_tok.t[0:P, s, 256 * g:256 * g + 256].rearrange("p (h d) -> p h d", h=4),
                in0=ob_.t[0:P, 0:256].rearrange("p (h d) -> p h d", h=4),
                in1=gv[:, 4 * g:4 * g + 4, 0].unsqueeze(2).to_broadcast([P, 4, 64]), op=ALU.mult), reads=[ob_, G], writes=[o_tok])

        def attend(branch, qs0, nsq, g, hi, Pq):
            h = 4 * g + hi
            ob_ = acc[branch - 1][hi % 2]
            kt_lo = 0 if branch == 1 else max(0, qs0 - 4)
            kt_hi = qs0 + nsq - 1
            kts = list(range(kt_lo, kt_hi + 1))

            def job(kt):
                sA = max(0, kt - qs0)
                sB = nsq - 1 if branch == 1 else min(nsq - 1, kt + 4 - qs0)
                return (kt, sA, sB, (sB - sA + 1) * 128)
            jobs = [job(kt) for kt in kts]
            units = []
            i_ = 0
            while i_ < len(jobs):
                if jobs[i_][3] == TT and i_ + 1 < len(jobs) and 2 * TT <= 512:
                    units.append([jobs[i_], jobs[i_ + 1]])
                    i_ += 2
                else:
                    units.append([jobs[i_]])
                    i_ += 1

            def emit_scores(unit):
                bank = rot("mm", mm)
                for j_, (kt, sA, sB, N) in enumerate(unit):
                    qc0 = sA * 128
                    o_ = TT * j_
                    if branch == 1:
                        S.op("pe", lambda: nc.tensor.matmul(bank[:, o_:o_ + N], lhsT=KE.t[:, g, kt * 128:(kt + 1) * 128], rhs=QN.t[:, h, qc0:qc0 + N],
                                                            start=True, stop=True), reads=[KE, QN], writes=[bank])
                    else:
                        kr = (kt * 128) % 768
                        S.op("pe", lambda: nc.tensor.matmul(bank[:, o_:o_ + N], lhsT=KW.t[0:64, g, kr:kr + 128], rhs=QN.t[0:64, h, qc0:qc0 + N],
                                                            start=True, stop=True), reads=[KW, QN], writes=[bank])
                return bank

            def emit_rest(unit, bank):
                pt = rot("PT", PT)
                ntot = TT * (len(unit) - 1) + unit[-1][3]
                S.op("act", lambda: nc.scalar.activation(out=pt[:, 0:ntot], in_=bank[:, 0:ntot], func=AF.Exp), reads=[bank], writes=[pt])
                for j_, (kt, sA, sB, N) in enumerate(unit):
                    o_ = TT * j_
                    for sq in range(sA, sB + 1):
                        delta = qs0 + sq - kt
                        c = o_ + (sq - sA) * 128
                        if delta == 0:
                            S.op("pool", lambda c=c: nc.gpsimd.tensor_tensor(out=pt[:, c:c + 128], in0=pt[:, c:c + 128], in1=tri.t[:, 0, :], op=ALU.mult),
                                 reads=[pt, tri], writes=[pt])
                        elif branch == 2 and delta == 4:
                            S.op("pool", lambda c=c: nc.gpsimd.tensor_tensor(out=pt[:, c:c + 128], in0=pt[:, c:c + 128], in1=tri.t[:, 1, :], op=ALU.mult),
                                 reads=[pt, tri], writes=[pt])
                for j_, (kt, sA, sB, N) in enumerate(unit):
                    o_ = TT * j_
                    for sq in range(sA, sB + 1):
                        c = o_ + (sq - sA) * 128
                        first = (kt == kt_lo and sq == sA)
                        last = (kt == qs0 + sq)
                        oc = sq * 65
                        vsrc = VS.t[:, kt, g, :] if branch == 1 else VW.t[:, kt % 6, g, :]
                        S.op("pe", lambda c=c, oc=oc, first=first, last=last, vsrc=vsrc: nc.tensor.matmul(
                            ob_[0:Pq, oc:oc + 65], lhsT=pt[:, c:c + Pq], rhs=vsrc, start=first, stop=last, skip_group_check=True),
                            reads=[pt, VS if branch == 1 else VW], writes=[ob_])

            pend = emit_scores(units[0])
            for ui, unit in enumerate(units):
                cur = pend
                if ui + 1 < len(units):
                    pend = emit_scores(units[ui + 1])
                emit_rest(unit, cur)
            for sq in range(nsq):
                oc = sq * 65
                sm = rot("small", small)
                S.op("dve", lambda: nc.vector.reciprocal(out=sm[0:Pq, 0:1], in_=ob_[0:Pq, oc + 64:oc + 65]), reads=[ob_], writes=[sm])
                S.op("dve", lambda: nc.vector.tensor_tensor(out=sm[0:Pq, 1:2], in0=sm[0:Pq, 0:1], in1=G.t[0:Pq, sq, 3 * h + branch:3 * h + branch + 1], op=ALU.mult),
                     reads=[sm, G], writes=[sm])
                S.op("dve", lambda: nc.vector.scalar_tensor_tensor(
                    out=o_tok.t[0:Pq, sq, 64 * h:64 * h + 64], in0=ob_[0:Pq, oc:oc + 64], scalar=sm[0:Pq, 1:2],
                    in1=o_tok.t[0:Pq, sq, 64 * h:64 * h + 64], op0=ALU.mult, op1=ALU.add), reads=[ob_, sm, o_tok], writes=[o_tok])

        def out_proj(subs):
            for (xb_buf, xs_ap, P, col0) in subs:
                sidx = col0 // 128
                h_ = rot("hb", hb)
                S.op("act", lambda: nc.scalar.copy(out=h_[0:P, :], in_=o_tok.t[0:P, sidx, :]), reads=[o_tok], writes=[h_])
                for c in range(8):
                    S.op("pe", lambda c=c: nc.tensor.transpose(trp[:, c * 128:c * 128 + P], h_[0:P, c * 128:(c + 1) * 128], ident[0:P, 0:P]),
                         reads=[h_, ident], writes=[trp])
                S.op("dve", lambda: nc.vector.tensor_copy(out=oT.t[:, :, 0:P], in_=trp.t[:, :].rearrange("p (a b) -> p a b", a=8)[:, :, 0:P]),
                     reads=[trp], writes=[oT])
                a = rot("acc", acc)
                for half in range(2):
                    for c in range(8):
                        S.op("pe", lambda c=c, half=half: nc.tensor.matmul(
                            a[half][0:P, :], lhsT=oT.t[:, c, 0:P], rhs=wd.t[:, c, half * 512:(half + 1) * 512],
                            start=(c == 0), stop=(c == 7)), reads=[oT, wd], writes=[a[half]])
                    S.op("dve", lambda half=half: nc.vector.scalar_tensor_tensor(
                        out=xs_ap[:, half * 512:(half + 1) * 512], in0=xs_ap[:, half * 512:(half + 1) * 512], scalar=ALPHA,
                        in1=a[half][0:P, :], op0=ALU.mult, op1=ALU.add), reads=[xb_buf, a[half]], writes=[xb_buf])
            layer_norm(subs, 2)
            for (xb_buf, xs_ap, P, col0) in subs:
                to_hT(xb_buf, xs_ap, P, col0)

        def nsa_prompt(i, subs):
            t0 = i * TT
            qs0 = t0 // 128
            S.dma("sp", wd.t[:, 0:8, :], wo_b.rearrange("p (a b) -> p a b", a=8), reads=[castB], writes=[wd], key="wd")
            nsa_project(i, subs)
            if i == 0:
                compress_new(0, TT // 16 - 1, 16)
            else:
                compress_new(t0 // 16 - 1, TT // 16, 0)
            for s in range(nsub):
                qs = qs0 + s
                sc = scm[s % 2]
                S.dma("sp", sc.t[:, :, :], scm_d[qs].rearrange("p (a b) -> p a b", a=2), writes=[sc])
                Nc = min(8 * qs + 7, 255)
                for g in range(4):
                    cmp_and_select(qs, s, 128, s * 128, g, Nc)
            for g in range(4):
                for hi in range(4):
                    attend(1, qs0, nsub, g, hi, 128)
                    attend(2, qs0, nsub, g, hi, 128)
            out_proj(subs)

        tailK = CKT.t[0:64, :, :].rearrange("p a b -> p (a b)")[:, 0:256].rearrange("p (a b c) -> p a b c", a=2, b=4)
        NCMP = 8 * NPG - 1
        NBLK = 2 * NPG + 1
        NRG = (NPG + 31) // 32
        hFflat = hF.t[:, :, :].rearrange("p a b -> p (a b)")
        CKTs = hFflat[0:64, 0:4096].rearrange("p (a b) -> p a b", a=4)
        Pb1 = hFflat[0:32, 4096:5120]
        CVTs_b = carve("CVTs", 36864, 8192, BF16, "p (a b) -> p a b", a=4)
        for ob_ in (E4, Pb, PTc):
            CVTs_b.aliases.append(ob_)
            ob_.aliases.append(CVTs_b)
        CVs = Bm.t[:, :, :].rearrange("p a b -> p (a b)").bitcast(BF16)[:, 0:2048].rearrange("p (a b c) -> p a b c", a=8, b=4)
        cmpTs = VS.t[:, :, :, :].rearrange("p a b c -> p (a b c)")[0:64, 0:8320].rearrange("p (a b) -> p a b", a=8)
        QNs = poolW.t[:, :, :].rearrange("p a b -> p (a b)").rearrange("p (r h c) -> p r h c", r=4, h=16)
        E1s = tmp[0]
        imps = xhalo
        blk_s, score_s, wk_s = U[0], U[1], U[2]
        negb_s = SG[0].t[:, :].bitcast(BF16)
        PTc1 = PT[0].t[:, 0:256].rearrange("p (a b) -> p a b", a=8)
        idx_i = C[0].t[:, 0:128].bitcast(I32)
        ptb_i = C[1].t[:, 0:128].bitcast(I32)
        ptf = C[2]
        Vp = [C[3].t[:, :].bitcast(BF16)[:, 0:260].rearrange("p (g d) -> p g d", g=4),
              SG[1].t[:, :].bitcast(BF16)[:, 0:260].rearrange("p (g d) -> p g d", g=4)]
        Vp_b = [C[3], SG[1]]
        ugs = qbs[1].t[:, :, :].rearrange("p a b -> p (a b)").bitcast(F32)
        ug2s = oT.t[:, :, :].rearrange("p a b -> p (a b)").bitcast(F32)
        gls = hb[0].t[:, 0:512]
        ptx = Buf("ptx", qbs[0].t[:, :, :].rearrange("p a b -> p (a b)"))
        ptx.aliases.append(qbs[0])
        qbs[0].aliases.append(ptx)
        PTs = [PT[1], ptx]
        bdm = S.sb("bdm", [32, 32], BF16)
        wm0 = S.sb("wm0", [128, 32], BF16)
        rowm = S.sb("rowm", [32, 4], F32)
        iota_f = S.sb("iota_f", [128, 1], F32)
        Gq = S.sb("Gq", [32, 48], F32)

        stgs = list(stg)
        if nsub > 1:
            for k_ in range(2):
                b_ = Buf(f"stgx{k_}", X[0][1].t[:, 512 * k_:512 * (k_ + 1)])
                b_.aliases.append(X[0][1])
                X[0][1].aliases.append(b_)
                stgs.append(b_)

        def gather_page(src_d, j):
            pf = rot("stg", stgs)
            S._deps("pool", [C[0]], [pf])
            ds_ = S.dsem(pf.name)
            ins = nc.gpsimd.indirect_dma_start(out=pf[:, :], out_offset=None, in_=src_d,
                                               in_offset=bass.IndirectOffsetOnAxis(ap=idx_i[:, j:j + 1], axis=0))
            ds_[1] += 16
            ins.then_inc(ds_[0], 16)
            S._commit((ds_[0], ds_[1], "dma"), "d_" + pf.name, [C[0]], [pf])
            S.n_ins += 1
            return pf

        def sample_seq(q):
            S.dma("sp", ptb_i[:, 0:NPG], ptab_d[q].partition_broadcast(128), writes=[C[1]])
            S.op("dve", lambda: nc.vector.tensor_copy(out=ptf[:, 0:NPG], in_=ptb_i[:, 0:NPG]), reads=[C[1]], writes=[ptf])
            S.op("dve", lambda: nc.vector.tensor_scalar(out=ptf[:, 0:NPG], in0=ptf[:, 0:NPG], scalar1=128.0, scalar2=iota_f[:, 0:1],
                                                        op0=ALU.mult, op1=ALU.add), reads=[ptf, iota_f], writes=[ptf])
            S.op("dve", lambda: nc.vector.tensor_copy(out=idx_i[:, 0:NPG], in_=ptf[:, 0:NPG]), reads=[ptf], writes=[C[0]])
            S.op("dve", lambda: nc.vector.tensor_scalar(out=Gq[:, :], in0=G.t[0:32, 0, :], scalar1=rowm[:, q:q + 1], scalar2=None, op0=ALU.mult),
                 reads=[G, rowm], writes=[Gq])
            nchunk = NPG // 8
            for c in range(nchunk):
                if c > 0:
                    S.op("pool", lambda: nc.gpsimd.tensor_copy(out=cmpTs[:, :, 0:16], in_=cmpTs[:, :, 1024:1040]), reads=[VS], writes=[VS])
                for pp in range(8):
                    pg_ = c * 8 + pp
                    if pg_ == 0:
                        cq = [gather_page(ccmp_d, k_) for k_ in range(min(3, NPG))]
                    if pg_ + 3 < NPG:
                        cq.append(gather_page(ccmp_d, pg_ + 3))
                    pf = cq.pop(0)
                    kb = rot("kvb", kvb)
                    S.op("act", lambda: nc.scalar.copy(out=kb[:, :], in_=pf[:, :]), reads=[pf], writes=[kb])
                    for j in range(8):
                        S.op("pe", lambda j=j: nc.tensor.transpose(trp[0:64, j * 128:(j + 1) * 128], kb[:, j * 64:(j + 1) * 64], ident[:, :]),
                             reads=[kb, ident], writes=[trp])
                    S.op("dve", lambda pp=pp: nc.vector.tensor_copy(
                        out=cmpTs[:, :, 16 + pp * 128:16 + (pp + 1) * 128],
                        in_=trp.t[0:64, :].rearrange("p (a b) -> p a b", a=8)), reads=[trp], writes=[VS])
                kw = dict(cmpT=VS, cT=cmpTs, CKT=hF, CKTv=CKTs, CVT=CVTs_b, CVTv=CVTs_b.t, CV=Bm, CVv=CVs,
                          ug=qbs[1], ugv=ugs, ug2=oT, ug2v=ug2s, gl=hb[0], glv=gls)
                if c == 0:
                    compress_new(0, 63, 16, **kw)
                else:
                    compress_new(64 * c - 1, 64, 0, **kw)
            for g in range(4):
                S.op("dve", lambda: nc.vector.memset(imps[0:32, :], 0.0), writes=[imps])
                for hi in range(4):
                    h = 4 * g + hi
                    pieces = [(a, min(512, NCMP - a)) for a in range(0, NCMP, 512)]
                    for pi, (a, wdt) in enumerate(pieces):
                        S.op("pe", lambda a=a, wdt=wdt, pi=pi: nc.tensor.matmul(
                            acc[1][pi][0:32, 0:wdt], lhsT=QN.t[0:64, h, 0:32], rhs=CKTs[0:64, g, a:a + wdt], start=True, stop=True),
                            reads=[QN, hF], writes=[acc[1][pi]])
                        S.op("act", lambda a=a, wdt=wdt, pi=pi: nc.scalar.activation(out=E1s[0:32, a:a + wdt], in_=acc[1][pi][0:32, 0:wdt], func=AF.Exp),
                             reads=[acc[1][pi]], writes=[E1s])
                    sm = rot("small", small)
                    S.op("dve", lambda: nc.vector.tensor_reduce(out=sm[0:32, 0:1], in_=E1s[0:32, 0:NCMP], axis=mybir.AxisListType.X, op=ALU.add),
                         reads=[E1s], writes=[sm])
                    S.op("dve", lambda: nc.vector.reciprocal(out=sm[0:32, 1:2], in_=sm[0:32, 0:1]), reads=[sm], writes=[sm])
                    S.op("dve", lambda: nc.vector.tensor_scalar(out=E1s[0:32, 0:NCMP], in0=E1s[0:32, 0:NCMP], scalar1=sm[0:32, 1:2], scalar2=None, op0=ALU.mult),
                         reads=[E1s, sm], writes=[E1s])
                    S.op("dve", lambda: nc.vector.tensor_tensor(out=imps[0:32, 1:1 + NCMP], in0=imps[0:32, 1:1 + NCMP], in1=E1s[0:32, 0:NCMP], op=ALU.add),
                         reads=[imps, E1s], writes=[imps])
                    S.op("act", lambda: nc.scalar.copy(out=Pb1[:, 0:NCMP], in_=E1s[0:32, 0:NCMP]), reads=[E1s], writes=[hF])
                    nnt = (NCMP + 127) // 128
                    for nt in range(nnt):
                        w = min(128, NCMP - 128 * nt)
                        S.op("pe", lambda nt=nt, w=w: nc.tensor.transpose(trp[0:w, nt * 32:(nt + 1) * 32], Pb1[:, nt * 128:nt * 128 + w], ident[0:32, 0:32]),
                             reads=[hF, ident], writes=[trp])
                    for nt in range(nnt):
                        w = min(128, NCMP - 128 * nt)
                        S.op("dve", lambda nt=nt, w=w: nc.vector.tensor_copy(out=PTc1[0:w, nt, :], in_=trp[0:w, nt * 32:(nt + 1) * 32]),
                             reads=[trp], writes=[PT[0]])
                    ob_ = acc[0][0]
                    for nt in range(nnt):
                        w = min(128, NCMP - 128 * nt)
                        S.op("pe", lambda nt=nt, w=w: nc.tensor.matmul(
                            ob_[0:32, hi * 64:(hi + 1) * 64], lhsT=PTc1[0:w, nt, :], rhs=CVs[0:w, nt, g, :],
                            start=(nt == 0), stop=(nt == nnt - 1)), reads=[PT[0], Bm], writes=[ob_])
                    S.op("dve", lambda: nc.vector.scalar_tensor_tensor(
                        out=o_tok.t[0:32, 0, 64 * h:64 * h + 64], in0=ob_[0:32, hi * 64:(hi + 1) * 64], scalar=Gq[:, 3 * h:3 * h + 1],
                        in1=o_tok.t[0:32, 0, 64 * h:64 * h + 64], op0=ALU.mult, op1=ALU.add), reads=[ob_, Gq, o_tok], writes=[o_tok])
                nb1 = NBLK - 1
                iv = imps.t[0:32, 0:4 * nb1].rearrange("p (j f) -> p j f", f=4)
                S.op("dve", lambda: nc.vector.memset(blk_s[0:32, 0:NBLK], 0.0), writes=[blk_s])
                S.op("dve", lambda: nc.vector.tensor_tensor(out=blk_s[0:32, 0:nb1], in0=iv[:, :, 0], in1=iv[:, :, 1], op=ALU.add), reads=[imps], writes=[blk_s])
                S.op("dve", lambda: nc.vector.tensor_tensor(out=blk_s[0:32, 0:nb1], in0=blk_s[0:32, 0:nb1], in1=iv[:, :, 2], op=ALU.add), reads=[imps, blk_s], writes=[blk_s])
                S.op("dve", lambda: nc.vector.tensor_tensor(out=blk_s[0:32, 0:nb1], in0=blk_s[0:32, 0:nb1], in1=iv[:, :, 3], op=ALU.add), reads=[imps, blk_s], writes=[blk_s])
                S.op("dve", lambda: nc.vector.tensor_tensor(out=blk_s[0:32, 0:nb1 - 1], in0=blk_s[0:32, 0:nb1 - 1], in1=iv[:, 1:nb1, 0], op=ALU.add),
                     reads=[imps, blk_s], writes=[blk_s])
                S.op("dve", lambda: nc.vector.memset(blk_s[0:32, 0:1], 1e9), writes=[blk_s])
                S.op("dve", lambda: nc.vector.memset(blk_s[0:32, NBLK - 2:NBLK], 1e9), writes=[blk_s])
                S.op("dve", lambda: nc.vector.max(out=m8[0:32, 0:8], in_=blk_s[0:32, 0:NBLK]), reads=[blk_s], writes=[m8])
                S.op("dve", lambda: nc.vector.match_replace(out=wk_s[0:32, 0:NBLK], in_to_replace=m8[0:32, 0:8], in_values=blk_s[0:32, 0:NBLK], imm_value=-1e30),
                     reads=[blk_s, m8], writes=[wk_s])
                S.op("dve", lambda: nc.vector.max(out=m8[0:32, 8:16], in_=wk_s[0:32, 0:NBLK]), reads=[wk_s], writes=[m8])
                S.op("dve", lambda: nc.vector.tensor_scalar(out=wk_s[0:32, 0:NBLK], in0=blk_s[0:32, 0:NBLK], scalar1=m8[0:32, 15:16], scalar2=None, op0=ALU.is_ge),
                     reads=[blk_s, m8], writes=[wk_s])
                S.op("dve", lambda: nc.vector.memset(negb_s[0:32, 0:64 * NRG + 64], 0.0), writes=[SG[0]])
                S.op("dve", lambda: nc.vector.tensor_scalar(out=negb_s[0:32, 0:NBLK], in0=wk_s[0:32, 0:NBLK], scalar1=-1.0, scalar2=30000.0, op0=ALU.add, op1=ALU.mult),
                     reads=[wk_s], writes=[SG[0]])
                for r in range(NRG):
                    S.op("pe", lambda r=r: nc.tensor.transpose(trp[0:64, r * 32:(r + 1) * 32], negb_s[0:32, 64 * r:64 * r + 64], ident[0:32, 0:32]),
                         reads=[SG[0], ident], writes=[trp])
                S.op("dve", lambda: nc.vector.tensor_copy(
                    out=QNs[64:128, 0:NRG, 4 * g:4 * g + 4, :],
                    in_=trp.t[0:64, 0:32 * NRG].rearrange("p (r c) -> p r c", r=NRG).unsqueeze(2).to_broadcast([64, NRG, 4, 32])),
                    reads=[trp], writes=[poolW])

            def stream_attend(branch, ntile, fetch, prep):
                banks = [acc[0][0], acc[0][1], acc[1][0]]

                def region(h):
                    return banks[h // 6], (h % 6) * 65
                started = [False, False, False]

                def scores(ti, desc, g):
                    bank = rot("mm", mm)
                    if desc is None:
                        S.op("pe", lambda: nc.tensor.matmul(bank[0:32, 0:128], lhsT=tailK[:, branch - 1, g, :], rhs=QN.t[0:64, 4 * g:4 * g + 4, 0:32],
                                                            start=True, stop=True), reads=[CKT, QN], writes=[bank])
                    elif branch == 1:
                        S.op("pe", lambda: nc.tensor.matmul(bank[:, 0:128], lhsT=desc[0](g), rhs=QNs[:, desc[3], 4 * g:4 * g + 4, :],
                                                            start=True, stop=True), reads=[KE, poolW], writes=[bank])
                    else:
                        S.op("pe", lambda: nc.tensor.matmul(bank[:, 0:128], lhsT=desc[0](g), rhs=QN.t[0:64, 4 * g:4 * g + 4, 0:32],
                                                            start=True, stop=True), reads=[KW, QN], writes=[bank])
                    return bank

                def rest(ti, desc, g, bank):
                    tail = desc is None
                    nk = 32 if tail else 128
                    pt = rot("PT", PTs)
                    S.op("act", lambda: nc.scalar.activation(out=pt[0:nk, 0:128], in_=bank[0:nk, 0:128], func=AF.Exp), reads=[bank], writes=[pt])
                    if tail:
                        S.op("pool", lambda: nc.gpsimd.tensor_tensor(
                            out=pt.t[0:32, 0:128].rearrange("p (h c) -> p h c", h=4), in0=pt.t[0:32, 0:128].rearrange("p (h c) -> p h c", h=4),
                            in1=bdm[:, :].unsqueeze(1).to_broadcast([32, 4, 32]), op=ALU.mult), reads=[pt, bdm], writes=[pt])
                    elif branch == 2 and ti == 0:
                        S.op("pool", lambda: nc.gpsimd.tensor_tensor(
                            out=pt.t[:, 0:128].rearrange("p (h c) -> p h c", h=4), in0=pt.t[:, 0:128].rearrange("p (h c) -> p h c", h=4),
                            in1=wm0[:, :].unsqueeze(1).to_broadcast([128, 4, 32]), op=ALU.mult), reads=[pt, wm0], writes=[pt])
                    first = not started[0]
                    started[0] = True
                    vsrc = VW.t[0:32, 6 - branch, g, :] if tail else desc[1](g)
                    vbuf = VW if (tail or branch == 2) else Vp_b[ti % 2]
                    S.op("pe", lambda first=first, vsrc=vsrc: nc.tensor.matmul(
                        banks[0][0:65, g * 128:(g + 1) * 128], lhsT=vsrc, rhs=pt[0:nk, 0:128], start=first, stop=tail, skip_group_check=True),
                        reads=[pt, vbuf], writes=[banks[0]])

                AHEAD = 3
                pfs = {}
                for t_ in range(min(AHEAD, ntile)):
                    pfs[t_] = fetch(t_)
                descs = {0: prep(0, pfs.pop(0))}
                work = []
                for ti in range(ntile + 1):
                    for g in range(4):
                        work.append((ti, g))
                pend = scores(0, descs[0], 0)
                for wi, (ti, g) in enumerate(work):
                    if g == 0 and ti < ntile:
                        if ti + AHEAD < ntile:
                            pfs[ti + AHEAD] = fetch(ti + AHEAD)
                        if ti + 1 < ntile:
                            descs[ti + 1] = prep(ti + 1, pfs.pop(ti + 1))
                    cur = pend
                    if wi + 1 < len(work):
                        nti, ng = work[wi + 1]
                        pend = scores(nti, descs.get(nti) if nti < ntile else None, ng)
                    rest(ti, descs.get(ti) if ti < ntile else None, g, cur)
                    if g == 3 and ti in descs:
                        del descs[ti]
                ots = rot("stg", stgs)
                S.op("dve", lambda: nc.vector.tensor_copy(out=ots[0:65, 0:512], in_=banks[0][0:65, 0:512]), reads=[banks[0]], writes=[ots])
                for h in range(16):
                    tb = rot("mm", mm)
                    S.op("pe", lambda h=h: nc.tensor.transpose(tb[0:32, 0:65], ots[0:65, h * 32:(h + 1) * 32], identF[0:65, 0:65]),
                         reads=[ots, identF], writes=[tb])
                    sm = rot("small", small)
                    S.op("dve", lambda: nc.vector.reciprocal(out=sm[0:32, 0:1], in_=tb[0:32, 64:65]), reads=[tb], writes=[sm])
                    S.op("dve", lambda h=h: nc.vector.tensor_tensor(out=sm[0:32, 1:2], in0=sm[0:32, 0:1], in1=Gq[:, 3 * h + branch:3 * h + branch + 1], op=ALU.mult),
                         reads=[sm, Gq], writes=[sm])
                    S.op("dve", lambda h=h: nc.vector.scalar_tensor_tensor(
                        out=o_tok.t[0:32, 0, 64 * h:64 * h + 64], in0=tb[0:32, 0:64], scalar=sm[0:32, 1:2],
                        in1=o_tok.t[0:32, 0, 64 * h:64 * h + 64], op0=ALU.mult, op1=ALU.add), reads=[tb, sm, o_tok], writes=[o_tok])

            def slc_fetch(j):
                return gather_page(cslc_d, j)

            def slc_prep(j, pf):
                kb = rot("kvb", kvb)
                S.op("act", lambda: nc.scalar.copy(out=kb[:, 0:256], in_=pf[:, 0:256]), reads=[pf], writes=[kb])
                vb, vv = Vp_b[j % 2], Vp[j % 2]
                S.op("pool", lambda: nc.gpsimd.tensor_copy(out=vv[:, :, 0:64], in_=pf.t[:, 256:512].rearrange("p (g d) -> p g d", g=4)), reads=[pf], writes=[vb])
                for gg in range(4):
                    S.op("pe", lambda gg=gg: nc.tensor.transpose(trp[0:64, gg * 128:(gg + 1) * 128], kb[:, gg * 64:(gg + 1) * 64], ident[:, :]),
                         reads=[kb, ident], writes=[trp])
                kt = j % 32
                S.op("dve", lambda: nc.vector.tensor_copy(out=KE.t[0:64, :, kt * 128:(kt + 1) * 128],
                                                          in_=trp.t[0:64, 0:512].rearrange("p (a b) -> p a b", a=4)), reads=[trp], writes=[KE])
                return (lambda g: KE.t[:, g, kt * 128:(kt + 1) * 128]), (lambda g: vv[:, g, :]), 128, j // 32

            for vb, vv in zip(Vp_b, Vp):
                S.op("dve", lambda vv=vv: nc.vector.memset(vv[:, :, 64:65], 1.0), writes=[vb])
            stream_attend(1, NPG, slc_fetch, slc_prep)

            def win_fetch(i):
                pf = rot("stg", stgs)
                S.dma("sp", pf[:, :], swin_d[q, i * 128:(i + 1) * 128, :], writes=[pf], key="hw_" + pf.name)
                return pf

            def win_prep(i, pf):
                kb = rot("kvb", kvb)
                S.op("act", lambda: nc.scalar.copy(out=kb[:, 0:256], in_=pf[:, 0:256]), reads=[pf], writes=[kb])
                S.op("pool", lambda: nc.gpsimd.tensor_copy(out=VW.t[:, i, :, 0:64], in_=pf.t[:, 256:512].rearrange("p (g d) -> p g d", g=4)), reads=[pf], writes=[VW])
                for gg in range(4):
                    S.op("pe", lambda gg=gg: nc.tensor.transpose(trp[0:64, gg * 128:(gg + 1) * 128], kb[:, gg * 64:(gg + 1) * 64], ident[:, :]),
                         reads=[kb, ident], writes=[trp])
                S.op("dve", lambda: nc.vector.tensor_copy(out=KW.t[0:64, :, i * 128:(i + 1) * 128],
                                                          in_=trp.t[0:64, 0:512].rearrange("p (a b) -> p a b", a=4)), reads=[trp], writes=[KW])
                return (lambda g: KW.t[0:64, g, i * 128:(i + 1) * 128]), (lambda g: VW.t[:, i, g, :]), 128, 0
            stream_attend(2, 4, win_fetch, win_prep)

        def nsa_sample(subs):
            S.dma("sp", wd.t[:, 0:8, :], wo_b.rearrange("p (a b) -> p a b", a=8), reads=[castB], writes=[wd], key="wd")
            S.dma("pool", bdm[:, :], bdm_d, writes=[bdm])
            S.dma("pool", wm0[:, :], wm0_d, writes=[wm0])
            S.dma("sp", rowm[:, :], rowm_d, writes=[rowm])
            S.dma("sp", iota_f[:, :], iota_d, writes=[iota_f])
            S.dma("sp", winkv_s[:, 0:504, :], swin_d[:, 8:512, :], key="o_pool")
            nsa_project(0, subs, sample=True)
            S.op("dve", lambda: nc.vector.memset(o_tok.t[0:32, 0, :], 0.0), writes=[o_tok])
            for r in range(NRG):
                S.op("dve", lambda r=r: nc.vector.tensor_copy(out=QNs[0:64, r, :, :], in_=QN.t[0:64, :, 0:32]), reads=[QN], writes=[poolW])
            for q in range(4):
                sample_seq(q)
            out_proj(subs)

        for i in range(ntile):
            t0 = i * TT
            Xi = X[0]
            for s in range(nsub):
                S.dma("sp", Xi[s][:, :], xp[t0 + s * 128:t0 + (s + 1) * 128, :], writes=[Xi[s]])
            for s in range(nsub):
                first = (i == 0 and s == 0)
                prev = xhalo if s == 0 else Xi[s - 1]
                for hh in range(2):
                    bank = rot("mm", mm)
                    for cc in range(4):
                        c = hh * 4 + cc
                        w = c // 2
                        S.op("pe", lambda c=c, cc=cc, w=w, bank=bank: nc.tensor.matmul(
                            bank[:, cc * 128:(cc + 1) * 128], lhsT=Xi[s][:, c * 128:(c + 1) * 128],
                            rhs=Bm.t[:, w * 3 + (1 if first else 0), :], start=True, stop=first),
                            reads=[Xi[s], Bm], writes=[bank])
                        if not first:
                            S.op("pe", lambda c=c, cc=cc, w=w, bank=bank: nc.tensor.matmul(
                                bank[:, cc * 128:(cc + 1) * 128], lhsT=prev[:, c * 128:(c + 1) * 128],
                                rhs=Bm.t[:, w * 3 + 2, :], start=False, stop=True),
                                reads=[prev, Bm], writes=[bank])
                    S.op("act", lambda hh=hh, bank=bank: nc.scalar.copy(
                        out=dT.t[:, hh * 4:hh * 4 + 4, s * 128:(s + 1) * 128],
                        in_=bank.t[:, :].rearrange("p (a b) -> p a b", a=4)), reads=[bank], writes=[dT])
            S.op("pool", lambda: nc.gpsimd.tensor_copy(out=xhalo[:, :], in_=Xi[nsub - 1][:, :]), reads=[Xi[nsub - 1]], writes=[xhalo])
            subs = [(Xi[s], Xi[s][:, :], 128, s * 128) for s in range(nsub)]
            mixer_pool_finish(subs)
            ffn(0, subs, TT, 1, TT, True, i == ntile - 1)
            for (xb_buf, xs_ap, P, col0) in subs:
                to_hT(xb_buf, xs_ap, P, col0)
            nsa_prompt(i, subs)
            ffn(1, subs, TT, 1, TT, True, i == ntile - 1)
            for s in range(nsub):
                S.dma("sp", y_p[t0 + s * 128:t0 + (s + 1) * 128, :], Xi[s][:, :], reads=[Xi[s]], key="o_" + Xi[s].name)

        Xs = X[0][0]
        S.op("dve", lambda: nc.vector.memset(XS[:, :], 0.0), writes=[XS])
        S.dma("sp", XS[0:92, :], xscat.rearrange("a b c -> (a b) c"), writes=[XS])
        S.dma("sp", Xs[0:32, :], xs, writes=[Xs])
        for hh in range(2):
            bank = rot("mm", mm)
            for cc in range(4):
                c = hh * 4 + cc
                S.op("pe", lambda c=c, cc=cc, bank=bank: nc.tensor.matmul(
                    bank[:, cc * 128:cc * 128 + 32], lhsT=XS[:, c * 128:(c + 1) * 128], rhs=Bs.t[:, c // 2, :],
                    start=True, stop=True), reads=[XS, Bs], writes=[bank])
            S.op("act", lambda hh=hh, bank=bank: nc.scalar.copy(
                out=dT.t[:, hh * 4:hh * 4 + 4, 0:32],
                in_=bank.t[:, :].rearrange("p (a b) -> p a b", a=4)[:, :, 0:32]), reads=[bank], writes=[dT])
        subs = [(Xs, Xs[0:32, :], 32, 0)]
        mixer_pool_finish(subs)
        ffn(0, subs, 32, 4, 8, False, True)
        to_hT(Xs, Xs[0:32, :], 32, 0)
        nsa_sample(subs)
        ffn(1, subs, 32, 4, 8, False, True)
        S.dma("sp", y_s, Xs[0:32, :], reads=[Xs], key="o_" + Xs.name)
        S.finish("sp")
    return nc


def pool_consts():
    bm = np.zeros((128, 12, 128), np.float32)
    bs = np.zeros((128, 4, 32), np.float32)
    for wi, win in enumerate(POOL_WINDOWS):
        for t in range(128):
            for s in range(max(0, t - win + 1), t + 1):
                bm[s, wi * 3 + 0, t] += 1.0 / win
                bm[s, wi * 3 + 1, t] += 1.0 / min(t + 1, win)
            bm[t, wi * 3 + 0, t] -= 1.0
            bm[t, wi * 3 + 1, t] -= 1.0
            for s in range(t - win + 1, 0):
                bm[128 + s, wi * 3 + 2, t] += 1.0 / win
        for q in range(4):
            for t in range(8):
                r = 15 + t
                for i in range(r - win + 1, r + 1):
                    bs[q * 23 + i, wi, q * 8 + t] += 1.0 / win
                bs[q * 23 + r, wi, q * 8 + t] -= 1.0
    return bm.reshape(128, -1), bs.reshape(128, -1)


def fc_features():
    idx = np.zeros((NFC, 128), np.int64)
    for pb in range(NPB):
        for fc in range(4):
            base = (0 if fc < 2 else DFF) + 256 * pb + 128 * (fc % 2)
            idx[pb * 4 + fc] = base + np.arange(128)
    return idx


def prep_shared(inp):
    f = np.float32
    sh = {}
    ln_g, ln_b = np.asarray(inp["ln_g"], f), np.asarray(inp["ln_b"], f)
    lnp = np.zeros((4, 128, 2 * D), f)
    for l in range(2):
        for k in range(2):
            lnp[l * 2 + k, :, :D] = ln_g[l, k][None]
            lnp[l * 2 + k, :, D:] = ln_b[l, k][None]
    sh["lnp"] = lnp
    sh["pscale"] = np.ascontiguousarray(np.broadcast_to(np.asarray(inp["pool_scale"], f)[0][None], (128, D)))
    pw = np.asarray(inp["pool_w"], f)[0]
    sh["poolw"] = np.ascontiguousarray(pw.reshape(4, 2, 128, 256).transpose(2, 0, 1, 3).reshape(128, 8 * 256))
    bm, bs = pool_consts()
    sh["bmat"], sh["bsamp"] = bm, bs
    sh["ident"] = np.eye(128, dtype=f)
    wu = np.asarray(inp["ffn_w_up"], f)
    idx = fc_features()
    wup = np.zeros((2, NPB, 128, 8, 512), f)
    for l in range(2):
        for pb in range(NPB):
            cols = idx[pb * 4:pb * 4 + 4].reshape(-1)
            blk = wu[l][:, cols]
            wup[l, pb] = blk.reshape(8, 128, 512).transpose(1, 0, 2)
    sh["wup"] = wup.reshape(2, NPB, 128, 8 * 512)
    wdn = np.asarray(inp["ffn_w_down"], f)
    sh["wdown"] = np.ascontiguousarray(wdn.reshape(2, NKC, 128, D).transpose(0, 2, 1, 3).reshape(2, 128, NKC * D))
    cwv = np.asarray(inp["ffn_conv_w"], f)
    cbv = np.asarray(inp["ffn_conv_b"], f)
    cwb = np.zeros((2, 128, NFC, 4), f)
    for l in range(2):
        for k in range(3):
            cwb[l, :, :, k] = cwv[l, k][idx].T
        cwb[l, :, :, 3] = cbv[l][idx].T
    sh["cwb"] = cwb.reshape(2, 128, NFC * 4)
    w_in = np.asarray(inp["nsa_w_in"], f)[0]
    wpad = np.zeros((D, 6 * 512), f)
    wpad[:, :2608] = w_in
    sh["win"] = np.ascontiguousarray(wpad.reshape(8, 128, 6, 512).transpose(2, 1, 0, 3).reshape(6, 128, 8 * 512))
    w_o = np.asarray(inp["nsa_w_o"], f)[0]
    sh["wo"] = np.ascontiguousarray(w_o.reshape(8, 128, D).transpose(1, 0, 2).reshape(128, 8 * D))
    w1 = np.asarray(inp["cmp_w1"], f)[0]
    sh["w1"] = np.ascontiguousarray(w1.transpose(0, 2, 1, 3).reshape(2, 64, 32 * 128))
    sh["posT"] = np.ascontiguousarray(np.asarray(inp["cmp_pos"], f)[0].transpose(2, 0, 1).reshape(64, 64))
    sh["b1T"] = np.ascontiguousarray(np.asarray(inp["cmp_b1"], f)[0].T)
    sh["w2"] = np.ascontiguousarray(np.asarray(inp["cmp_w2"], f)[0].transpose(1, 0, 2).reshape(128, 128))
    sh["b2T"] = np.ascontiguousarray(np.asarray(inp["cmp_b2"], f)[0].T)
    return sh, idx


def nsa_consts(T, pos0=0):
    f = np.float32
    c = {}
    inv = np.power(f(500000.0), (f(-2.0) * np.arange(8, dtype=f) / f(16.0))).astype(f)
    ang = (np.arange(pos0, pos0 + T).astype(f)[:, None] * inv[None, :]).astype(f)
    c["rope"] = np.concatenate([np.cos(ang), np.sin(ang)], 1).astype(f)
    k = np.arange(128)
    tri = np.zeros((128, 2, 128), f)
    tri[:, 0, :] = (k[:, None] <= k[None, :])
    tri[:, 1, :] = (k[:, None] > k[None, :])
    c["tri"] = tri.reshape(128, 256)
    c["pat"] = (k[:, None] >= 16 * np.arange(8)[None, :] + 15).astype(f)
    c["emat"] = (np.arange(max(T, 4096))[None, :] // 64 == np.arange(64)[:, None]).astype(f)
    t = np.arange(T)
    j = np.arange(64)
    cur = t // 64
    valid = (j[None, :] * 64 <= t[:, None])
    forced = (j[None, :] == 0) | (j[None, :] == cur[:, None]) | (j[None, :] == cur[:, None] - 1)
    mul = (valid & ~forced).astype(f)
    add = np.where(valid & forced, f(1e9), np.where(valid, f(0.0), f(-1.0))).astype(f)
    c["scm"] = np.ascontiguousarray(np.stack([mul, add], 1).reshape(T // 128, 128, 128))
    return c


def sample_consts(past_len):
    f = np.float32
    c = {}
    inv = np.power(f(500000.0), (f(-2.0) * np.arange(8, dtype=f) / f(16.0))).astype(f)
    pos = np.tile(past_len + np.arange(8), 4).astype(f)
    ang = (pos[:, None] * inv[None, :]).astype(f)
    c["rope_s"] = np.concatenate([np.cos(ang), np.sin(ang)], 1).astype(f)
    r = np.arange(32)
    c["bdm"] = ((r[:, None] // 8 == r[None, :] // 8) & (r[:, None] % 8 <= r[None, :] % 8)).astype(f)
    c["wm0"] = (np.arange(128)[:, None] > (r[None, :] % 8)).astype(f)
    c["rowm"] = (r[:, None] // 8 == np.arange(4)[None, :]).astype(f)
    c["iota"] = np.arange(128, dtype=f).reshape(128, 1)
    return c


def kernel(**inp):
    f = np.float32
    T, TT = 4096, 256
    nc = build_nc(T, TT)
    sh, idx = prep_shared(inp)
    sh.update(nsa_consts(T))
    sh.update(sample_consts(16384))
    ccmp = np.asarray(inp["cache_cmp_kv"], f)[0]
    cslc = np.asarray(inp["cache_slc_kv"], f)[0]
    sh["ccmp"] = ccmp.reshape(ccmp.shape[0] * 128, 512)
    sh["cslc"] = cslc.reshape(cslc.shape[0] * 128, 512)
    swin = np.asarray(inp["state_win_kv"], f)[0].reshape(32, 512, 512)
    ptab = np.asarray(inp["page_table"], np.int32)
    x_prompt = np.asarray(inp["x_prompt"], f)
    x_sample = np.asarray(inp["x_sample"], f)
    state_pool = np.asarray(inp["state_pool"], f)[0]
    state_ffn = np.asarray(inp["state_ffn"], f)
    in_maps = []
    for c in range(NCORES):
        m = dict(sh)
        m["xp"] = np.ascontiguousarray(x_prompt[c])
        sl = slice(4 * c, 4 * c + 4)
        m["xs"] = np.ascontiguousarray(x_sample[sl].reshape(32, D))
        m["xscat"] = np.ascontiguousarray(np.concatenate([state_pool[sl], x_sample[sl]], 1))
        sf = state_ffn[:, sl]
        m["sffn"] = np.ascontiguousarray(sf[:, :, :, idx].transpose(0, 4, 3, 1, 2).reshape(2, 128, NFC * 8))
        m["swin"] = np.ascontiguousarray(swin[sl])
        m["ptab"] = np.ascontiguousarray(ptab[sl])
        in_maps.append(m)
    res = run_bass_kernel_spmd(nc, in_maps, core_ids=list(range(NCORES)))
    R = res.results
    y_prompt = np.stack([R[c]["y_p"] for c in range(NCORES)], 0)
    y_sample = np.concatenate([R[c]["y_s"].reshape(4, 8, D) for c in range(NCORES)], 0)
    pool_p = np.stack([R[c]["pool_p"] for c in range(NCORES)], 0)[None]
    pool_s = np.concatenate([R[c]["pool_s"] for c in range(NCORES)], 0)[None]
    ffn_p = np.zeros((2, NCORES, 2, 2 * DFF), f)
    ffn_s = np.zeros((2, 4 * NCORES, 2, 2 * DFF), f)
    for c in range(NCORES):
        fp = R[c]["ffn_p"].reshape(2, 128, NFC, 2)
        fs = R[c]["ffn_s"].reshape(2, 128, NFC, 4, 2)
        for l in range(2):
            ffn_p[l, c][:, idx.reshape(-1)] = fp[l].transpose(2, 1, 0).reshape(2, -1)
            ffn_s[l, 4 * c:4 * c + 4][:, :, idx.reshape(-1)] = fs[l].transpose(2, 3, 1, 0).reshape(4, 2, -1)
    z = lambda *s: np.zeros(s, f)
    cmp_p = np.stack([R[c]["cmpkv_p"] for c in range(NCORES)], 0).reshape(1, NCORES, T, 2, 4, 64)
    slc_p = np.stack([R[c]["slckv_p"] for c in range(NCORES)], 0).reshape(1, NCORES, T, 2, 4, 64)
    win_p = np.stack([R[c]["winkv_p"] for c in range(NCORES)], 0).reshape(1, NCORES, 512, 2, 4, 64)
    cmp_s = np.concatenate([R[c]["cmpkv_s"] for c in range(NCORES)], 0).reshape(1, 32, 8, 2, 4, 64)
    slc_s = np.concatenate([R[c]["slckv_s"] for c in range(NCORES)], 0).reshape(1, 32, 8, 2, 4, 64)
    win_s = np.concatenate([R[c]["winkv_s"] for c in range(NCORES)], 0).reshape(1, 32, 512, 2, 4, 64)
    return (y_prompt, y_sample, pool_p, pool_s, cmp_p, cmp_s, slc_p, slc_s, win_p, win_s, ffn_p, ffn_s)
```
